# Optimizing a Trainium2 kernel written in Bass

```python
import math
import jax, jax.numpy as jnp
from jax import lax
import numpy as np

D_MODEL = 1024
BATCH = 8
SEQ = 2048
DEPTH = 1
DEC_BATCH = 128
DEC_SEQ = 1
PAST_LEN = 2048
PAGE_SIZE = 128

H_RET = 4
DK_RET = D_MODEL // 8
DV_RET = D_MODEL // 8
RET_WIDTH = H_RET * DV_RET
H_DIFF = 4
DH_DIFF = D_MODEL // 16
DV_DIFF = 2 * DH_DIFF
DIFF_WIDTH = H_DIFF * DV_DIFF
MIX_WIDTH = RET_WIDTH + DIFF_WIDTH
IN_SPLITS = (H_RET * DK_RET, H_RET * DK_RET, H_RET * DV_RET, RET_WIDTH,
             H_DIFF * 2 * DH_DIFF, H_DIFF * 2 * DH_DIFF, H_DIFF * DV_DIFF)
IN_WIDTH = sum(IN_SPLITS)
D_FF = ((8 * D_MODEL // 3 + 127) // 128) * 128
CONV_W = 3
RET_CHUNK = 128
Q_BLOCK = 128
EPS = 1e-6
NEG_INF = -1e30

kernel_name = "hybrid_retention_diffattn_convffn_step"


def rms_norm(x, g):
    xf = x.astype(jnp.float32)
    y = xf * lax.rsqrt(jnp.mean(xf * xf, axis=-1, keepdims=True) + EPS)
    return (y * g.astype(jnp.float32)).astype(x.dtype)


def ret_log_decay():
    return jnp.log(1.0 - 2.0 ** (-5.0 - jnp.arange(H_RET, dtype=jnp.float32)))


def alibi_slopes(n):
    return 2.0 ** (-8.0 / n * jnp.arange(1, n + 1, dtype=jnp.float32))


def alibi_bias(qpos, kpos, slopes):
    dist = (qpos[:, None] - kpos[None, :]).astype(jnp.float32)
    return jnp.where(dist[None] >= 0, -slopes[:, None, None] * dist[None], NEG_INF)


def lambda_init(layer):
    return 0.8 - 0.6 * math.exp(-0.3 * layer)


def in_project(h, w_in):
    B, T = h.shape[0], h.shape[1]
    z = jnp.einsum('btd,de->bte', h, w_in)
    idx = [int(i) for i in np.cumsum(IN_SPLITS)[:-1]]
    rq, rk, rv, rg, dq, dk, dv = jnp.split(z, idx, axis=-1)
    f32 = jnp.float32
    rq = rq.reshape(B, T, H_RET, DK_RET).transpose(0, 2, 1, 3).astype(f32)
    rk = rk.reshape(B, T, H_RET, DK_RET).transpose(0, 2, 1, 3).astype(f32) * (DK_RET ** -0.5)
    rv = rv.reshape(B, T, H_RET, DV_RET).transpose(0, 2, 1, 3).astype(f32)
    dq = dq.reshape(B, T, H_DIFF, 2, DH_DIFF)
    dk = dk.reshape(B, T, H_DIFF, 2, DH_DIFF)
    dv = dv.reshape(B, T, H_DIFF, DV_DIFF)
    return rq, rk, rv, rg, dq, dk, dv


def retention_chunk(q, k, v, s, log_g):
    L = q.shape[2]
    i = jnp.arange(L, dtype=jnp.float32)
    diff = i[:, None] - i[None, :]
    causal = diff >= 0
    decay = jnp.where(causal[None], jnp.exp(jnp.where(causal, diff, 0.0)[None] * log_g[:, None, None]), 0.0)
    att = jnp.einsum('bhid,bhjd->bhij', q, k) * decay[None]
    q_dec = q * jnp.exp((i + 1.0)[None, :] * log_g[:, None])[None, :, :, None]
    o = jnp.einsum('bhij,bhje->bhie', att, v) + jnp.einsum('bhid,bhde->bhie', q_dec, s)
    k_dec = k * jnp.exp((L - 1.0 - i)[None, :] * log_g[:, None])[None, :, :, None]
    s_new = jnp.exp(L * log_g)[None, :, None, None] * s + jnp.einsum('bhjd,bhje->bhde', k_dec, v)
    return o, s_new


def retention_prompt(q, k, v, log_g):
    B, H, T, _ = q.shape
    nc = T // RET_CHUNK

    def to_chunks(a):
        return a.reshape(B, H, nc, RET_CHUNK, a.shape[-1]).transpose(2, 0, 1, 3, 4)

    def step(s, qkv):
        qc, kc, vc = qkv
        o, s = retention_chunk(qc, kc, vc, s, log_g)
        return s, o

    s0 = jnp.zeros((B, H, DK_RET, DV_RET), jnp.float32)
    s_fin, o = lax.scan(step, s0, (to_chunks(q), to_chunks(k), to_chunks(v)))
    o = o.transpose(1, 2, 0, 3, 4).reshape(B, H, T, DV_RET)
    return o, s_fin


def retention_out(o, g):
    B, T = g.shape[0], g.shape[1]
    o = o * lax.rsqrt(jnp.mean(o * o, axis=-1, keepdims=True) + EPS)
    o = o.transpose(0, 2, 1, 3).reshape(B, T, RET_WIDTH)
    return o.astype(g.dtype) * jax.nn.silu(g)


def diff_combine(p, lam):
    return p[:, :, 0] - lam * p[:, :, 1]


def diff_attn_prompt(dq, dk, dv, lam):
    B, T = dq.shape[0], dq.shape[1]
    nb = T // Q_BLOCK
    slopes = alibi_slopes(H_DIFF)
    kpos = jnp.arange(T)
    qb = dq.reshape(B, nb, Q_BLOCK, H_DIFF, 2, DH_DIFF).transpose(1, 0, 2, 3, 4, 5)

    def block(args):
        q, start = args
        qpos = start + jnp.arange(Q_BLOCK)
        s = jnp.einsum('bqhmd,bkhmd->bhmqk', q, dk).astype(jnp.float32) * (DH_DIFF ** -0.5)
        s = s + alibi_bias(qpos, kpos, slopes)[None, :, None]
        a = diff_combine(jax.nn.softmax(s, axis=-1), lam)
        return jnp.einsum('bhqk,bkhe->bqhe', a.astype(dv.dtype), dv)

    out = lax.map(block, (qb, jnp.arange(nb) * Q_BLOCK))
    return out.transpose(1, 0, 2, 3, 4).reshape(B, T, H_DIFF, DV_DIFF)


def diff_attn_sample(dq, dk, dv, k_past, v_past, lam):
    T = dq.shape[1]
    P = k_past.shape[1]
    slopes = alibi_slopes(H_DIFF)
    qpos = P + jnp.arange(T)
    scale = DH_DIFF ** -0.5
    s_past = jnp.einsum('bqhmd,bkhmd->bhmqk', dq, k_past).astype(jnp.float32) * scale
    s_new = jnp.einsum('bqhmd,bkhmd->bhmqk', dq, dk).astype(jnp.float32) * scale
    s_past = s_past + alibi_bias(qpos, jnp.arange(P), slopes)[None, :, None]
    s_new = s_new + alibi_bias(qpos, qpos, slopes)[None, :, None]
    p = jax.nn.softmax(jnp.concatenate([s_past, s_new], axis=-1), axis=-1)
    a = diff_combine(p, lam).astype(dv.dtype)
    return (jnp.einsum('bhqk,bkhe->bqhe', a[..., :P], v_past)
            + jnp.einsum('bhqk,bkhe->bqhe', a[..., P:], dv))


def mix_out(ret_o, rg, diff_o, subln_g, lam_i, w_o):
    B, T = rg.shape[0], rg.shape[1]
    a = retention_out(ret_o, rg)
    b = (rms_norm(diff_o, subln_g) * (1.0 - lam_i)).reshape(B, T, DIFF_WIDTH)
    m = jnp.concatenate([a, b.astype(a.dtype)], axis=-1)
    return jnp.einsum('bte,ed->btd', m, w_o)


def conv_ffn(h, buf, w_in, conv_w, conv_b, w_out):
    T = h.shape[1]
    z = jnp.einsum('btd,df->btf', h, w_in)
    g, u = jnp.split(z, [D_FF], axis=-1)
    xp = jnp.concatenate([buf.astype(g.dtype), g], axis=1)
    c = sum(conv_w[j] * xp[:, j:j + T] for j in range(CONV_W)) + conv_b
    y = jnp.einsum('btf,fd->btd', jax.nn.gelu(c) * u, w_out)
    return y, xp[:, -(CONV_W - 1):]


def setup_inputs(seed: int = 0) -> dict:
    key = jax.random.key(seed)
    ks = jax.random.split(key, 24)
    f32 = jnp.float32
    n_pages = PAST_LEN // PAGE_SIZE
    n_pool = (5 * DEC_BATCH * n_pages + 3) // 4
    nrm = lambda k, shape, s: jax.random.normal(k, shape, f32) * s
    perm = jax.random.permutation(ks[0], n_pool)[:DEC_BATCH * n_pages]
    return {
        'x_prompt': nrm(ks[1], (BATCH, SEQ, D_MODEL), 1.0),
        'x_sample': nrm(ks[2], (DEC_BATCH, DEC_SEQ, D_MODEL), 1.0),
        'state_ret': nrm(ks[3], (DEPTH, DEC_BATCH, H_RET, DK_RET, DV_RET), 0.5),
        'cache_k': nrm(ks[4], (DEPTH, n_pool, PAGE_SIZE, H_DIFF, 2, DH_DIFF), 1.0),
        'cache_v': nrm(ks[5], (DEPTH, n_pool, PAGE_SIZE, H_DIFF, DV_DIFF), 1.0),
        'state_conv': nrm(ks[6], (DEPTH, DEC_BATCH, CONV_W - 1, D_FF), 1.0),
        'page_table': perm.reshape(DEC_BATCH, n_pages).astype(jnp.int32),
        'norm_mix_pre': 1.0 + nrm(ks[7], (DEPTH, D_MODEL), 0.02),
        'norm_mix_post': 1.0 + nrm(ks[8], (DEPTH, D_MODEL), 0.02),
        'w_in': nrm(ks[9], (DEPTH, D_MODEL, IN_WIDTH), D_MODEL ** -0.5),
        'w_o': nrm(ks[10], (DEPTH, MIX_WIDTH, D_MODEL), MIX_WIDTH ** -0.5),
        'lambda_q1': nrm(ks[11], (DEPTH, DH_DIFF), 0.1),
        'lambda_k1': nrm(ks[12], (DEPTH, DH_DIFF), 0.1),
        'lambda_q2': nrm(ks[13], (DEPTH, DH_DIFF), 0.1),
        'lambda_k2': nrm(ks[14], (DEPTH, DH_DIFF), 0.1),
        'subln_g': 1.0 + nrm(ks[15], (DEPTH, DV_DIFF), 0.02),
        'norm_ffn_pre': 1.0 + nrm(ks[16], (DEPTH, D_MODEL), 0.02),
        'norm_ffn_post': 1.0 + nrm(ks[17], (DEPTH, D_MODEL), 0.02),
        'w_ffn_in': nrm(ks[18], (DEPTH, D_MODEL, 2 * D_FF), D_MODEL ** -0.5),
        'conv_w': nrm(ks[19], (DEPTH, CONV_W, D_FF), CONV_W ** -0.5),
        'conv_b': nrm(ks[20], (DEPTH, D_FF), 0.01),
        'w_ffn_out': nrm(ks[21], (DEPTH, D_FF, D_MODEL), D_FF ** -0.5),
    }


def reference(x_prompt, x_sample, state_ret, cache_k, cache_v, state_conv, page_table,
              norm_mix_pre, norm_mix_post, w_in, w_o, lambda_q1, lambda_k1, lambda_q2, lambda_k2,
              subln_g, norm_ffn_pre, norm_ffn_post, w_ffn_in, conv_w, conv_b, w_ffn_out):
    f32 = jnp.float32
    log_g = ret_log_decay()
    B = x_prompt.shape[0]
    DB = x_sample.shape[0]
    xp, xs = x_prompt, x_sample
    rsp, rss, kp, vp, kss, vss, cp, cs = [], [], [], [], [], [], [], []
    for l in range(DEPTH):
        lam_i = lambda_init(l)
        lam = (jnp.exp(jnp.sum(lambda_q1[l].astype(f32) * lambda_k1[l].astype(f32)))
               - jnp.exp(jnp.sum(lambda_q2[l].astype(f32) * lambda_k2[l].astype(f32))) + lam_i)

        h = rms_norm(xp, norm_mix_pre[l])
        rq, rk, rv, rg, dq, dk, dv = in_project(h, w_in[l])
        ro, s_fin = retention_prompt(rq, rk, rv, log_g)
        do = diff_attn_prompt(dq, dk, dv, lam)
        xp = xp + rms_norm(mix_out(ro, rg, do, subln_g[l], lam_i, w_o[l]), norm_mix_post[l])
        h = rms_norm(xp, norm_ffn_pre[l])
        f, cbuf = conv_ffn(h, jnp.zeros((B, CONV_W - 1, D_FF), h.dtype),
                           w_ffn_in[l], conv_w[l], conv_b[l], w_ffn_out[l])
        xp = xp + rms_norm(f, norm_ffn_post[l])
        rsp.append(s_fin.astype(x_prompt.dtype))
        kp.append(dk)
        vp.append(dv)
        cp.append(cbuf)

        h = rms_norm(xs, norm_mix_pre[l])
        rq, rk, rv, rg, dq, dk, dv = in_project(h, w_in[l])
        ro, s_new = retention_chunk(rq, rk, rv, state_ret[l].astype(f32), log_g)
        k_past = cache_k[l][page_table].reshape(DB, -1, H_DIFF, 2, DH_DIFF)
        v_past = cache_v[l][page_table].reshape(DB, -1, H_DIFF, DV_DIFF)
        do = diff_attn_sample(dq, dk, dv, k_past.astype(dq.dtype), v_past.astype(dv.dtype), lam)
        xs = xs + rms_norm(mix_out(ro, rg, do, subln_g[l], lam_i, w_o[l]), norm_mix_post[l])
        h = rms_norm(xs, norm_ffn_pre[l])
        f, cbuf = conv_ffn(h, state_conv[l], w_ffn_in[l], conv_w[l], conv_b[l], w_ffn_out[l])
        xs = xs + rms_norm(f, norm_ffn_post[l])
        rss.append(s_new.astype(state_ret.dtype))
        kss.append(dk)
        vss.append(dv)
        cs.append(cbuf)

    return (xp, xs, jnp.stack(rsp), jnp.stack(rss), jnp.stack(kp), jnp.stack(vp),
            jnp.stack(kss), jnp.stack(vss), jnp.stack(cp), jnp.stack(cs))
```

```python
import math
import bisect
import contextlib
import numpy as np
import concourse.bass as bass
import concourse.mybir as mybir
from concourse.bass_utils import run_bass_kernel_spmd

F32 = mybir.dt.float32
BF16 = mybir.dt.bfloat16
I32 = mybir.dt.int32
AF = mybir.ActivationFunctionType
ALU = mybir.AluOpType
AX = mybir.AxisListType

D = 1024
T = 2048
NT = 16
GT = 512
NG = 4
DFF = 2816
NFC = 22
INW = 3584
EPS = 1e-6
NS = 16
H = 4
C_RQ, C_RK, C_RV, C_RG, C_DQ, C_DK, C_DV = 0, 512, 1024, 1536, 2048, 2560, 3072
GAM = [1.0 - 2.0 ** (-5.0 - h) for h in range(4)]
SLOPE = [2.0 ** (-2.0 * (h + 1)) for h in range(4)]
LAM_INIT = 0.8 - 0.6 * math.exp(-0.3 * 0)

K_ID = 0
K_DECAY = 128
K_QDEC = K_DECAY + 512
K_KDEC = K_QDEC + 512
K_CAUS = K_KDEC + 512
K_ABIAS = K_CAUS + 128
K_END = K_ABIAS + 64


def make_consts():
    c = np.zeros((128, K_END), np.float32)
    c[:, K_ID:K_ID + 128] = np.eye(128, dtype=np.float32)
    i = np.arange(128, dtype=np.float64)
    for h in range(4):
        g = GAM[h]
        diff = i[None, :] - i[:, None]
        dec = np.where(diff >= 0, g ** np.maximum(diff, 0), 0.0) * (128 ** -0.5)
        c[:, K_DECAY + 128 * h:K_DECAY + 128 * (h + 1)] = dec
        c[:, K_QDEC + 128 * h:K_QDEC + 128 * (h + 1)] = (g ** (i + 1.0))[None, :]
        c[:, K_KDEC + 128 * h:K_KDEC + 128 * (h + 1)] = ((128 ** -0.5) * g ** (127.0 - i))[:, None]
        for di in range(16):
            dd = di - 3
            c[:, K_ABIAS + 16 * h + di] = SLOPE[h] * (i - 128.0 * dd)
    c[:, K_CAUS:K_CAUS + 128] = (i[None, :] >= i[:, None]).astype(np.float32)
    sh = np.zeros((1, 4 * 512), np.float32)
    tl = np.arange(512, dtype=np.float64)
    for h in range(4):
        sh[0, 512 * h:512 * (h + 1)] = -8.0 * SLOPE[h] * tl
    return c, sh


K2_OH = 0
K2_OFFS = 256
K2_BIAS = 264
K2_Z = K2_BIAS + 128
K2_END = K2_Z + 32
NQ = 8


def make_consts2():
    c = np.zeros((128, K2_END), np.float32)
    for b in range(16):
        c[:, K2_OH + 16 * b + b] = 1.0
    p = np.arange(128)
    for q in range(NQ):
        c[:, K2_OFFS + q] = (p % 8) * 8 + q
        for t in range(2):
            s = (p // 8) * 128 + (p % 8) * 16 + 2 * q + t
            for hh in range(4):
                for m in range(2):
                    c[:, K2_BIAS + q * 16 + t * 8 + hh * 2 + m] = -8.0 * SLOPE[hh] * (2048.0 - s)
    return c


class Op:
    __slots__ = ("eng", "fn", "deps", "chan", "chan_idx", "sig", "seq", "idx", "hard")


class Sched:
    ENGS = ("pe", "act", "dve", "pool", "sp")

    def __init__(self, excl=None, tag=""):
        if excl is not None:
            self.EXCL = tuple(excl)
        self.tag = tag
        self.ops = []
        self.last_w = {}
        self.readers = {}
        self.chan_cnt = {}

    EXCL = ("Z0", "Z1", "SC0", "SC1", "AV", "DEN", "RT", "TP")

    def add(self, eng, fn, r=(), w=(), chan=None, hard=False):
        w = list(w) + [k for k in r if k in self.EXCL]
        r = [k for k in r if k not in self.EXCL]
        op = Op()
        op.eng, op.fn, op.chan, op.sig, op.seq = eng, fn, chan, False, 0
        op.hard = hard
        op.idx = len(self.ops)
        deps = set()
        for k in r:
            lw = self.last_w.get(k)
            if lw is not None:
                deps.add(lw)
        for k in w:
            lw = self.last_w.get(k)
            if lw is not None:
                deps.add(lw)
            for rd in self.readers.get(k, {}).values():
                deps.add(rd)
        rkey = ("c", chan) if chan is not None else eng
        for k in r:
            self.readers.setdefault(k, {})[rkey] = op
        for k in w:
            self.last_w[k] = op
            self.readers[k] = {}
        deps.discard(op)
        op.deps = deps
        if chan is not None:
            self.chan_cnt[chan] = self.chan_cnt.get(chan, 0) + 1
            op.chan_idx = self.chan_cnt[chan]
        else:
            op.chan_idx = 0
        self.ops.append(op)
        return op

    def emit(self, nc, final_eng="sp", limit=None, sem_stack=None):
        ops = self.ops
        if limit is not None:
            ops = ops[:limit]
            self.chan_cnt = {}
            for x in ops:
                if x.chan is not None:
                    self.chan_cnt[x.chan] = self.chan_cnt.get(x.chan, 0) + 1
        def needs_sem(x, d):
            if d.chan is not None:
                return True
            if d.eng == x.eng and x.chan is None and not x.hard:
                return False
            return True
        for x in ops:
            for d in x.deps:
                if needs_sem(x, d) and d.chan is None:
                    d.sig = True
        cnt = {e: 0 for e in self.ENGS}
        for x in ops:
            if x.chan is None and x.sig:
                cnt[x.eng] += 1
                x.seq = cnt[x.eng]
        chans = sorted(self.chan_cnt.keys())
        with contextlib.ExitStack() as es:
            ss = sem_stack if sem_stack is not None else es
            esem = {e: ss.enter_context(nc.semaphore(self.tag + "sem_" + e)) for e in self.ENGS}
            csem = {c: ss.enter_context(nc.semaphore(self.tag + "ch_" + str(c))) for c in chans}
            block = es.enter_context(nc.Block())
            per_eng = {e: [x for x in ops if x.eng == e] for e in self.ENGS}
            chan_idx_list = {c: [x.idx for x in ops if x.chan == c] for c in chans}

            def body(e, eng):
                waited = {}
                for x in per_eng[e]:
                    need = {}
                    for d in x.deps:
                        if not needs_sem(x, d):
                            continue
                        if d.chan is not None:
                            key, val = ("c", d.chan), 16 * bisect.bisect_left(chan_idx_list[d.chan], x.idx)
                        else:
                            key, val = ("e", d.eng), d.seq
                        if val > need.get(key, 0):
                            need[key] = val
                    for key, val in need.items():
                        if waited.get(key, 0) >= val:
                            continue
                        waited[key] = val
                        sem = csem[key[1]] if key[0] == "c" else esem[key[1]]
                        eng.wait_ge(sem, val)
                    inst = x.fn(eng)
                    if x.chan is not None:
                        inst.then_inc(csem[x.chan], 16)
                    elif x.sig:
                        inst.then_inc(esem[x.eng], 1)
                if e == final_eng:
                    for c in chans:
                        eng.wait_ge(csem[c], 16 * self.chan_cnt[c])

            @block.tensor
            def _(eng):
                body("pe", eng)

            @block.scalar
            def _(eng):
                body("act", eng)

            @block.vector
            def _(eng):
                body("dve", eng)

            @block.gpsimd
            def _(eng):
                body("pool", eng)

            @block.sync
            def _(eng):
                body("sp", eng)


def MM(out, lhsT, rhs, start=True, stop=True):
    return lambda e: e.matmul(out=out, lhsT=lhsT, rhs=rhs, start=start, stop=stop)


def TR(out, in_, identity):
    return lambda e: e.transpose(out=out, in_=in_, identity=identity)


def ACT(out, in_, func, **kw):
    return lambda e: e.activation(out=out, in_=in_, func=func, **kw)


def TT(out, in0, in1, op):
    return lambda e: e.tensor_tensor(out=out, in0=in0, in1=in1, op=op)


def STT(out, in0, scalar, in1, op0, op1):
    return lambda e: e.scalar_tensor_tensor(out=out, in0=in0, scalar=scalar, in1=in1, op0=op0, op1=op1)


def TS(out, in0, s1, s2, op0, op1=None):
    if op1 is None:
        return lambda e: e.tensor_scalar(out=out, in0=in0, scalar1=s1, scalar2=None, op0=op0)
    return lambda e: e.tensor_scalar(out=out, in0=in0, scalar1=s1, scalar2=s2, op0=op0, op1=op1)


def CP(out, in_):
    return lambda e: e.tensor_copy(out=out, in_=in_)


def RCP(out, in_):
    return lambda e: e.reciprocal(out=out, in_=in_)


def MSET(ap, v):
    return lambda e: e.memset(ap, v)


def RED(out, in_, op=None):
    return lambda e: e.tensor_reduce(out=out, in_=in_, axis=AX.X, op=(op or ALU.add))


def DMA(out, in_, slow=False):
    if slow:
        return lambda e: e.dma_start(out=out, in_=in_, allow_slow_non_contiguous=True)
    return lambda e: e.dma_start(out=out, in_=in_)


def IDMA(out, in_, idx_ap):
    return lambda e: e.indirect_dma_start(out=out, out_offset=None, in_=in_,
                                          in_offset=bass.IndirectOffsetOnAxis(ap=idx_ap, axis=0))


def build_sample(nc, esP, L):
    cst, ident_bf, ones_f, wbig, wo_sb, lamw, subgc, scol = (L[k] for k in
                                                              ("cst", "ident_bf", "ones_f", "wbig", "wo_sb", "lamw", "subgc", "scol"))
    xs, sret, ck, cv, sconv, ptab = (L[k] for k in ("xs", "sret", "ck", "cv", "sconv", "ptab"))
    w_in, w_fi = L["w_in"], L["w_fi"]
    n_mpre, n_mpost, n_fpre, n_fpost, subg, convw, convb = (L[k] for k in
                                                            ("n_mpre", "n_mpost", "n_fpre", "n_fpost", "subg", "convw", "convb"))
    y_s, o_rss, o_ks, o_vs, o_cs = (L[k] for k in ("y_s", "o_rss", "o_ks", "o_vs", "o_cs"))
    consts2_d = L["consts2_d"]
    neglam = lamw[0:NS, 2:3]
    epsc = scol[0:NS, 7:8]
    wfo_sb = wbig.rearrange("p a b -> p (a b)")[:, 0:NFC * D].rearrange("p (c d) -> p c d", d=D)

    banks = ("TPs", "Zs0", "Zs1", "QB", "SN", "Rps", "Oacc", "MISC")
    S = Sched(excl=banks, tag="s2_")
    es = contextlib.ExitStack()
    with es:
        def sb(name, shape, dtp=F32):
            return es.enter_context(nc.sbuf_tensor("s2_" + name, list(shape), dtp))

        def ps(name, shape, dtp=F32):
            return es.enter_context(nc.psum_tensor("s2_" + name, list(shape), dtp))

        cst2 = sb("cst2", [128, K2_END])
        xs_t = sb("xs_t", [NS, D])
        gA = sb("gA", [NS, D])
        gB = sb("gB", [NS, D])
        hs_bf = sb("hs_bf", [NS, D], BF16)
        hsT = sb("hsT", [128, 8, NS], BF16)
        wblk = sb("wblk", [128, 8, 512], BF16)
        zs = [sb("zs%d" % i, [NS, 512]) for i in range(7)]
        rq, rk, rv, rg, dq, dk, dv = zs
        ptsel = sb("ptsel", [128, NS], I32)
        idx = sb("idx", [128, NS, NQ], I32)
        Kq = [sb("Kq%d" % i, [128, 2, 512]) for i in range(2)]
        Vq = [sb("Vq%d" % i, [128, 2, 512]) for i in range(2)]
        prod = [sb("prod%d" % i, [128, 2, 512]) for i in range(2)]
        Vb = [sb("Vb%d" % i, [128, 2, 512], BF16) for i in range(2)]
        qb = [sb("qb%d" % i, [128, 512]) for i in range(2)]
        S4 = [sb("S4_%d" % i, [128, 16]) for i in range(2)]
        S4b = [sb("S4b_%d" % i, [128, 16]) for i in range(2)]
        Pc = [sb("Pc%d" % i, [128, 2, 8]) for i in range(2)]
        Pacc = sb("Pacc", [128, NS, 8])
        Pz = [sb("Pz%d" % i, [128, 2, 4, 32], BF16) for i in range(2)]
        St = [sb("St%d" % i, [128, 4, 128]) for i in range(2)]
        Sn = [sb("Sn%d" % i, [128, 4, 128]) for i in range(2)]
        Qz = sb("Qz", [128, 4, NS, NS])
        rqTs = sb("rqTs", [128, 4, NS])
        Km = [sb("Km%d" % i, [NS, 512]) for i in range(2)]
        t16 = [sb("t16_%d" % i, [32 if i == 0 else NS, 512]) for i in range(7)]
        sm = sb("sm", [NS, 64])
        ms_bf = sb("ms_bf", [NS, D], BF16)
        a2_bf = sb("a2_bf", [NS, DFF], BF16)
        a2T = sb("a2T", [128, NFC, NS], BF16)
        cwb = sb("cwb", [NS, 4, 256])
        scb = sb("scb", [NS, 2, 256])
        cc = [sb("cc%d" % i, [NS, 256]) for i in range(3)]
        gsub_bc = sb("gsub_bc", [NS, 4, 128])
        tmpS = t16[2]

        TPs = ps("TPs", [128, 8, 128], BF16)
        Zs = [ps("Zs%d" % i, [128, 512]) for i in range(2)]
        QB = ps("QB", [128, 512])
        SN = ps("SN", [128, 512])
        Rps = ps("Rps", [128, 512])
        Oacc = ps("Oacc", [128, 512])
        MISC = ps("MISC", [128, 512])

        A = S.add
        A("sp", DMA(cst2[:], consts2_d), w=["cst2"], chan="k_c")
        A("sp", DMA(xs_t[:], xs), w=["xs_t"], chan="k_c")
        A("sp", DMA(gA[:], n_mpre.to_broadcast([NS, D])), w=["gA"], chan="k_c")
        A("sp", DMA(gB[:], n_mpost.to_broadcast([NS, D])), w=["gB"], chan="k_c")
        for h in range(4):
            A("sp", DMA(gsub_bc[:, h, :], subg.to_broadcast([NS, 128])), w=["gsub_bc"], chan="k_c")
        for j in range(16):
            A("sp", DMA(ptsel[8 * j:8 * j + 8, :], ptab[:, j:j + 1].rearrange("b o -> o b").to_broadcast([8, NS]), slow=True),
              w=["ptsel"], chan="k_c")
        A("sp", DMA(o_cs[:, 0, :], sconv[:, 1, :]), chan="k_out")
        A("dve", TS(gsub_bc[:], gsub_bc[:], 1.0 - LAM_INIT, None, ALU.mult), r=["gsub_bc"], w=["gsub_bc"])
        for q in range(NQ):
            A("dve", TS(idx[:, :, q], ptsel[:, :], 64.0, cst2[:, K2_OFFS + q:K2_OFFS + q + 1], ALU.mult, ALU.add),
              r=["ptsel", "cst2"], w=["idx"])

        if L.get("dbg") == "idx":
            A("sp", DMA(y_s.rearrange("b (x c) -> (b x) c", x=8), idx[:].rearrange("p b q -> p (b q)").bitcast(F32)), r=["idx"], chan="k_out")
            S.emit(nc, sem_stack=esP)
            return

        def smc(c, n=1):
            return sm[:, c:c + n]

        def rms16(src, src_key, g_tile, g_key, out_ap, out_key, col):
            a = smc(col)
            A("act", ACT(tmpS[:, :], src[:, 0:512], AF.Square, accum_out=smc(col)), r=[src_key], w=["t2", "sm%d" % col])
            A("act", ACT(tmpS[:, :], src[:, 512:1024], AF.Square, accum_out=smc(col + 1)), r=[src_key], w=["t2", "sm%d" % (col + 1)])
            A("dve", TT(a, a, smc(col + 1), ALU.add), r=["sm%d" % col, "sm%d" % (col + 1)], w=["sm%d" % col])
            A("act", ACT(a, a, AF.Sqrt, scale=1.0 / D, bias=epsc), r=["sm%d" % col], w=["sm%d" % col])
            A("dve", RCP(a, a), r=["sm%d" % col], w=["sm%d" % col])
            for hf in range(2):
                hs = slice(hf * 512, (hf + 1) * 512)
                A("act", ACT(tmpS[:, :], src[:, hs], AF.Copy, scale=a), r=[src_key, "sm%d" % col], w=["t2"])
                A("dve", TT(out_ap[:, hs], tmpS[:, :], g_tile[:, hs], ALU.mult), r=["t2", g_key], w=[out_key])

        def transpose16(src_bf, src_key, dstT, dst_key, nchunk):
            for c0 in range(0, nchunk, 8):
                n = min(8, nchunk - c0)
                for c in range(n):
                    A("pe", TR(TPs[:, c, 0:NS], src_bf[:, (c0 + c) * 128:(c0 + c + 1) * 128], ident_bf[0:NS, 0:NS]),
                      r=[src_key, "ident_bf"], w=["TPs"])
                A("act", ACT(dstT[:, c0:c0 + n, :], TPs[:, 0:n, 0:NS], AF.Copy), r=["TPs"], w=[dst_key])

        def postnorm16(zb, g_tile, g_key, res, res_key, col):
            a = smc(col)
            A("act", ACT(tmpS[:, :], Zs[zb[0]][0:NS, :], AF.Square, accum_out=smc(col)), r=["Zs%d" % zb[0]], w=["t2", "sm%d" % col])
            A("act", ACT(tmpS[:, :], Zs[zb[1]][0:NS, :], AF.Square, accum_out=smc(col + 1)), r=["Zs%d" % zb[1]], w=["t2", "sm%d" % (col + 1)])
            A("dve", TT(a, a, smc(col + 1), ALU.add), r=["sm%d" % col, "sm%d" % (col + 1)], w=["sm%d" % col])
            A("act", ACT(a, a, AF.Sqrt, scale=1.0 / D, bias=epsc), r=["sm%d" % col], w=["sm%d" % col])
            A("dve", RCP(a, a), r=["sm%d" % col], w=["sm%d" % col])
            for hf in range(2):
                hs = slice(hf * 512, (hf + 1) * 512)
                A("act", ACT(tmpS[:, :], Zs[zb[hf]][0:NS, :], AF.Copy, scale=a), r=["Zs%d" % zb[hf], "sm%d" % col], w=["t2"])
                A("pool", TT(tmpS[:, :], tmpS[:, :], g_tile[:, hs], ALU.mult), r=["t2", g_key], w=["t2"])
                A("dve", TT(res[:, hs], res[:, hs], tmpS[:, :], ALU.add), r=["t2", res_key], w=[res_key])

        rms16(xs_t, "xs_t", gA, "gA", hs_bf, "hs_bf", 0)
        transpose16(hs_bf, "hs_bf", hsT, "hsT", 8)
        winv = w_in.rearrange("(k p) c -> p k c", p=128)
        for blk in range(7):
            A("pool", DMA(wblk[:, :, :], winv[:, :, blk * 512:(blk + 1) * 512]), w=["wblk"], chan="k_w")
            z = blk % 2
            for k in range(8):
                A("pe", MM(Zs[z][0:NS, :], hsT[:, k, :], wblk[:, k, :], start=(k == 0), stop=(k == 7)), r=["hsT", "wblk"], w=["Zs%d" % z])
            if blk == 3:
                A("act", ACT(zs[blk][:, :], Zs[z][0:NS, :], AF.Silu), r=["Zs%d" % z], w=["zs%d" % blk])
            else:
                A("act", ACT(zs[blk][:, :], Zs[z][0:NS, :], AF.Copy), r=["Zs%d" % z], w=["zs%d" % blk])
        A("sp", DMA(o_ks, dk[:, :]), r=["zs5"], chan="k_out")
        A("sp", DMA(o_vs, dv[:, :]), r=["zs6"], chan="k_out")

        A("pool", TS(rk[:, :], rk[:, :], 128 ** -0.5, None, ALU.mult), r=["zs1"], w=["zs1"])
        A("pool", TT(t16[1][:, :], rq[:, :], rk[:, :], ALU.mult), r=["zs0", "zs1"], w=["t1"])
        A("dve", RED(smc(8, 4), t16[1][:, :].rearrange("p (h d) -> p h d", d=128)), r=["t1"], w=["qk"])
        for h in range(4):
            A("pe", TR(MISC[:, h * NS:(h + 1) * NS], rq[:, h * 128:(h + 1) * 128], cst[0:NS, K_ID:K_ID + NS]), r=["zs0", "cst"], w=["MISC"])
        A("act", ACT(rqTs[:, :, :], MISC[:, 0:4 * NS].rearrange("p (h b) -> p h b", b=NS), AF.Copy), r=["MISC"], w=["rqTs"])
        for h in range(4):
            for b in range(NS):
                A("dve", TS(Qz[:, h, b, :], cst2[:, K2_OH + 16 * b:K2_OH + 16 * b + 16], rqTs[:, h, b:b + 1], None, ALU.mult),
                  r=["cst2", "rqTs"], w=["Qz"])
        A("pe", MM(Rps[0:NS, :], cst2[:, K2_Z:K2_Z + NS], cst[:, 0:512], start=True, stop=False), r=["cst2", "cst"], w=["Rps"])
        srv = sret.rearrange("b h d e -> b d h e")
        orv = o_rss.rearrange("b h d e -> b d h e")
        for b in range(NS):
            sbuf = b % 2
            A("sp", DMA(St[sbuf][:, :, :], srv[b]), w=["St%d" % sbuf], chan="k_st%d" % sbuf)
            for h in range(4):
                A("pe", MM(Rps[0:NS, h * 128:(h + 1) * 128], Qz[:, h, b, :], St[sbuf][:, h, :], start=False, stop=(b == NS - 1)),
                  r=["Qz", "St%d" % sbuf], w=["Rps"])
            A("pool", TS(Km[sbuf][:, :], rk[:, :], cst[0:NS, K_ID + b:K_ID + b + 1], None, ALU.mult), r=["zs1", "cst"], w=["Km%d" % sbuf])
            for h in range(4):
                hs = slice(h * 128, (h + 1) * 128)
                A("pe", MM(SN[:, hs], Km[sbuf][:, hs], rv[:, hs], start=True, stop=True), r=["Km%d" % sbuf, "zs2"], w=["SN"])
            for h in range(4):
                hs = slice(h * 128, (h + 1) * 128)
                A("dve", STT(Sn[sbuf][:, h, :], St[sbuf][:, h, :], GAM[h], SN[:, hs], ALU.mult, ALU.add),
                  r=["St%d" % sbuf, "SN"], w=["Sn%d" % sbuf])
            A("sp", DMA(orv[b], Sn[sbuf][:, :, :]), r=["Sn%d" % sbuf], chan="k_sn%d" % sbuf)
        o_ret = t16[1]
        for h in range(4):
            hs = slice(h * 128, (h + 1) * 128)
            A("pool", TS(t16[2][:, hs], rv[:, hs], smc(8 + h), None, ALU.mult), r=["zs2", "qk"], w=["t2"])
        for h in range(4):
            hs = slice(h * 128, (h + 1) * 128)
            A("dve", STT(o_ret[:, hs], Rps[0:NS, hs], GAM[h], t16[2][:, hs], ALU.mult, ALU.add), r=["Rps", "t2"], w=["t1"])
        A("pool", TT(t16[2][:, :], o_ret[:, :], o_ret[:, :], ALU.mult), r=["t1"], w=["t2"])
        A("dve", RED(smc(12, 4), t16[2][:, :].rearrange("p (h d) -> p h d", d=128)), r=["t2"], w=["ss1"])
        A("act", ACT(smc(12, 4), smc(12, 4), AF.Sqrt, scale=1.0 / 128, bias=epsc), r=["ss1"], w=["ss1"])
        A("dve", RCP(smc(12, 4), smc(12, 4)), r=["ss1"], w=["ss1"])
        for h in range(4):
            hs = slice(h * 128, (h + 1) * 128)
            A("act", ACT(t16[2][:, hs], o_ret[:, hs], AF.Copy, scale=smc(12 + h)), r=["t1", "ss1"], w=["t2"])
        A("dve", TT(ms_bf[:, 0:512], t16[2][:, :], rg[:, :], ALU.mult), r=["t2", "zs3"], w=["ms_bf"])

        ck2 = ck.rearrange("(r t) f -> r (t f)", t=2)
        cv2 = cv.rearrange("(r t) f -> r (t f)", t=2)
        A("pe", MM(Oacc[0:32, :], cst2[:, K2_Z:K2_Z + 32], cst[:, 0:512], start=True, stop=False), r=["cst2", "cst"], w=["Oacc"])
        A("pool", MSET(Pacc[:], 0.0), w=["Pacc"])
        first = False
        it = 0
        for b in range(NS):
            qbuf = b % 2
            A("pool", TS(Km[qbuf][:, :], dq[:, :], cst[0:NS, K_ID + b:K_ID + b + 1], None, ALU.mult), r=["zs4", "cst"], w=["Km%d" % qbuf])
            A("pe", MM(QB[:, :], ones_f[0:NS, :], Km[qbuf][:, :], start=True, stop=True), r=["ones_f", "Km%d" % qbuf], w=["QB"])
            A("act", ACT(qb[qbuf][:, :], QB[:, :], AF.Copy), r=["QB"], w=["qb%d" % qbuf])
            for q in range(NQ):
                kb = it % 2
                it += 1
                A("pool", IDMA(Kq[kb][:, :, :].rearrange("p t f -> p (t f)"), ck2, idx[:, b, q:q + 1]), r=["idx"], w=["Kq%d" % kb], chan="k_kq%d" % kb)
                A("pool", IDMA(Vq[kb][:, :, :].rearrange("p t f -> p (t f)"), cv2, idx[:, b, q:q + 1]), r=["idx"], w=["Vq%d" % kb], chan="k_vq%d" % kb)
                A("dve", TT(prod[kb][:, 0, :], Kq[kb][:, 0, :], qb[qbuf][:, :], ALU.mult), r=["Kq%d" % kb, "qb%d" % qbuf], w=["prod%d" % kb])
                A("pool", TT(prod[kb][:, 1, :], Kq[kb][:, 1, :], qb[qbuf][:, :], ALU.mult), r=["Kq%d" % kb, "qb%d" % qbuf], w=["prod%d" % kb])
                A("dve", RED(S4[kb][:, :], prod[kb][:, :, :].rearrange("p t (g d) -> p (t g) d", d=64)), r=["prod%d" % kb], w=["S4_%d" % kb])
                A("pool", TT(S4b[kb][:, :], S4[kb][:, :], cst2[:, K2_BIAS + 16 * q:K2_BIAS + 16 * q + 16], ALU.add), r=["S4_%d" % kb, "cst2"], w=["S4b_%d" % kb])
                A("act", ACT(Pc[kb][:, :, :], S4b[kb][:, :].rearrange("p (t g) -> p t g", g=8), AF.Exp, scale=0.125), r=["S4b_%d" % kb], w=["Pc%d" % kb])
                A("pool", MSET(Pz[kb][:], 0.0), w=["Pz%d" % kb])
                A("dve", CP(Pz[kb][:].rearrange("p t h (m s) -> p t h m s", s=16)[:, :, :, :, b],
                            Pc[kb][:, :, :].rearrange("p t (h m) -> p t h m", m=2)), r=["Pc%d" % kb], w=["Pz%d" % kb])
                for t in range(2):
                    A("pool", TT(Pacc[:, b, :], Pacc[:, b, :], Pc[kb][:, t, :], ALU.add), r=["Pc%d" % kb, "Pacc"], w=["Pacc"])
                A("act", ACT(Vb[kb][:, :, :], Vq[kb][:, :, :], AF.Copy), r=["Vq%d" % kb], w=["Vb%d" % kb])
                last = (b == NS - 1 and q == NQ - 1)
                for t in range(2):
                    for h in range(4):
                        hs = slice(h * 128, (h + 1) * 128)
                        A("pe", MM(Oacc[0:32, hs], Pz[kb][:, t, h, :], Vb[kb][:, t, hs], start=(first and t == 0), stop=(last and t == 1)),
                          r=["Pz%d" % kb, "Vb%d" % kb], w=["Oacc"])
                first = False
        if L.get("dbg") == "pacc":
            A("sp", DMA(y_s.rearrange("b (x c) -> (b x) c", x=8), Pacc[:].rearrange("p b g -> p (b g)")), r=["Pacc"], chan="k_out")
            S.emit(nc, sem_stack=esP)
            return
        O_sb = t16[0]
        A("act", ACT(O_sb[:, :], Oacc[0:32, :], AF.Copy), r=["Oacc"], w=["t0"])
        A("pe", MM(Zs[0][0:NS, :], cst[0:32, K_ID + 16:K_ID + 32], O_sb[:, :], start=True, stop=True), r=["cst", "t0"], w=["Zs0"])
        for b in range(NS):
            A("pe", MM(MISC[0:NS, 64:72], cst2[:, K2_OH + 16 * b:K2_OH + 16 * b + 16], Pacc[:, b, :], start=(b == 0), stop=(b == NS - 1)),
              r=["cst2", "Pacc"], w=["MISC"])
        A("pool", TT(t16[3][:, :], dq[:, :], dk[:, :], ALU.mult), r=["zs4", "zs5"], w=["t3"])
        A("dve", RED(smc(16, 8), t16[3][:, :].rearrange("p (g d) -> p g d", d=64)), r=["t3"], w=["pnew"])
        A("act", ACT(smc(16, 8), smc(16, 8), AF.Exp, scale=0.125), r=["pnew"], w=["pnew"])
        A("act", ACT(smc(24, 8), MISC[0:NS, 64:72], AF.Copy), r=["MISC"], w=["den"])
        A("pool", TT(smc(24, 8), smc(24, 8), smc(16, 8), ALU.add), r=["den", "pnew"], w=["den"])
        A("dve", RCP(smc(24, 8), smc(24, 8)), r=["den"], w=["den"])
        Ot = [t16[3], t16[4]]
        for h in range(4):
            hs = slice(h * 128, (h + 1) * 128)
            A("dve", STT(Ot[0][:, hs], dv[:, hs], smc(16 + 2 * h), O_sb[0:NS, hs], ALU.mult, ALU.add), r=["zs6", "pnew", "t0"], w=["t3"])
            A("dve", STT(Ot[1][:, hs], dv[:, hs], smc(17 + 2 * h), Zs[0][0:NS, hs], ALU.mult, ALU.add), r=["zs6", "pnew", "Zs0"], w=["t4"])
        on = [t16[5], t16[6]]
        for h in range(4):
            hs = slice(h * 128, (h + 1) * 128)
            A("act", ACT(on[0][:, hs], Ot[0][:, hs], AF.Copy, scale=smc(24 + 2 * h)), r=["t3", "den"], w=["t5"])
            A("act", ACT(on[1][:, hs], Ot[1][:, hs], AF.Copy, scale=smc(25 + 2 * h)), r=["t4", "den"], w=["t6"])
        do = t16[1]
        A("dve", STT(do[:, :], on[1][:, :], neglam, on[0][:, :], ALU.mult, ALU.add), r=["t5", "t6"], w=["t1"])
        if L.get("dbg") == "do":
            A("sp", DMA(y_s[:, 0:512], do[:, :]), r=["t1"], chan="k_out")
            A("sp", DMA(y_s[:, 512:520], smc(24, 8)), r=["den"], chan="k_out")
            A("sp", DMA(y_s[:, 520:528], smc(16, 8)), r=["pnew"], chan="k_out")
            S.emit(nc, sem_stack=esP)
            return
        A("pool", TT(t16[3][:, :], do[:, :], do[:, :], ALU.mult), r=["t1"], w=["t3"])
        A("dve", RED(smc(32, 4), t16[3][:, :].rearrange("p (h d) -> p h d", d=128)), r=["t3"], w=["ss2"])
        A("act", ACT(smc(32, 4), smc(32, 4), AF.Sqrt, scale=1.0 / 128, bias=epsc), r=["ss2"], w=["ss2"])
        A("dve", RCP(smc(32, 4), smc(32, 4)), r=["ss2"], w=["ss2"])
        for h in range(4):
            hs = slice(h * 128, (h + 1) * 128)
            A("act", ACT(t16[3][:, hs], do[:, hs], AF.Copy, scale=smc(32 + h)), r=["t1", "ss2"], w=["t3"])
        A("dve", TT(ms_bf[:, 512:1024], t16[3][:, :], gsub_bc[:].rearrange("p h e -> p (h e)"), ALU.mult), r=["t3", "gsub_bc"], w=["ms_bf"])

        if L.get("dbg") == "ms":
            A("pool", DMA(y_s, ms_bf[:, :]), r=["ms_bf"], chan="k_out")
            S.emit(nc, sem_stack=esP)
            return
        msT = hsT
        transpose16(ms_bf, "ms_bf", msT, "hsT", 8)
        for hf in range(2):
            for c in range(8):
                A("pe", MM(Zs[hf][0:NS, :], msT[:, c, :], wo_sb[:, c, hf * 512:(hf + 1) * 512], start=(c == 0), stop=(c == 7)),
                  r=["hsT"], w=["Zs%d" % hf])
        postnorm16([0, 1], gB, "gB", xs_t, "xs_t", 36)

        A("sp", DMA(gA[:], n_fpre.to_broadcast([NS, D])), w=["gA"], chan="k_g2")
        A("sp", DMA(gB[:], n_fpost.to_broadcast([NS, D])), w=["gB"], chan="k_g2")
        rms16(xs_t, "xs_t", gA, "gA", hs_bf, "hs_bf", 38)
        transpose16(hs_bf, "hs_bf", hsT, "hsT", 8)
        wfiv = w_fi.rearrange("(k p) c -> p k c", p=128)
        for i in range(11):
            cs_ = slice(i * 256, (i + 1) * 256)
            A("pool", DMA(wblk[:, :, 0:256], wfiv[:, :, i * 256:(i + 1) * 256]), w=["wblk"], chan="k_w")
            A("pool", DMA(wblk[:, :, 256:512], wfiv[:, :, DFF + i * 256:DFF + (i + 1) * 256]), w=["wblk"], chan="k_w")
            for j in range(3):
                A("sp", DMA(cwb[:, j, :], convw[j:j + 1, cs_].to_broadcast([NS, 256])), w=["cwb"], chan="k_cw")
            A("sp", DMA(cwb[:, 3, :], convb[0:1, cs_].to_broadcast([NS, 256])), w=["cwb"], chan="k_cw")
            A("sp", DMA(scb[:, :, :], sconv[:, :, cs_]), w=["scb"], chan="k_cw")
            for k in range(8):
                A("pe", MM(Zs[0][0:NS, 0:256], hsT[:, k, :], wblk[:, k, 0:256], start=(k == 0), stop=(k == 7)), r=["hsT", "wblk"], w=["Zs0"])
            for k in range(8):
                A("pe", MM(Zs[1][0:NS, 0:256], hsT[:, k, :], wblk[:, k, 256:512], start=(k == 0), stop=(k == 7)), r=["hsT", "wblk"], w=["Zs1"])
            gS = cc[0]
            A("act", ACT(gS[:, :], Zs[0][0:NS, 0:256], AF.Copy), r=["Zs0"], w=["cc0"])
            A("sp", DMA(o_cs[:, 1, cs_], gS[:, :]), r=["cc0"], chan="k_gs")
            A("pool", TT(cc[1][:, :], gS[:, :], cwb[:, 2, :], ALU.mult), r=["cc0", "cwb"], w=["cc1"])
            A("pool", TT(cc[1][:, :], cc[1][:, :], cwb[:, 3, :], ALU.add), r=["cc1", "cwb"], w=["cc1"])
            A("pool", TT(cc[2][:, :], scb[:, 0, :], cwb[:, 0, :], ALU.mult), r=["scb", "cwb"], w=["cc2"])
            A("pool", TT(cc[1][:, :], cc[1][:, :], cc[2][:, :], ALU.add), r=["cc1", "cc2"], w=["cc1"])
            A("pool", TT(cc[2][:, :], scb[:, 1, :], cwb[:, 1, :], ALU.mult), r=["scb", "cwb"], w=["cc2"])
            A("pool", TT(cc[1][:, :], cc[1][:, :], cc[2][:, :], ALU.add), r=["cc1", "cc2"], w=["cc1"])
            A("act", ACT(cc[2][:, :], cc[1][:, :], AF.Gelu_apprx_tanh), r=["cc1"], w=["cc2"])
            A("dve", TT(a2_bf[:, cs_], Zs[1][0:NS, 0:256], cc[2][:, :], ALU.mult), r=["Zs1", "cc2"], w=["a2_bf"])
        transpose16(a2_bf, "a2_bf", a2T, "a2T", NFC)
        for hf in range(2):
            for fc in range(NFC):
                A("pe", MM(Zs[hf][0:NS, :], a2T[:, fc, :], wfo_sb[:, fc, hf * 512:(hf + 1) * 512], start=(fc == 0), stop=(fc == NFC - 1)),
                  r=["a2T"], w=["Zs%d" % hf])
        postnorm16([0, 1], gB, "gB", xs_t, "xs_t", 40)
        A("sp", DMA(y_s, xs_t[:, :]), r=["xs_t"], chan="k_out")
        print("n_ops sample", len(S.ops))
        S.emit(nc, sem_stack=esP)

def build_program(n_pool, with_sample=True, limit=None, dbg=None):
    nc = bass.Bass("TRN2", target_bir_lowering=False)
    S = Sched()
    dt = nc.dram_tensor

    def inp(name, shape, dtp=F32):
        return dt(name, list(shape), dtp, kind="ExternalInput").ap()

    def outp(name, shape):
        return dt(name, list(shape), F32, kind="ExternalOutput").ap()

    xp = inp("xp", [T, D])
    w_in = inp("w_in", [D, INW])
    w_o = inp("w_o", [D, D])
    w_fi = inp("w_fi", [D, 2 * DFF])
    w_fo = inp("w_fo", [DFF, D])
    n_mpre = inp("n_mpre", [1, D])
    n_mpost = inp("n_mpost", [1, D])
    n_fpre = inp("n_fpre", [1, D])
    n_fpost = inp("n_fpost", [1, D])
    lq1 = inp("lq1", [1, 64])
    lk1 = inp("lk1", [1, 64])
    lq2 = inp("lq2", [1, 64])
    lk2 = inp("lk2", [1, 64])
    subg = inp("subg", [1, 128])
    convw = inp("convw", [3, DFF])
    convb = inp("convb", [1, DFF])
    consts_d = inp("consts", [128, K_END])
    shift_d = inp("shiftrow", [1, 2048])
    consts2_d = inp("consts2", [128, K2_END])
    xs = inp("xs", [NS, D])
    sret = inp("sret", [NS, 4, 128, 128])
    ck = inp("ck", [n_pool * 128, 512])
    cv = inp("cv", [n_pool * 128, 512])
    sconv = inp("sconv", [NS, 2, DFF])
    ptab = inp("ptab", [NS, 16], I32)

    y_p = outp("y_p", [T, D])
    y_s = outp("y_s", [NS, D])
    o_rsp = outp("o_rsp", [4, 128, 128])
    o_rss = outp("o_rss", [NS, 4, 128, 128])
    o_kp = outp("o_kp", [T, 512])
    o_vp = outp("o_vp", [T, 512])
    o_ks = outp("o_ks", [NS, 512])
    o_vs = outp("o_vs", [NS, 512])
    o_cp = outp("o_cp", [2, DFF])
    o_cs = outp("o_cs", [NS, 2, DFF])
    x1_d = dt("x1_scratch", [T, D], F32, kind="Internal").ap()

    esP = contextlib.ExitStack()
    es = contextlib.ExitStack()
    with esP, es:
        def sbP(name, shape, dtp=F32):
            return esP.enter_context(nc.sbuf_tensor(name, list(shape), dtp))

        def sb(name, shape, dtp=F32):
            return es.enter_context(nc.sbuf_tensor(name, list(shape), dtp))

        def ps(name, shape, dtp=F32):
            return es.enter_context(nc.psum_tensor(name, list(shape), dtp))

        cst = sbP("cst", [128, K_END])
        ident_bf = sbP("ident_bf", [128, 128], BF16)
        ones_bf = sbP("ones_bf", [128, 128], BF16)
        ones_f = sbP("ones_f", [128, 128])
        lamw = sbP("lamw", [128, 8])
        subgc = sbP("subgc", [128, 2])
        cwc = sbP("cwc", [128, NFC, 4])
        wbig = sbP("wbig", [128, 8, INW], BF16)
        wo_sb = sbP("wo_sb", [128, 8, D], BF16)
        scol = sbP("scol", [128, 8])
        shift_bf = sb("shift_bf", [1, 2048], BF16)
        gpre = sb("gpre", [128, D])
        gpost = sb("gpost", [128, D])
        lamt = sb("lamt", [128, 4, 64])
        kT = sb("kT", [128, 4, T], BF16)
        dv_sb = sb("dv_sb", [128, NT, 512], BF16)
        xt = [sb("xt%d" % i, [128, D]) for i in range(2)]
        hbf = [sb("hbf%d" % i, [128, D], BF16) for i in range(2)]
        hT = sb("hT", [128, 8, GT], BF16)
        rqT = sb("rqT", [128, 4, GT], BF16)
        rkT = sb("rkT", [128, 4, GT], BF16)
        rgT = sb("rgT", [128, 4, GT], BF16)
        dqT = sb("dqT", [128, 4, GT], BF16)
        rv_sb = sb("rv_sb", [128, 4, 512], BF16)
        kd_sb = sb("kd_sb", [128, 4, 512], BF16)
        mT = sb("mT", [128, 8, GT], BF16)
        stg = [sb("stg%d" % i, [128, 512]) for i in range(2)]
        PT = [sb("PT%d" % i, [128, 512], BF16) for i in range(3)]
        attm = [sb("attm%d" % i, [128, 128], BF16) for i in range(2)]
        qd = [sb("qd%d" % i, [128, 128], BF16) for i in range(2)]
        oT = sb("oT", [128, 4, GT])
        Sf = sb("Sf", [128, 4, 128])
        Sb = sb("Sb", [128, 4, 128], BF16)
        onT = [oT[:, 0, :], oT[:, 1, :]]
        rden = oT[:, 2, :]
        doT = oT[:, 3, :]
        sq = sb("sq", [128, 512])
        rstd = sb("rstd", [128, 512])
        tmpA = sb("tmpA", [128, 512])
        gpv_t = sb("gpv_t", [128, 2, NFC])

        Z = [ps("Z%d" % i, [128, 512]) for i in range(2)]
        SC = [ps("SC%d" % i, [128, 512]) for i in range(2)]
        AV = ps("AV", [128, 512])
        DEN = ps("DEN", [128, 512])
        RT = ps("RT", [128, 512])
        TP = ps("TP", [128, 8, 128], BF16)
        SFK = ["Sf0", "Sf1", "Sf2", "Sf3"]
        SBK = ["Sb0", "Sb1", "Sb2", "Sb3"]

        S.add("sp", DMA(cst[:], consts_d), w=["cst"], chan="c_cst")
        S.add("pool", DMA(shift_bf[:], shift_d), w=["shift_bf"], chan="c_win")
        S.add("sp", DMA(gpre[:], n_mpre.to_broadcast([128, D])), w=["gpre"], chan="c_cst")
        S.add("sp", DMA(gpost[:], n_mpost.to_broadcast([128, D])), w=["gpost"], chan="c_cst")
        for i, d_ in enumerate((lq1, lk1, lq2, lk2)):
            S.add("sp", DMA(lamt[:, i, :], d_.to_broadcast([128, 64])), w=["lamt"], chan="c_cst")
        S.add("sp", DMA(subgc[:, 0:1], subg.rearrange("o e -> e o"), slow=True), w=["subgc"], chan="c_cst")
        for j in range(3):
            S.add("sp", DMA(cwc[:, :, j:j + 1], convw[j:j + 1, :].rearrange("o (c p) -> p c o", p=128), slow=True),
                  w=["cwc"], chan="c_cst")
        S.add("sp", DMA(cwc[:, :, 3:4], convb.rearrange("o (c p) -> p c o", p=128), slow=True), w=["cwc"], chan="c_cst")
        for c in range(8):
            S.add("pool", DMA(wbig[:, c, :], w_in[c * 128:(c + 1) * 128, :]), w=["wbig"], chan="c_win")
        for c in range(8):
            S.add("pool", DMA(wo_sb[:, c, :], w_o[c * 128:(c + 1) * 128, :]), w=["wo_sb"], chan="c_wo")

        S.add("dve", CP(ident_bf[:], cst[:, K_ID:K_ID + 128]), r=["cst"], w=["ident_bf"])
        S.add("dve", MSET(ones_bf[:], 1.0), w=["ones_bf"])
        S.add("dve", MSET(ones_f[:], 1.0), w=["ones_f"])
        S.add("dve", MSET(Sf[:], 0.0), w=SFK)
        S.add("dve", MSET(Sb[:], 0.0), w=SBK)
        S.add("dve", MSET(gpv_t[:], 0.0), w=["gpv"])
        S.add("pool", TT(lamt[:, 0, :], lamt[:, 0, :], lamt[:, 1, :], ALU.mult), r=["lamt"], w=["lamt"])
        S.add("pool", TT(lamt[:, 2, :], lamt[:, 2, :], lamt[:, 3, :], ALU.mult), r=["lamt"], w=["lamt"])
        S.add("dve", RED(lamw[:, 0:1], lamt[:, 0, :]), r=["lamt"], w=["lamw"])
        S.add("dve", RED(lamw[:, 1:2], lamt[:, 2, :]), r=["lamt"], w=["lamw"])
        S.add("act", ACT(lamw[:, 3:5], lamw[:, 0:2], AF.Exp), r=["lamw"], w=["lamw2"])
        S.add("dve", STT(lamw[:, 2:3], lamw[:, 4:5], -LAM_INIT, lamw[:, 3:4], ALU.add, ALU.subtract), r=["lamw2"], w=["neglam"])
        S.add("dve", TS(subgc[:, 1:2], subgc[:, 0:1], 1.0 - LAM_INIT, None, ALU.mult), r=["subgc"], w=["subgc2"])
        neglam = lamw[:, 2:3]
        gsub = subgc[:, 1:2]

        zc = [0]

        def nextZ():
            zc[0] += 1
            return zc[0] % 2

        def sck(c):
            return "scol%d" % c

        def rms_to_bf16(src, src_key, g_tile, g_key, out_bf, out_key, col):
            a, b_ = scol[:, col:col + 1], scol[:, col + 1:col + 2]
            S.add("act", ACT(tmpA[:, 0:512], src[:, 0:512], AF.Square, accum_out=a), r=[src_key], w=["tmpA", sck(col)])
            S.add("act", ACT(tmpA[:, 0:512], src[:, 512:1024], AF.Square, accum_out=b_), r=[src_key], w=["tmpA", sck(col + 1)])
            S.add("dve", TT(a, a, b_, ALU.add), r=[sck(col), sck(col + 1)], w=[sck(col)])
            S.add("act", ACT(a, a, AF.Sqrt, scale=1.0 / D, bias=epsc), r=[sck(col), "epsc"], w=[sck(col)])
            S.add("dve", RCP(a, a), r=[sck(col)], w=[sck(col)])
            for hf in range(2):
                hs = slice(hf * 512, (hf + 1) * 512)
                S.add("act", ACT(tmpA[:, :], src[:, hs], AF.Copy, scale=a), r=[src_key, sck(col)], w=["tmpA"])
                S.add("dve", TT(out_bf[:, hs], tmpA[:, :], g_tile[:, hs], ALU.mult), r=["tmpA", g_key], w=[out_key])

        def post_norm_residual(zb, g_tile, g_key, res, res_key, col):
            a, b_ = scol[:, col:col + 1], scol[:, col + 1:col + 2]
            S.add("act", ACT(tmpA[:, :], Z[zb[0]][:, :], AF.Square, accum_out=a), r=["Z%d" % zb[0]], w=["tmpA", sck(col)])
            S.add("act", ACT(tmpA[:, :], Z[zb[1]][:, :], AF.Square, accum_out=b_), r=["Z%d" % zb[1]], w=["tmpA", sck(col + 1)])
            S.add("dve", TT(a, a, b_, ALU.add), r=[sck(col), sck(col + 1)], w=[sck(col)])
            S.add("act", ACT(a, a, AF.Sqrt, scale=1.0 / D, bias=epsc), r=[sck(col), "epsc"], w=[sck(col)])
            S.add("dve", RCP(a, a), r=[sck(col)], w=[sck(col)])
            for hf in range(2):
                hs = slice(hf * 512, (hf + 1) * 512)
                S.add("act", ACT(tmpA[:, :], Z[zb[hf]][:, :], AF.Copy, scale=a), r=["Z%d" % zb[hf], sck(col)], w=["tmpA"])
                S.add("dve", TT(tmpA[:, :], tmpA[:, :], g_tile[:, hs], ALU.mult), r=["tmpA", g_key], w=["tmpA"])
                S.add("dve", TT(res[:, hs], res[:, hs], tmpA[:, :], ALU.add), r=["tmpA", res_key], w=[res_key])

        def transpose_tile(src_bf, src_key, dstT, dst_key, col0):
            for c in range(8):
                S.add("pe", TR(TP[:, c, :], src_bf[:, c * 128:(c + 1) * 128], ident_bf[:]), r=[src_key, "ident_bf"], w=["TP"])
            S.add("act", ACT(dstT[:, :, col0:col0 + 128], TP[:, :, :], AF.Copy), r=["TP"], w=[dst_key])

        epsc = scol[:, 7:8]
        S.add("dve", MSET(epsc, EPS), w=["epsc"])

        xpv = xp.rearrange("(t p) d -> t p d", p=128)
        x1v = x1_d.rearrange("(t p) d -> t p d", p=128)
        okp = o_kp.rearrange("(t p) d -> t p d", p=128)
        ovp = o_vp.rearrange("(t p) d -> t p d", p=128)
        stc = [0]

        def featproj(col0, eng, fn_of_z, wkeys):
            z = nextZ()
            for k in range(8):
                S.add("pe", MM(Z[z][:, :], wbig[:, k, col0:col0 + 128], hT[:, k, :], start=(k == 0), stop=(k == 7)),
                      r=["wbig", "hT"], w=["Z%d" % z])
            S.add(eng, fn_of_z(Z[z][:, :]), r=["Z%d" % z], w=wkeys)

        def tokproj(tt, col0):
            z = nextZ()
            for k in range(8):
                S.add("pe", MM(Z[z][:, :], hT[:, k, tt * 128:(tt + 1) * 128], wbig[:, k, col0:col0 + 512], start=(k == 0), stop=(k == 7)),
                      r=["wbig", "hT"], w=["Z%d" % z])
            return z

        for g in range(NG):
            for tt in range(4):
                ti = 4 * g + tt
                b = ti % 2
                S.add("sp", DMA(xt[b][:], xpv[ti]), w=["xt%d" % b], chan="c_xt%d" % b)
                rms_to_bf16(xt[b], "xt%d" % b, gpre, "gpre", hbf[b], "hbf%d" % b, 0)
                transpose_tile(hbf[b], "hbf%d" % b, hT, "hT", tt * 128)
            for h in range(4):
                featproj(C_RQ + 128 * h, "act", lambda zz, h=h: ACT(rqT[:, h, :], zz, AF.Copy), ["rqT"])
                featproj(C_RK + 128 * h, "dve", lambda zz, h=h: CP(rkT[:, h, :], zz), ["rkT"])
                featproj(C_RG + 128 * h, "act", lambda zz, h=h: ACT(rgT[:, h, :], zz, AF.Silu), ["rgT"])
                featproj(C_DQ + 128 * h, "dve", lambda zz, h=h: CP(dqT[:, h, :], zz), ["dqT"])
                featproj(C_DK + 128 * h, "act", lambda zz, h=h, g=g: ACT(kT[:, h, g * GT:(g + 1) * GT], zz, AF.Copy), ["kT"])
            for tt in range(4):
                ti = 4 * g + tt
                z = tokproj(tt, C_RV)
                S.add("act", ACT(rv_sb[:, tt, :], Z[z][:, :], AF.Copy), r=["Z%d" % z], w=["rv_sb"])
                z = tokproj(tt, C_RK)
                S.add("dve", TT(kd_sb[:, tt, :], Z[z][:, :], cst[:, K_KDEC:K_KDEC + 512], ALU.mult), r=["Z%d" % z, "cst"], w=["kd_sb"])
                z = tokproj(tt, C_DK)
                si = stc[0] % 2
                stc[0] += 1
                S.add("act", ACT(stg[si][:, :], Z[z][:, :], AF.Copy), r=["Z%d" % z], w=["stg%d" % si])
                S.add("sp", DMA(okp[ti], stg[si][:, :]), r=["stg%d" % si], chan="c_stg%d" % si)
                z = tokproj(tt, C_DV)
                si = stc[0] % 2
                stc[0] += 1
                S.add("act", ACT(stg[si][:, :], Z[z][:, :], AF.Copy), r=["Z%d" % z], w=["stg%d" % si])
                S.add("dve", CP(dv_sb[:, ti, :], Z[z][:, :]), r=["Z%d" % z], w=["dv_sb"])
                S.add("sp", DMA(ovp[ti], stg[si][:, :]), r=["stg%d" % si], chan="c_stg%d" % si)

            for ci in range(4):
                cs = slice(ci * 128, (ci + 1) * 128)
                for h in range(4):
                    ab = (ci * 4 + h) % 2
                    hs = slice(128 * h, 128 * (h + 1))
                    S.add("pe", MM(RT[:, 0:128], rkT[:, h, cs], rqT[:, h, cs]), r=["rkT", "rqT"], w=["RT"])
                    S.add("dve", TT(attm[ab][:, :], RT[:, 0:128], cst[:, K_DECAY + 128 * h:K_DECAY + 128 * (h + 1)], ALU.mult),
                          r=["RT", "cst"], w=["attm%d" % ab])
                    S.add("pool", TT(qd[ab][:, :], rqT[:, h, cs], cst[:, K_QDEC + 128 * h:K_QDEC + 128 * (h + 1)], ALU.mult),
                          r=["rqT", "cst"], w=["qd%d" % ab])
                    S.add("pe", MM(AV[:, 0:128], rv_sb[:, ci, hs], attm[ab][:, :], start=True, stop=False),
                          r=["rv_sb", "attm%d" % ab], w=["AV"])
                    S.add("pe", MM(AV[:, 0:128], Sb[:, h, :], qd[ab][:, :], start=False, stop=True),
                          r=[SBK[h], "qd%d" % ab], w=["AV"])
                    S.add("act", ACT(oT[:, h, cs], AV[:, 0:128], AF.Copy), r=["AV"], w=["oT"])
                    S.add("pe", MM(SC[0][:, 0:128], kd_sb[:, ci, hs], rv_sb[:, ci, hs]), r=["kd_sb", "rv_sb"], w=["SC0"])
                    S.add("dve", STT(Sf[:, h, :], Sf[:, h, :], GAM[h] ** 128, SC[0][:, 0:128], ALU.mult, ALU.add),
                          r=["SC0", SFK[h]], w=[SFK[h]])
                    S.add("act", ACT(Sb[:, h, :], Sf[:, h, :], AF.Copy), r=[SFK[h]], w=[SBK[h]])
            for h in range(4):
                S.add("act", ACT(sq[:, :], oT[:, h, :], AF.Square), r=["oT"], w=["sq"])
                S.add("pe", MM(DEN[:, :], ones_f[:, :], sq[:, :]), r=["ones_f", "sq"], w=["DEN"])
                S.add("act", ACT(rstd[:, :], DEN[:, :], AF.Sqrt, scale=1.0 / 128, bias=epsc), r=["DEN", "epsc"], w=["rstd"])
                S.add("dve", RCP(rstd[:, :], rstd[:, :]), r=["rstd"], w=["rstd"])
                S.add("dve", TT(rstd[:, :], rstd[:, :], rgT[:, h, :], ALU.mult), r=["rstd", "rgT"], w=["rstd"])
                S.add("dve", TT(mT[:, h, :], rstd[:, :], oT[:, h, :], ALU.mult), r=["rstd", "oT"], w=["mT"])

            nkb = 4 * g + 4
            items = [(h, m, kb) for h in range(4) for m in range(2) for kb in range(nkb)]

            def emit_score(i):
                h, m, kb = items[i]
                rows = slice(64 * m, 64 * (m + 1))
                c0 = 128 * max(0, kb - 4 * g)
                sc = i % 2
                S.add("pe", MM(SC[sc][:, c0:512], kT[rows, h, kb * 128:(kb + 1) * 128], dqT[rows, h, c0:512], start=True, stop=False),
                      r=["kT", "dqT"], w=["SC%d" % sc])
                S.add("pe", MM(SC[sc][:, c0:512], ones_bf[0:1, :], shift_bf[0:1, 512 * h:512 * h + 512 - c0], start=False, stop=True),
                      r=["ones_bf", "shift_bf"], w=["SC%d" % sc])
            emit_score(0)
            for i, (h, m, kb) in enumerate(items):
                hs = slice(128 * h, 128 * (h + 1))
                r_ = max(0, kb - 4 * g)
                c0 = 128 * r_
                sc = i % 2
                pb = i % 3
                bidx = K_ABIAS + 16 * h + (4 * g + r_ - kb) + 3
                last = (kb >= 4 * g)
                if i + 1 < len(items):
                    emit_score(i + 1)
                S.add("act", ACT(PT[pb][:, c0:512], SC[sc][:, c0:512], AF.Exp, scale=0.125, bias=cst[:, bidx:bidx + 1]),
                      r=["SC%d" % sc, "cst"], w=["PT%d" % pb])
                if last:
                    S.add("pool", TT(PT[pb][:, c0:c0 + 128], PT[pb][:, c0:c0 + 128], cst[:, K_CAUS:K_CAUS + 128], ALU.mult),
                          r=["PT%d" % pb, "cst"], w=["PT%d" % pb])
                S.add("pe", MM(AV[:, c0:512], dv_sb[:, kb, hs], PT[pb][:, c0:512], start=(kb == 0), stop=last),
                      r=["dv_sb", "PT%d" % pb], w=["AV"])
                S.add("pe", MM(DEN[:, c0:512], ones_bf[:, :], PT[pb][:, c0:512], start=(kb == 0), stop=last),
                      r=["ones_bf", "PT%d" % pb], w=["DEN"])
                if kb == nkb - 1:
                    S.add("dve", RCP(rden, DEN[:, :]), r=["DEN"], w=["oT"])
                    S.add("dve", TT(onT[m], AV[:, :], rden, ALU.mult), r=["AV", "oT"], w=["oT"])
                    if m == 1:
                        S.add("dve", STT(doT, onT[1], neglam, onT[0], ALU.mult, ALU.add), r=["oT", "neglam"], w=["oT"])
                        S.add("act", ACT(sq[:, :], doT, AF.Square), r=["oT"], w=["sq"])
                        S.add("pe", MM(RT[:, :], ones_f[:, :], sq[:, :]), r=["ones_f", "sq"], w=["RT"])
                        S.add("act", ACT(rstd[:, :], RT[:, :], AF.Sqrt, scale=1.0 / 128, bias=epsc), r=["RT", "epsc"], w=["rstd"])
                        S.add("dve", RCP(rstd[:, :], rstd[:, :]), r=["rstd"], w=["rstd"])
                        S.add("dve", STT(mT[:, 4 + h, :], doT, gsub, rstd[:, :], ALU.mult, ALU.mult), r=["oT", "rstd", "subgc2"], w=["mT"])

            if dbg == "m" and g == 0:
                S.add("pool", DMA(y_p[0:1024, 0:512].rearrange("(c p) t -> p c t", p=128), mT[:, :, :]), r=["mT"], chan="c_misc")
                S.emit(nc, sem_stack=esP)
                return nc
            for tt in range(4):
                ti = 4 * g + tt
                b = ti % 2
                S.add("sp", DMA(xt[b][:], xpv[ti]), w=["xt%d" % b], chan="c_xt%d" % b)
                zb = []
                for hf in range(2):
                    z = nextZ()
                    zb.append(z)
                    for c in range(8):
                        S.add("pe", MM(Z[z][:, :], mT[:, c, tt * 128:(tt + 1) * 128], wo_sb[:, c, hf * 512:(hf + 1) * 512], start=(c == 0), stop=(c == 7)),
                              r=["mT", "wo_sb"], w=["Z%d" % z])
                post_norm_residual(zb, gpost, "gpost", xt[b], "xt%d" % b, 2)
                S.add("sp", DMA(x1v[ti], xt[b][:]), r=["xt%d" % b], w=["x1d%d" % ti], chan="c_xo%d" % b)

        S.add("sp", DMA(o_rsp.rearrange("h d e -> d h e"), Sf[:, :, :]), r=SFK, chan="c_misc")
        if dbg == "x1":
            S.add("sp", DMA(y_p, x1_d), r=["x1d%d" % i for i in range(NT)], chan="c_misc")
            S.emit(nc, sem_stack=esP)
            return nc

        S.add("sp", DMA(gpre[:], n_fpre.to_broadcast([128, D])), w=["gpre"], chan="c_g2")
        S.add("sp", DMA(gpost[:], n_fpost.to_broadcast([128, D])), w=["gpost"], chan="c_g2")
        wfo_sb = wbig.rearrange("p a b -> p (a b)")[:, 0:NFC * D].rearrange("p (c d) -> p c d", d=D)
        for c in range(NFC):
            S.add("pool", DMA(wfo_sb[:, c, :], w_fo[c * 128:(c + 1) * 128, :]), w=["wbig"], chan="c_wfo")
        aT = kT.rearrange("p a b -> p (a b)")
        aT2 = dv_sb.rearrange("p a b -> p (a b)")

        def aTc(fc):
            if fc < 16:
                return aT[:, fc * 512:(fc + 1) * 512], "kT"
            return aT2[:, (fc - 16) * 512:(fc - 15) * 512], "dv_sb"
        wfi = [t_.rearrange("p a b -> p (a b)").rearrange("p (k c) -> p k c", c=256) for t_ in (rqT, rkT, rgT, dqT)]
        wfik = ["rqT", "rkT", "rgT", "dqT"]
        NWB = 4
        gsb = [oT[:, 0, :], oT[:, 1, :]]
        c1 = sq
        c2 = rstd
        yv = y_p.rearrange("(t p) d -> t p d", p=128)
        wfiv = w_fi.rearrange("(k p) c -> p k c", p=128)

        for g in range(NG):
            for tt in range(4):
                ti = 4 * g + tt
                b = ti % 2
                S.add("sp", DMA(xt[b][:], x1v[ti]), r=["x1d%d" % ti], w=["xt%d" % b], chan="c_xt%d" % b)
                rms_to_bf16(xt[b], "xt%d" % b, gpre, "gpre", hbf[b], "hbf%d" % b, 0)
                transpose_tile(hbf[b], "hbf%d" % b, hT, "hT", tt * 128)
            for fc in range(NFC):
                wb = (g * NFC + fc) % NWB
                S.add("pool", DMA(wfi[wb][:, :, 0:128], wfiv[:, :, fc * 128:(fc + 1) * 128]), w=[wfik[wb]], chan="c_wfi%d" % wb)
                S.add("pool", DMA(wfi[wb][:, :, 128:256], wfiv[:, :, DFF + fc * 128:DFF + (fc + 1) * 128]), w=[wfik[wb]], chan="c_wfi%d" % wb)
                if fc % 2 == 0:
                    G_, U_, gkey, ukey = SC[0], SC[1], "SC0", "SC1"
                else:
                    G_, U_, gkey, ukey = AV, DEN, "AV", "DEN"
                for k in range(8):
                    S.add("pe", MM(G_[:, :], wfi[wb][:, k, 0:128], hT[:, k, :], start=(k == 0), stop=(k == 7)), r=[wfik[wb], "hT"], w=[gkey])
                for k in range(8):
                    S.add("pe", MM(U_[:, :], wfi[wb][:, k, 128:256], hT[:, k, :], start=(k == 0), stop=(k == 7)), r=[wfik[wb], "hT"], w=[ukey])
                gs = gsb[fc % 2]
                gk = "gsb%d" % (fc % 2)
                w0, w1, w2, bb = cwc[:, fc, 0:1], cwc[:, fc, 1:2], cwc[:, fc, 2:3], cwc[:, fc, 3:4]
                S.add("act", ACT(gs[:, :], G_[:, :], AF.Copy), r=[gkey], w=[gk])
                S.add("act", ACT(c1[:, :], G_[:, :], AF.Copy, scale=w2), r=[gkey, "cwc"], w=["c1"])
                S.add("dve", STT(c1[:, 1:512], gs[:, 0:511], w1, c1[:, 1:512], ALU.mult, ALU.add), r=[gk, "c1", "cwc"], w=["c1"])
                S.add("dve", STT(c1[:, 0:1], gpv_t[:, 1:2, fc], w1, c1[:, 0:1], ALU.mult, ALU.add), r=["gpv", "c1", "cwc"], w=["c1"])
                S.add("dve", STT(c1[:, 2:512], gs[:, 0:510], w0, c1[:, 2:512], ALU.mult, ALU.add), r=[gk, "c1", "cwc"], w=["c1"])
                S.add("dve", STT(c1[:, 0:2], gpv_t[:, 0:2, fc], w0, c1[:, 0:2], ALU.mult, ALU.add), r=["gpv", "c1", "cwc"], w=["c1"])
                S.add("dve", CP(gpv_t[:, 0:2, fc], gs[:, 510:512]), r=[gk, "gpv"], w=["gpv"])
                S.add("act", ACT(c2[:, :], c1[:, :], AF.Gelu_apprx_tanh, bias=bb), r=["c1", "cwc"], w=["c2"])
                a_ap, a_key = aTc(fc)
                S.add("dve", TT(a_ap, U_[:, :], c2[:, :], ALU.mult), r=[ukey, "c2"], w=[a_key])
            for tt in range(4):
                ti = 4 * g + tt
                b = ti % 2
                S.add("sp", DMA(xt[b][:], x1v[ti]), r=["x1d%d" % ti], w=["xt%d" % b], chan="c_xt%d" % b)
                zb = []
                for hf in range(2):
                    z = nextZ()
                    zb.append(z)
                    for fc in range(NFC):
                        a_ap, a_key = aTc(fc)
                        S.add("pe", MM(Z[z][:, :], a_ap[:, tt * 128:(tt + 1) * 128], wfo_sb[:, fc, hf * 512:(hf + 1) * 512],
                                       start=(fc == 0), stop=(fc == NFC - 1)), r=[a_key, "wbig"], w=["Z%d" % z])
                post_norm_residual(zb, gpost, "gpost", xt[b], "xt%d" % b, 2)
                S.add("sp", DMA(yv[ti], xt[b][:]), r=["xt%d" % b], chan="c_xo%d" % b)
        for j in range(2):
            S.add("sp", DMA(o_cp[j:j + 1, :].rearrange("o (c p) -> p (o c)", p=128), gpv_t[:, j, :], slow=True), r=["gpv"], chan="c_misc")

        print('n_ops', len(S.ops))
        S.emit(nc, limit=limit, sem_stack=esP)
        es.close()
        if with_sample and limit is None:
            nc.all_engine_barrier()
            build_sample(nc, esP, locals())
    return nc


_CACHE = {}


def _get_program(n_pool):
    if n_pool not in _CACHE:
        _CACHE[n_pool] = build_program(n_pool)
    return _CACHE[n_pool]


def kernel(x_prompt, x_sample, state_ret, cache_k, cache_v, state_conv, page_table,
           norm_mix_pre, norm_mix_post, w_in, w_o, lambda_q1, lambda_k1, lambda_q2, lambda_k2,
           subln_g, norm_ffn_pre, norm_ffn_post, w_ffn_in, conv_w, conv_b, w_ffn_out):
    f = lambda a: np.ascontiguousarray(np.asarray(a, dtype=np.float32))
    x_prompt, x_sample = f(x_prompt), f(x_sample)
    n_pool = int(np.asarray(cache_k).shape[1])
    nc = _get_program(n_pool)
    consts, shiftrow = make_consts()
    ck = f(cache_k)[0].reshape(n_pool * 128, 512)
    cv = f(cache_v)[0].reshape(n_pool * 128, 512)
    pt = np.ascontiguousarray(np.asarray(page_table, dtype=np.int32))
    shared = dict(
        w_in=f(w_in)[0], w_o=f(w_o)[0], w_fi=f(w_ffn_in)[0], w_fo=f(w_ffn_out)[0],
        n_mpre=f(norm_mix_pre), n_mpost=f(norm_mix_post), n_fpre=f(norm_ffn_pre), n_fpost=f(norm_ffn_post),
        lq1=f(lambda_q1), lk1=f(lambda_k1), lq2=f(lambda_q2), lk2=f(lambda_k2), subg=f(subln_g),
        convw=f(conv_w)[0], convb=f(conv_b), consts=consts, shiftrow=shiftrow, ck=ck, cv=cv, consts2=make_consts2(),
    )
    sret = f(state_ret)[0]
    sconv = f(state_conv)[0]
    in_maps = []
    for c in range(8):
        m = dict(shared)
        m["xp"] = x_prompt[c]
        m["xs"] = np.ascontiguousarray(x_sample[c * NS:(c + 1) * NS, 0, :])
        m["sret"] = np.ascontiguousarray(sret[c * NS:(c + 1) * NS])
        m["sconv"] = np.ascontiguousarray(sconv[c * NS:(c + 1) * NS])
        m["ptab"] = np.ascontiguousarray(pt[c * NS:(c + 1) * NS])
        in_maps.append(m)
    res = run_bass_kernel_spmd(nc, in_maps, core_ids=list(range(8)))
    R = res.results
    cat = lambda k: np.stack([np.asarray(r[k]) for r in R], axis=0)
    y_prompt = cat("y_p")
    y_sample = np.concatenate([np.asarray(r["y_s"]) for r in R], axis=0).reshape(128, 1, D)
    rsp = cat("o_rsp")[None]
    rss = np.concatenate([np.asarray(r["o_rss"]) for r in R], axis=0)[None]
    kp = cat("o_kp").reshape(1, 8, T, 4, 2, 64)
    vp = cat("o_vp").reshape(1, 8, T, 4, 128)
    ks = np.concatenate([np.asarray(r["o_ks"]) for r in R], axis=0).reshape(1, 128, 1, 4, 2, 64)
    vs = np.concatenate([np.asarray(r["o_vs"]) for r in R], axis=0).reshape(1, 128, 1, 4, 128)
    cp = cat("o_cp")[None]
    cs = np.concatenate([np.asarray(r["o_cs"]) for r in R], axis=0)[None]
    return (y_prompt, y_sample, rsp, rss, kp, vp, ks, vs, cp, cs)
```

```python
import math
import bisect
import contextlib
import numpy as np
import concourse.bass as bass
import concourse.mybir as mybir
from concourse.bass_utils import run_bass_kernel_spmd

F32 = mybir.dt.float32
BF16 = mybir.dt.bfloat16
I32 = mybir.dt.int32
AF = mybir.ActivationFunctionType
ALU = mybir.AluOpType
AX = mybir.AxisListType

D = 1024
T = 2048
NT = 16
GT = 512
NG = 4
DFF = 2816
NFC = 22
INW = 3584
EPS = 1e-6
NS = 16
H = 4
C_RQ, C_RK, C_RV, C_RG, C_DQ, C_DK, C_DV = 0, 512, 1024, 1536, 2048, 2560, 3072
GAM = [1.0 - 2.0 ** (-5.0 - h) for h in range(4)]
SLOPE = [2.0 ** (-2.0 * (h + 1)) for h in range(4)]
LAM_INIT = 0.8 - 0.6 * math.exp(-0.3 * 0)

K_ID = 0
K_DECAY = 128
K_QDEC = K_DECAY + 512
K_KDEC = K_QDEC + 512
K_CAUS = K_KDEC + 512
K_ABIAS = K_CAUS + 128
K_END = K_ABIAS + 64


def make_consts():
    c = np.zeros((128, K_END), np.float32)
    c[:, K_ID:K_ID + 128] = np.eye(128, dtype=np.float32)
    i = np.arange(128, dtype=np.float64)
    for h in range(4):
        g = GAM[h]
        diff = i[None, :] - i[:, None]
        dec = np.where(diff >= 0, g ** np.maximum(diff, 0), 0.0) * (128 ** -0.5)
        c[:, K_DECAY + 128 * h:K_DECAY + 128 * (h + 1)] = dec
        c[:, K_QDEC + 128 * h:K_QDEC + 128 * (h + 1)] = (g ** (i + 1.0))[None, :]
        c[:, K_KDEC + 128 * h:K_KDEC + 128 * (h + 1)] = ((128 ** -0.5) * g ** (127.0 - i))[:, None]
        for di in range(16):
            dd = di - 3
            c[:, K_ABIAS + 16 * h + di] = SLOPE[h] * (i - 128.0 * dd)
    c[:, K_CAUS:K_CAUS + 128] = (i[None, :] >= i[:, None]).astype(np.float32)
    sh = np.zeros((1, 4 * 512), np.float32)
    tl = np.arange(512, dtype=np.float64)
    for h in range(4):
        sh[0, 512 * h:512 * (h + 1)] = -8.0 * SLOPE[h] * tl
    return c, sh


K2_OH = 0
K2_OFFS = 256
K2_BIAS = 264
K2_Z = K2_BIAS + 128
K2_END = K2_Z + 32
NQ = 8


def make_consts2():
    c = np.zeros((128, K2_END), np.float32)
    for b in range(16):
        c[:, K2_OH + 16 * b + b] = 1.0
    p = np.arange(128)
    for q in range(NQ):
        c[:, K2_OFFS + q] = (p % 8) * 8 + q
        for t in range(2):
            s = (p // 8) * 128 + (p % 8) * 16 + 2 * q + t
            for hh in range(4):
                for m in range(2):
                    c[:, K2_BIAS + q * 16 + t * 8 + hh * 2 + m] = -8.0 * SLOPE[hh] * (2048.0 - s)
    return c


class Op:
    __slots__ = ("eng", "fn", "deps", "chan", "chan_idx", "sig", "seq", "idx", "hard")


class Sched:
    ENGS = ("pe", "act", "dve", "pool", "sp")

    def __init__(self, excl=None, tag=""):
        if excl is not None:
            self.EXCL = tuple(excl)
        self.tag = tag
        self.ops = []
        self.last_w = {}
        self.readers = {}
        self.chan_cnt = {}

    EXCL = ("Z0", "Z1", "SC0", "SC1", "AV", "DEN", "RT", "TP")

    def add(self, eng, fn, r=(), w=(), chan=None, hard=False):
        w = list(w) + [k for k in r if k in self.EXCL]
        r = [k for k in r if k not in self.EXCL]
        op = Op()
        op.eng, op.fn, op.chan, op.sig, op.seq = eng, fn, chan, False, 0
        op.hard = hard
        op.idx = len(self.ops)
        deps = set()
        for k in r:
            lw = self.last_w.get(k)
            if lw is not None:
                deps.add(lw)
        for k in w:
            lw = self.last_w.get(k)
            if lw is not None:
                deps.add(lw)
            for rd in self.readers.get(k, {}).values():
                deps.add(rd)
        rkey = ("c", chan) if chan is not None else eng
        for k in r:
            self.readers.setdefault(k, {})[rkey] = op
        for k in w:
            self.last_w[k] = op
            self.readers[k] = {}
        deps.discard(op)
        op.deps = deps
        if chan is not None:
            self.chan_cnt[chan] = self.chan_cnt.get(chan, 0) + 1
            op.chan_idx = self.chan_cnt[chan]
        else:
            op.chan_idx = 0
        self.ops.append(op)
        return op

    def emit(self, nc, final_eng="sp", limit=None, sem_stack=None):
        ops = self.ops
        if limit is not None:
            ops = ops[:limit]
            self.chan_cnt = {}
            for x in ops:
                if x.chan is not None:
                    self.chan_cnt[x.chan] = self.chan_cnt.get(x.chan, 0) + 1
        def needs_sem(x, d):
            if d.chan is not None:
                return True
            if d.eng == x.eng and x.chan is None and not x.hard:
                return False
            return True
        for x in ops:
            for d in x.deps:
                if needs_sem(x, d) and d.chan is None:
                    d.sig = True
        cnt = {e: 0 for e in self.ENGS}
        for x in ops:
            if x.chan is None and x.sig:
                cnt[x.eng] += 1
                x.seq = cnt[x.eng]
        chans = sorted(self.chan_cnt.keys())
        with contextlib.ExitStack() as es:
            ss = sem_stack if sem_stack is not None else es
            esem = {e: ss.enter_context(nc.semaphore(self.tag + "sem_" + e)) for e in self.ENGS}
            csem = {c: ss.enter_context(nc.semaphore(self.tag + "ch_" + str(c))) for c in chans}
            block = es.enter_context(nc.Block())
            per_eng = {e: [x for x in ops if x.eng == e] for e in self.ENGS}
            chan_idx_list = {c: [x.idx for x in ops if x.chan == c] for c in chans}

            def body(e, eng):
                waited = {}
                for x in per_eng[e]:
                    need = {}
                    for d in x.deps:
                        if not needs_sem(x, d):
                            continue
                        if d.chan is not None:
                            key, val = ("c", d.chan), 16 * bisect.bisect_left(chan_idx_list[d.chan], x.idx)
                        else:
                            key, val = ("e", d.eng), d.seq
                        if val > need.get(key, 0):
                            need[key] = val
                    for key, val in need.items():
                        if waited.get(key, 0) >= val:
                            continue
                        waited[key] = val
                        sem = csem[key[1]] if key[0] == "c" else esem[key[1]]
                        eng.wait_ge(sem, val)
                    inst = x.fn(eng)
                    if x.chan is not None:
                        inst.then_inc(csem[x.chan], 16)
                    elif x.sig:
                        inst.then_inc(esem[x.eng], 1)
                if e == final_eng:
                    for c in chans:
                        eng.wait_ge(csem[c], 16 * self.chan_cnt[c])

            @block.tensor
            def _(eng):
                body("pe", eng)

            @block.scalar
            def _(eng):
                body("act", eng)

            @block.vector
            def _(eng):
                body("dve", eng)

            @block.gpsimd
            def _(eng):
                body("pool", eng)

            @block.sync
            def _(eng):
                body("sp", eng)


def MM(out, lhsT, rhs, start=True, stop=True):
    return lambda e: e.matmul(out=out, lhsT=lhsT, rhs=rhs, start=start, stop=stop)


def TR(out, in_, identity):
    return lambda e: e.transpose(out=out, in_=in_, identity=identity)


def ACT(out, in_, func, **kw):
    return lambda e: e.activation(out=out, in_=in_, func=func, **kw)


def TT(out, in0, in1, op):
    return lambda e: e.tensor_tensor(out=out, in0=in0, in1=in1, op=op)


def STT(out, in0, scalar, in1, op0, op1):
    return lambda e: e.scalar_tensor_tensor(out=out, in0=in0, scalar=scalar, in1=in1, op0=op0, op1=op1)


def TS(out, in0, s1, s2, op0, op1=None):
    if op1 is None:
        return lambda e: e.tensor_scalar(out=out, in0=in0, scalar1=s1, scalar2=None, op0=op0)
    return lambda e: e.tensor_scalar(out=out, in0=in0, scalar1=s1, scalar2=s2, op0=op0, op1=op1)


def CP(out, in_):
    return lambda e: e.tensor_copy(out=out, in_=in_)


def RCP(out, in_):
    return lambda e: e.reciprocal(out=out, in_=in_)


def MSET(ap, v):
    return lambda e: e.memset(ap, v)


def RED(out, in_, op=None):
    return lambda e: e.tensor_reduce(out=out, in_=in_, axis=AX.X, op=(op or ALU.add))


def DMA(out, in_, slow=False):
    if slow:
        return lambda e: e.dma_start(out=out, in_=in_, allow_slow_non_contiguous=True)
    return lambda e: e.dma_start(out=out, in_=in_)


def IDMA(out, in_, idx_ap):
    return lambda e: e.indirect_dma_start(out=out, out_offset=None, in_=in_,
                                          in_offset=bass.IndirectOffsetOnAxis(ap=idx_ap, axis=0))


def build_sample(nc, esP, L):
    cst, ident_bf, ones_f, wbig, wo_sb, lamw, subgc, scol = (L[k] for k in
                                                              ("cst", "ident_bf", "ones_f", "wbig", "wo_sb", "lamw", "subgc", "scol"))
    xs, sret, ck, cv, sconv, ptab = (L[k] for k in ("xs", "sret", "ck", "cv", "sconv", "ptab"))
    w_in, w_fi = L["w_in"], L["w_fi"]
    n_mpre, n_mpost, n_fpre, n_fpost, subg, convw, convb = (L[k] for k in
                                                            ("n_mpre", "n_mpost", "n_fpre", "n_fpost", "subg", "convw", "convb"))
    y_s, o_rss, o_ks, o_vs, o_cs = (L[k] for k in ("y_s", "o_rss", "o_ks", "o_vs", "o_cs"))
    consts2_d = L["consts2_d"]
    neglam = lamw[0:NS, 2:3]
    epsc = scol[0:NS, 7:8]
    wfo_sb = wbig.rearrange("p a b -> p (a b)")[:, 0:NFC * D].rearrange("p (c d) -> p c d", d=D)

    banks = ("TPs", "Zs0", "Zs1", "QB", "SN", "Rps", "Oacc", "MISC")
    S = Sched(excl=banks, tag="s2_")
    es = contextlib.ExitStack()
    with es:
        def sb(name, shape, dtp=F32):
            return es.enter_context(nc.sbuf_tensor("s2_" + name, list(shape), dtp))

        def ps(name, shape, dtp=F32):
            return es.enter_context(nc.psum_tensor("s2_" + name, list(shape), dtp))

        cst2 = sb("cst2", [128, K2_END])
        xs_t = sb("xs_t", [NS, D])
        gA = sb("gA", [NS, D])
        gB = sb("gB", [NS, D])
        hs_bf = sb("hs_bf", [NS, D], BF16)
        hsT = sb("hsT", [128, 8, NS], BF16)
        wblk = sb("wblk", [128, 8, 512], BF16)
        zs = [sb("zs%d" % i, [NS, 512]) for i in range(7)]
        rq, rk, rv, rg, dq, dk, dv = zs
        ptsel = sb("ptsel", [128, NS], I32)
        idx = sb("idx", [128, NS, NQ], I32)
        Kq = [sb("Kq%d" % i, [128, 2, 512]) for i in range(2)]
        Vq = [sb("Vq%d" % i, [128, 2, 512]) for i in range(2)]
        prod = [sb("prod%d" % i, [128, 2, 512]) for i in range(2)]
        Vb = [sb("Vb%d" % i, [128, 2, 512], BF16) for i in range(2)]
        qb = [sb("qb%d" % i, [128, 512]) for i in range(2)]
        S4 = [sb("S4_%d" % i, [128, 16]) for i in range(2)]
        S4b = [sb("S4b_%d" % i, [128, 16]) for i in range(2)]
        Pc = [sb("Pc%d" % i, [128, 2, 8]) for i in range(2)]
        Pacc = sb("Pacc", [128, NS, 8])

        Pz = [sb("Pz%d" % i, [128, 2, 4, 32], BF16) for i in range(2)]
        St = [sb("St%d" % i, [128, 4, 128]) for i in range(2)]
        Sn = [sb("Sn%d" % i, [128, 4, 128]) for i in range(2)]
        Qz = sb("Qz", [128, 4, NS, NS])
        rqTs = sb("rqTs", [128, 4, NS])
        Km = [sb("Km%d" % i, [NS, 512]) for i in range(2)]
        Kr = [sb("Kr%d" % i, [NS, 512]) for i in range(2)]
        t16 = [sb("t16_%d" % i, [32 if i == 0 else NS, 512]) for i in range(7)]
        sm = sb("sm", [NS, 64])
        ms_bf = sb("ms_bf", [NS, D], BF16)
        a2_bf = sb("a2_bf", [NS, DFF], BF16)
        a2T = sb("a2T", [128, NFC, NS], BF16)
        arena = sb("arena", [128, 1024])
        Pq = arena[:, :].rearrange("p (q b g) -> p q b g", q=NQ, b=NS)
        cwb = arena[0:NS, :].rearrange("p (j c) -> p j c", j=4)
        scb = sb("scb", [NS, 2, 256])
        cc = [sb("cc%d" % i, [NS, 256]) for i in range(3)]
        gsub_bc = sb("gsub_bc", [NS, 4, 128])
        tmpS = t16[2]

        TPs = ps("TPs", [128, 8, 128], BF16)
        Zs = [ps("Zs%d" % i, [128, 512]) for i in range(2)]
        QB = ps("QB", [128, 512])
        SN = ps("SN", [128, 512])
        Rps = ps("Rps", [128, 512])
        Oacc = ps("Oacc", [128, 512])
        MISC = ps("MISC", [128, 512])

        A = S.add
        A("sp", DMA(cst2[:], consts2_d), w=["cst2"], chan="k_c")
        A("sp", DMA(xs_t[:], xs), w=["xs_t"], chan="k_c")
        A("sp", DMA(gA[:], n_mpre.to_broadcast([NS, D])), w=["gA"], chan="k_c")
        A("sp", DMA(gB[:], n_mpost.to_broadcast([NS, D])), w=["gB"], chan="k_c")
        for h in range(4):
            A("sp", DMA(gsub_bc[:, h, :], subg.to_broadcast([NS, 128])), w=["gsub_bc"], chan="k_c")
        for j in range(16):
            A("sp", DMA(ptsel[8 * j:8 * j + 8, :], ptab[:, j:j + 1].rearrange("b o -> o b").to_broadcast([8, NS]), slow=True),
              w=["ptsel"], chan="k_c")
        A("sp", DMA(o_cs[:, 0, :], sconv[:, 1, :]), chan="k_out")
        A("dve", TS(gsub_bc[:], gsub_bc[:], 1.0 - LAM_INIT, None, ALU.mult), r=["gsub_bc"], w=["gsub_bc"])
        for q in range(NQ):
            A("dve", TS(idx[:, :, q], ptsel[:, :], 64.0, cst2[:, K2_OFFS + q:K2_OFFS + q + 1], ALU.mult, ALU.add),
              r=["ptsel", "cst2"], w=["idx"])

        if L.get("dbg") == "idx":
            A("sp", DMA(y_s.rearrange("b (x c) -> (b x) c", x=8), idx[:].rearrange("p b q -> p (b q)").bitcast(F32)), r=["idx"], chan="k_out")
            S.emit(nc, sem_stack=esP)
            return

        def smc(c, n=1):
            return sm[:, c:c + n]

        def rms16(src, src_key, g_tile, g_key, out_ap, out_key, col):
            a = smc(col)
            A("act", ACT(tmpS[:, :], src[:, 0:512], AF.Square, accum_out=smc(col)), r=[src_key], w=["t2", "sm%d" % col])
            A("act", ACT(tmpS[:, :], src[:, 512:1024], AF.Square, accum_out=smc(col + 1)), r=[src_key], w=["t2", "sm%d" % (col + 1)])
            A("dve", TT(a, a, smc(col + 1), ALU.add), r=["sm%d" % col, "sm%d" % (col + 1)], w=["sm%d" % col])
            A("act", ACT(a, a, AF.Sqrt, scale=1.0 / D, bias=epsc), r=["sm%d" % col], w=["sm%d" % col])
            A("dve", RCP(a, a), r=["sm%d" % col], w=["sm%d" % col])
            for hf in range(2):
                hs = slice(hf * 512, (hf + 1) * 512)
                A("act", ACT(tmpS[:, :], src[:, hs], AF.Copy, scale=a), r=[src_key, "sm%d" % col], w=["t2"])
                A("dve", TT(out_ap[:, hs], tmpS[:, :], g_tile[:, hs], ALU.mult), r=["t2", g_key], w=[out_key])

        def transpose16(src_bf, src_key, dstT, dst_key, nchunk):
            for c0 in range(0, nchunk, 8):
                n = min(8, nchunk - c0)
                for c in range(n):
                    A("pe", TR(TPs[:, c, 0:NS], src_bf[:, (c0 + c) * 128:(c0 + c + 1) * 128], ident_bf[0:NS, 0:NS]),
                      r=[src_key, "ident_bf"], w=["TPs"])
                A("act", ACT(dstT[:, c0:c0 + n, :], TPs[:, 0:n, 0:NS], AF.Copy), r=["TPs"], w=[dst_key])

        def postnorm16(zb, g_tile, g_key, res, res_key, col):
            a = smc(col)
            A("act", ACT(tmpS[:, :], Zs[zb[0]][0:NS, :], AF.Square, accum_out=smc(col)), r=["Zs%d" % zb[0]], w=["t2", "sm%d" % col])
            A("act", ACT(tmpS[:, :], Zs[zb[1]][0:NS, :], AF.Square, accum_out=smc(col + 1)), r=["Zs%d" % zb[1]], w=["t2", "sm%d" % (col + 1)])
            A("dve", TT(a, a, smc(col + 1), ALU.add), r=["sm%d" % col, "sm%d" % (col + 1)], w=["sm%d" % col])
            A("act", ACT(a, a, AF.Sqrt, scale=1.0 / D, bias=epsc), r=["sm%d" % col], w=["sm%d" % col])
            A("dve", RCP(a, a), r=["sm%d" % col], w=["sm%d" % col])
            for hf in range(2):
                hs = slice(hf * 512, (hf + 1) * 512)
                A("act", ACT(tmpS[:, :], Zs[zb[hf]][0:NS, :], AF.Copy, scale=a), r=["Zs%d" % zb[hf], "sm%d" % col], w=["t2"])
                A("pool", TT(tmpS[:, :], tmpS[:, :], g_tile[:, hs], ALU.mult), r=["t2", g_key], w=["t2"])
                A("dve", TT(res[:, hs], res[:, hs], tmpS[:, :], ALU.add), r=["t2", res_key], w=[res_key])

        rms16(xs_t, "xs_t", gA, "gA", hs_bf, "hs_bf", 0)
        transpose16(hs_bf, "hs_bf", hsT, "hsT", 8)
        winv = w_in.rearrange("(k p) c -> p k c", p=128)
        for blk in range(7):
            A("pool", DMA(wblk[:, :, :], winv[:, :, blk * 512:(blk + 1) * 512]), w=["wblk"], chan="k_w")
            z = blk % 2
            for k in range(8):
                A("pe", MM(Zs[z][0:NS, :], hsT[:, k, :], wblk[:, k, :], start=(k == 0), stop=(k == 7)), r=["hsT", "wblk"], w=["Zs%d" % z])
            if blk == 3:
                A("act", ACT(zs[blk][:, :], Zs[z][0:NS, :], AF.Silu), r=["Zs%d" % z], w=["zs%d" % blk])
            else:
                A("act", ACT(zs[blk][:, :], Zs[z][0:NS, :], AF.Copy), r=["Zs%d" % z], w=["zs%d" % blk])
        A("sp", DMA(o_ks, dk[:, :]), r=["zs5"], chan="k_out")
        A("sp", DMA(o_vs, dv[:, :]), r=["zs6"], chan="k_out")

        A("act", ACT(rk[:, :], rk[:, :], AF.Copy, scale=128 ** -0.5), r=["zs1"], w=["zs1"])
        A("pool", TT(t16[1][:, :], rq[:, :], rk[:, :], ALU.mult), r=["zs0", "zs1"], w=["t1"])
        A("dve", RED(smc(8, 4), t16[1][:, :].rearrange("p (h d) -> p h d", d=128)), r=["t1"], w=["qk"])
        for h in range(4):
            A("pe", TR(MISC[:, h * NS:(h + 1) * NS], rq[:, h * 128:(h + 1) * 128], cst[0:NS, K_ID:K_ID + NS]), r=["zs0", "cst"], w=["MISC"])
        A("act", ACT(rqTs[:, :, :], MISC[:, 0:4 * NS].rearrange("p (h b) -> p h b", b=NS), AF.Copy), r=["MISC"], w=["rqTs"])
        for h in range(4):
            for b in range(NS):
                A("dve", TS(Qz[:, h, b, :], cst2[:, K2_OH + 16 * b:K2_OH + 16 * b + 16], rqTs[:, h, b:b + 1], None, ALU.mult),
                  r=["cst2", "rqTs"], w=["Qz"])
        A("pe", MM(Rps[0:NS, :], cst2[:, K2_Z:K2_Z + NS], cst[:, 0:512], start=True, stop=False), r=["cst2", "cst"], w=["Rps"])
        srv = sret.rearrange("b h d e -> b d h e")
        orv = o_rss.rearrange("b h d e -> b d h e")
        def ret_step(b):
            sbuf = b % 2
            A("sp", DMA(St[sbuf][:, :, :], srv[b]), w=["St%d" % sbuf], chan="k_st%d" % sbuf)
            for h in range(4):
                A("pe", MM(Rps[0:NS, h * 128:(h + 1) * 128], Qz[:, h, b, :], St[sbuf][:, h, :], start=False, stop=(b == NS - 1)),
                  r=["Qz", "St%d" % sbuf], w=["Rps"])
            A("act", ACT(Kr[sbuf][:, :], rk[:, :], AF.Copy, scale=cst[0:NS, K_ID + b:K_ID + b + 1]), r=["zs1", "cst"], w=["Kr%d" % sbuf])
            for h in range(4):
                hs = slice(h * 128, (h + 1) * 128)
                A("pe", MM(SN[:, hs], Kr[sbuf][:, hs], rv[:, hs], start=True, stop=True), r=["Kr%d" % sbuf, "zs2"], w=["SN"])
            for h in range(4):
                hs = slice(h * 128, (h + 1) * 128)
                A("dve", STT(Sn[sbuf][:, h, :], St[sbuf][:, h, :], GAM[h], SN[:, hs], ALU.mult, ALU.add),
                  r=["St%d" % sbuf, "SN"], w=["Sn%d" % sbuf])
            A("sp", DMA(orv[b], Sn[sbuf][:, :, :]), r=["Sn%d" % sbuf], chan="k_sn%d" % sbuf)
        ck2 = ck.rearrange("(r t) f -> r (t f)", t=2)
        cv2 = cv.rearrange("(r t) f -> r (t f)", t=2)
        A("pe", MM(Oacc[0:32, :], cst2[:, K2_Z:K2_Z + 32], cst[:, 0:512], start=True, stop=False), r=["cst2", "cst"], w=["Oacc"])

        first = False
        it = 0
        gl = [(b_, q_) for b_ in range(NS) for q_ in range(NQ)]

        def issue_gather(i):
            b_, q_ = gl[i]
            kb_ = i % 2
            A("pool", IDMA(Kq[kb_][:, :, :].rearrange("p t f -> p (t f)"), ck2, idx[:, b_, q_:q_ + 1]), r=["idx"], w=["Kq%d" % kb_], chan="k_kq%d" % kb_)
            A("pool", IDMA(Vq[kb_][:, :, :].rearrange("p t f -> p (t f)"), cv2, idx[:, b_, q_:q_ + 1]), r=["idx"], w=["Vq%d" % kb_], chan="k_vq%d" % kb_)
        issue_gather(0)
        for b in range(NS):
            qbuf = b % 2
            ret_step(b)
            A("act", ACT(Km[qbuf][:, :], dq[:, :], AF.Copy, scale=cst[0:NS, K_ID + b:K_ID + b + 1]), r=["zs4", "cst"], w=["Km%d" % qbuf])
            A("pe", MM(QB[:, :], ones_f[0:NS, :], Km[qbuf][:, :], start=True, stop=True), r=["ones_f", "Km%d" % qbuf], w=["QB"])
            A("act", ACT(qb[qbuf][:, :], QB[:, :], AF.Copy), r=["QB"], w=["qb%d" % qbuf])
            for q in range(NQ):
                kb = it % 2
                it += 1
                if it < len(gl):
                    issue_gather(it)
                A("dve", TT(prod[kb][:, 0, :], Kq[kb][:, 0, :], qb[qbuf][:, :], ALU.mult), r=["Kq%d" % kb, "qb%d" % qbuf], w=["prod%d" % kb])
                A("dve", TT(prod[kb][:, 1, :], Kq[kb][:, 1, :], qb[qbuf][:, :], ALU.mult), r=["Kq%d" % kb, "qb%d" % qbuf], w=["prod%d" % kb])
                A("dve", RED(S4[kb][:, :], prod[kb][:, :, :].rearrange("p t (g d) -> p (t g) d", d=64)), r=["prod%d" % kb], w=["S4_%d" % kb])
                A("pool", TT(S4b[kb][:, :], S4[kb][:, :], cst2[:, K2_BIAS + 16 * q:K2_BIAS + 16 * q + 16], ALU.add), r=["S4_%d" % kb, "cst2"], w=["S4b_%d" % kb])
                A("act", ACT(Pc[kb][:, :, :], S4b[kb][:, :].rearrange("p (t g) -> p t g", g=8), AF.Exp, scale=0.125), r=["S4b_%d" % kb], w=["Pc%d" % kb])
                if q < 2:
                    A("pool", MSET(Pz[kb][:], 0.0), w=["Pz%d" % kb])
                A("dve", CP(Pz[kb][:].rearrange("p t h (m s) -> p t h m s", s=16)[:, :, :, :, b],
                            Pc[kb][:, :, :].rearrange("p t (h m) -> p t h m", m=2)), r=["Pc%d" % kb], w=["Pz%d" % kb])
                A("dve", TT(Pq[:, q, b, :], Pc[kb][:, 0, :], Pc[kb][:, 1, :], ALU.add), r=["Pc%d" % kb], w=["Pq"])
                A("act", ACT(Vb[kb][:, :, :], Vq[kb][:, :, :], AF.Copy), r=["Vq%d" % kb], w=["Vb%d" % kb])
                last = (b == NS - 1 and q == NQ - 1)
                for t in range(2):
                    for h in range(4):
                        hs = slice(h * 128, (h + 1) * 128)
                        A("pe", MM(Oacc[0:32, hs], Pz[kb][:, t, h, :], Vb[kb][:, t, hs], start=(first and t == 0), stop=(last and t == 1)),
                          r=["Pz%d" % kb, "Vb%d" % kb], w=["Oacc"])
                first = False
        if L.get("dbg") == "pacc":
            A("sp", DMA(y_s.rearrange("b (x c) -> (b x) c", x=8), Pacc[:].rearrange("p b g -> p (b g)")), r=["Pacc"], chan="k_out")
            S.emit(nc, sem_stack=esP)
            return
        o_ret = t16[1]
        for h in range(4):
            hs = slice(h * 128, (h + 1) * 128)
            A("act", ACT(t16[2][:, hs], rv[:, hs], AF.Copy, scale=smc(8 + h)), r=["zs2", "qk"], w=["t2"])
        for h in range(4):
            hs = slice(h * 128, (h + 1) * 128)
            A("dve", STT(o_ret[:, hs], Rps[0:NS, hs], GAM[h], t16[2][:, hs], ALU.mult, ALU.add), r=["Rps", "t2"], w=["t1"])
        A("pool", TT(t16[2][:, :], o_ret[:, :], o_ret[:, :], ALU.mult), r=["t1"], w=["t2"])
        A("dve", RED(smc(12, 4), t16[2][:, :].rearrange("p (h d) -> p h d", d=128)), r=["t2"], w=["ss1"])
        A("act", ACT(smc(12, 4), smc(12, 4), AF.Sqrt, scale=1.0 / 128, bias=epsc), r=["ss1"], w=["ss1"])
        A("dve", RCP(smc(12, 4), smc(12, 4)), r=["ss1"], w=["ss1"])
        for h in range(4):
            hs = slice(h * 128, (h + 1) * 128)
            A("act", ACT(t16[2][:, hs], o_ret[:, hs], AF.Copy, scale=smc(12 + h)), r=["t1", "ss1"], w=["t2"])
        A("dve", TT(ms_bf[:, 0:512], t16[2][:, :], rg[:, :], ALU.mult), r=["t2", "zs3"], w=["ms_bf"])

        O_sb = t16[0]
        A("act", ACT(O_sb[:, :], Oacc[0:32, :], AF.Copy), r=["Oacc"], w=["t0"])
        A("pe", MM(Zs[0][0:NS, :], cst[0:32, K_ID + 16:K_ID + 32], O_sb[:, :], start=True, stop=True), r=["cst", "t0"], w=["Zs0"])
        A("pool", TT(Pacc[:, :, :], Pq[:, 0, :, :], Pq[:, 1, :, :], ALU.add), r=["Pq"], w=["Pacc"])
        for q_ in range(2, NQ):
            A("pool", TT(Pacc[:, :, :], Pacc[:, :, :], Pq[:, q_, :, :], ALU.add), r=["Pq", "Pacc"], w=["Pacc"])
        for b in range(NS):
            A("pe", MM(MISC[0:NS, 64:72], cst2[:, K2_OH + 16 * b:K2_OH + 16 * b + 16], Pacc[:, b, :], start=(b == 0), stop=(b == NS - 1)),
              r=["cst2", "Pacc"], w=["MISC"])
        A("pool", TT(t16[3][:, :], dq[:, :], dk[:, :], ALU.mult), r=["zs4", "zs5"], w=["t3"])
        A("dve", RED(smc(16, 8), t16[3][:, :].rearrange("p (g d) -> p g d", d=64)), r=["t3"], w=["pnew"])
        A("act", ACT(smc(16, 8), smc(16, 8), AF.Exp, scale=0.125), r=["pnew"], w=["pnew"])
        A("act", ACT(smc(24, 8), MISC[0:NS, 64:72], AF.Copy), r=["MISC"], w=["den"])
        A("pool", TT(smc(24, 8), smc(24, 8), smc(16, 8), ALU.add), r=["den", "pnew"], w=["den"])
        A("dve", RCP(smc(24, 8), smc(24, 8)), r=["den"], w=["den"])
        Ot = [t16[3], t16[4]]
        for h in range(4):
            hs = slice(h * 128, (h + 1) * 128)
            A("dve", STT(Ot[0][:, hs], dv[:, hs], smc(16 + 2 * h), O_sb[0:NS, hs], ALU.mult, ALU.add), r=["zs6", "pnew", "t0"], w=["t3"])
            A("dve", STT(Ot[1][:, hs], dv[:, hs], smc(17 + 2 * h), Zs[0][0:NS, hs], ALU.mult, ALU.add), r=["zs6", "pnew", "Zs0"], w=["t4"])
        on = [t16[5], t16[6]]
        for h in range(4):
            hs = slice(h * 128, (h + 1) * 128)
            A("act", ACT(on[0][:, hs], Ot[0][:, hs], AF.Copy, scale=smc(24 + 2 * h)), r=["t3", "den"], w=["t5"])
            A("act", ACT(on[1][:, hs], Ot[1][:, hs], AF.Copy, scale=smc(25 + 2 * h)), r=["t4", "den"], w=["t6"])
        do = t16[1]
        A("dve", STT(do[:, :], on[1][:, :], neglam, on[0][:, :], ALU.mult, ALU.add), r=["t5", "t6"], w=["t1"])
        if L.get("dbg") == "do":
            A("sp", DMA(y_s[:, 0:512], do[:, :]), r=["t1"], chan="k_out")
            A("sp", DMA(y_s[:, 512:520], smc(24, 8)), r=["den"], chan="k_out")
            A("sp", DMA(y_s[:, 520:528], smc(16, 8)), r=["pnew"], chan="k_out")
            S.emit(nc, sem_stack=esP)
            return
        A("pool", TT(t16[3][:, :], do[:, :], do[:, :], ALU.mult), r=["t1"], w=["t3"])
        A("dve", RED(smc(32, 4), t16[3][:, :].rearrange("p (h d) -> p h d", d=128)), r=["t3"], w=["ss2"])
        A("act", ACT(smc(32, 4), smc(32, 4), AF.Sqrt, scale=1.0 / 128, bias=epsc), r=["ss2"], w=["ss2"])
        A("dve", RCP(smc(32, 4), smc(32, 4)), r=["ss2"], w=["ss2"])
        for h in range(4):
            hs = slice(h * 128, (h + 1) * 128)
            A("act", ACT(t16[3][:, hs], do[:, hs], AF.Copy, scale=smc(32 + h)), r=["t1", "ss2"], w=["t3"])
        A("dve", TT(ms_bf[:, 512:1024], t16[3][:, :], gsub_bc[:].rearrange("p h e -> p (h e)"), ALU.mult), r=["t3", "gsub_bc"], w=["ms_bf"])

        if L.get("dbg") == "ms":
            A("pool", DMA(y_s, ms_bf[:, :]), r=["ms_bf"], chan="k_out")
            S.emit(nc, sem_stack=esP)
            return
        msT = hsT
        transpose16(ms_bf, "ms_bf", msT, "hsT", 8)
        for hf in range(2):
            for c in range(8):
                A("pe", MM(Zs[hf][0:NS, :], msT[:, c, :], wo_sb[:, c, hf * 512:(hf + 1) * 512], start=(c == 0), stop=(c == 7)),
                  r=["hsT"], w=["Zs%d" % hf])
        postnorm16([0, 1], gB, "gB", xs_t, "xs_t", 36)

        A("sp", DMA(gA[:], n_fpre.to_broadcast([NS, D])), w=["gA"], chan="k_g2")
        A("sp", DMA(gB[:], n_fpost.to_broadcast([NS, D])), w=["gB"], chan="k_g2")
        rms16(xs_t, "xs_t", gA, "gA", hs_bf, "hs_bf", 38)
        transpose16(hs_bf, "hs_bf", hsT, "hsT", 8)
        wfiv = w_fi.rearrange("(k p) c -> p k c", p=128)
        for i in range(11):
            cs_ = slice(i * 256, (i + 1) * 256)
            A("pool", DMA(wblk[:, :, 0:256], wfiv[:, :, i * 256:(i + 1) * 256]), w=["wblk"], chan="k_w")
            A("pool", DMA(wblk[:, :, 256:512], wfiv[:, :, DFF + i * 256:DFF + (i + 1) * 256]), w=["wblk"], chan="k_w")
            for j in range(3):
                A("sp", DMA(cwb[:, j, :], convw[j:j + 1, cs_].to_broadcast([NS, 256])), w=["Pq"], chan="k_cw")
            A("sp", DMA(cwb[:, 3, :], convb[0:1, cs_].to_broadcast([NS, 256])), w=["Pq"], chan="k_cw")
            A("sp", DMA(scb[:, :, :], sconv[:, :, cs_]), w=["scb"], chan="k_cw")
            for k in range(8):
                A("pe", MM(Zs[0][0:NS, 0:256], hsT[:, k, :], wblk[:, k, 0:256], start=(k == 0), stop=(k == 7)), r=["hsT", "wblk"], w=["Zs0"])
            for k in range(8):
                A("pe", MM(Zs[1][0:NS, 0:256], hsT[:, k, :], wblk[:, k, 256:512], start=(k == 0), stop=(k == 7)), r=["hsT", "wblk"], w=["Zs1"])
            gS = cc[0]
            A("act", ACT(gS[:, :], Zs[0][0:NS, 0:256], AF.Copy), r=["Zs0"], w=["cc0"])
            A("sp", DMA(o_cs[:, 1, cs_], gS[:, :]), r=["cc0"], chan="k_gs")
            A("pool", TT(cc[1][:, :], gS[:, :], cwb[:, 2, :], ALU.mult), r=["cc0", "Pq"], w=["cc1"])
            A("pool", TT(cc[1][:, :], cc[1][:, :], cwb[:, 3, :], ALU.add), r=["cc1", "Pq"], w=["cc1"])
            A("pool", TT(cc[2][:, :], scb[:, 0, :], cwb[:, 0, :], ALU.mult), r=["scb", "Pq"], w=["cc2"])
            A("pool", TT(cc[1][:, :], cc[1][:, :], cc[2][:, :], ALU.add), r=["cc1", "cc2"], w=["cc1"])
            A("pool", TT(cc[2][:, :], scb[:, 1, :], cwb[:, 1, :], ALU.mult), r=["scb", "Pq"], w=["cc2"])
            A("pool", TT(cc[1][:, :], cc[1][:, :], cc[2][:, :], ALU.add), r=["cc1", "cc2"], w=["cc1"])
            A("act", ACT(cc[2][:, :], cc[1][:, :], AF.Gelu_apprx_tanh), r=["cc1"], w=["cc2"])
            A("dve", TT(a2_bf[:, cs_], Zs[1][0:NS, 0:256], cc[2][:, :], ALU.mult), r=["Zs1", "cc2"], w=["a2_bf"])
        transpose16(a2_bf, "a2_bf", a2T, "a2T", NFC)
        for hf in range(2):
            for fc in range(NFC):
                A("pe", MM(Zs[hf][0:NS, :], a2T[:, fc, :], wfo_sb[:, fc, hf * 512:(hf + 1) * 512], start=(fc == 0), stop=(fc == NFC - 1)),
                  r=["a2T"], w=["Zs%d" % hf])
        postnorm16([0, 1], gB, "gB", xs_t, "xs_t", 40)
        A("sp", DMA(y_s, xs_t[:, :]), r=["xs_t"], chan="k_out")
        print("n_ops sample", len(S.ops))
        S.emit(nc, sem_stack=esP)

def build_program(n_pool, with_sample=True, limit=None, dbg=None):
    nc = bass.Bass("TRN2", target_bir_lowering=False)
    S = Sched()
    dt = nc.dram_tensor

    def inp(name, shape, dtp=F32):
        return dt(name, list(shape), dtp, kind="ExternalInput").ap()

    def outp(name, shape):
        return dt(name, list(shape), F32, kind="ExternalOutput").ap()

    xp = inp("xp", [T, D])
    w_in = inp("w_in", [D, INW])
    w_o = inp("w_o", [D, D])
    w_fi = inp("w_fi", [D, 2 * DFF])
    w_fo = inp("w_fo", [DFF, D])
    n_mpre = inp("n_mpre", [1, D])
    n_mpost = inp("n_mpost", [1, D])
    n_fpre = inp("n_fpre", [1, D])
    n_fpost = inp("n_fpost", [1, D])
    lq1 = inp("lq1", [1, 64])
    lk1 = inp("lk1", [1, 64])
    lq2 = inp("lq2", [1, 64])
    lk2 = inp("lk2", [1, 64])
    subg = inp("subg", [1, 128])
    convw = inp("convw", [3, DFF])
    convb = inp("convb", [1, DFF])
    consts_d = inp("consts", [128, K_END])
    shift_d = inp("shiftrow", [1, 2048])
    consts2_d = inp("consts2", [128, K2_END])
    xs = inp("xs", [NS, D])
    sret = inp("sret", [NS, 4, 128, 128])
    ck = inp("ck", [n_pool * 128, 512])
    cv = inp("cv", [n_pool * 128, 512])
    sconv = inp("sconv", [NS, 2, DFF])
    ptab = inp("ptab", [NS, 16], I32)

    y_p = outp("y_p", [T, D])
    y_s = outp("y_s", [NS, D])
    o_rsp = outp("o_rsp", [4, 128, 128])
    o_rss = outp("o_rss", [NS, 4, 128, 128])
    o_kp = outp("o_kp", [T, 512])
    o_vp = outp("o_vp", [T, 512])
    o_ks = outp("o_ks", [NS, 512])
    o_vs = outp("o_vs", [NS, 512])
    o_cp = outp("o_cp", [2, DFF])
    o_cs = outp("o_cs", [NS, 2, DFF])
    x1_d = dt("x1_scratch", [T, D], F32, kind="Internal").ap()

    esP = contextlib.ExitStack()
    es = contextlib.ExitStack()
    with esP, es:
        def sbP(name, shape, dtp=F32):
            return esP.enter_context(nc.sbuf_tensor(name, list(shape), dtp))

        def sb(name, shape, dtp=F32):
            return es.enter_context(nc.sbuf_tensor(name, list(shape), dtp))

        def ps(name, shape, dtp=F32):
            return es.enter_context(nc.psum_tensor(name, list(shape), dtp))

        cst = sbP("cst", [128, K_END])
        ident_bf = sbP("ident_bf", [128, 128], BF16)
        ones_bf = sbP("ones_bf", [128, 128], BF16)
        ones_f = sbP("ones_f", [128, 128])
        lamw = sbP("lamw", [128, 8])
        subgc = sbP("subgc", [128, 2])
        cwc = sbP("cwc", [128, NFC, 4])
        wbig = sbP("wbig", [128, 8, INW], BF16)
        wo_sb = sbP("wo_sb", [128, 8, D], BF16)
        scol = sbP("scol", [128, 8])
        shift_bf = sb("shift_bf", [1, 2048], BF16)
        gpre = sb("gpre", [128, D])
        gpost = sb("gpost", [128, D])
        lamt = sb("lamt", [128, 4, 64])
        kT = sb("kT", [128, 4, T], BF16)
        dv_sb = sb("dv_sb", [128, NT, 512], BF16)
        xt = [sb("xt%d" % i, [128, D]) for i in range(2)]
        hbf = [sb("hbf%d" % i, [128, D], BF16) for i in range(2)]
        hT = sb("hT", [128, 8, GT], BF16)
        feat4 = sb("feat4", [128, 4, 4, GT], BF16)
        rqT, rkT, rgT, dqT = feat4[:, 0], feat4[:, 1], feat4[:, 2], feat4[:, 3]
        tok2 = sb("tok2", [128, 2, 4, 512], BF16)
        rv_sb, kd_sb = tok2[:, 0], tok2[:, 1]
        mT = sb("mT", [128, 8, GT], BF16)
        stg = [sb("stg%d" % i, [128, 512]) for i in range(2)]
        PT = [sb("PT%d" % i, [128, 512], BF16) for i in range(3)]
        attm4 = sb("attm4", [128, 4, 128], BF16)
        qd4 = sb("qd4", [128, 4, 128], BF16)
        oT = sb("oT", [128, 4, GT])
        Sf = sb("Sf", [128, 4, 128])
        Sb = sb("Sb", [128, 4, 128], BF16)
        onT = [oT[:, 0, :], oT[:, 1, :]]
        rden = oT[:, 2, :]
        doT = oT[:, 3, :]
        sq = sb("sq", [128, 512])
        rstd = sb("rstd", [128, 512])
        tmpA = sb("tmpA", [128, 512])
        gpv_t = sb("gpv_t", [128, 2, NFC])

        Z = [ps("Z%d" % i, [128, 512]) for i in range(2)]
        SC = [ps("SC%d" % i, [128, 512]) for i in range(2)]
        AV = ps("AV", [128, 512])
        DEN = ps("DEN", [128, 512])
        RT = ps("RT", [128, 512])
        TP = ps("TP", [128, 8, 128], BF16)
        SFK = ["Sf0", "Sf1", "Sf2", "Sf3"]
        SBK = ["Sb0", "Sb1", "Sb2", "Sb3"]

        S.add("sp", DMA(cst[:], consts_d), w=["cst"], chan="c_cst")
        S.add("pool", DMA(shift_bf[:], shift_d), w=["shift_bf"], chan="c_win")
        S.add("sp", DMA(gpre[:], n_mpre.to_broadcast([128, D])), w=["gpre"], chan="c_cst")
        S.add("sp", DMA(gpost[:], n_mpost.to_broadcast([128, D])), w=["gpost"], chan="c_cst")
        for i, d_ in enumerate((lq1, lk1, lq2, lk2)):
            S.add("sp", DMA(lamt[:, i, :], d_.to_broadcast([128, 64])), w=["lamt"], chan="c_cst")
        S.add("sp", DMA(subgc[:, 0:1], subg.rearrange("o e -> e o"), slow=True), w=["subgc"], chan="c_cst")
        for j in range(3):
            S.add("sp", DMA(cwc[:, :, j:j + 1], convw[j:j + 1, :].rearrange("o (c p) -> p c o", p=128), slow=True),
                  w=["cwc"], chan="c_cst")
        S.add("sp", DMA(cwc[:, :, 3:4], convb.rearrange("o (c p) -> p c o", p=128), slow=True), w=["cwc"], chan="c_cst")
        for c in range(8):
            S.add("pool", DMA(wbig[:, c, :], w_in[c * 128:(c + 1) * 128, :]), w=["wbig"], chan="c_win")
        for c in range(8):
            S.add("pool", DMA(wo_sb[:, c, :], w_o[c * 128:(c + 1) * 128, :]), w=["wo_sb"], chan="c_wo")

        S.add("dve", CP(ident_bf[:], cst[:, K_ID:K_ID + 128]), r=["cst"], w=["ident_bf"])
        S.add("dve", MSET(ones_bf[:], 1.0), w=["ones_bf"])
        S.add("dve", MSET(ones_f[:], 1.0), w=["ones_f"])
        S.add("dve", MSET(Sf[:], 0.0), w=SFK)
        S.add("dve", MSET(Sb[:], 0.0), w=SBK)
        S.add("dve", MSET(gpv_t[:], 0.0), w=["gpv"])
        S.add("pool", TT(lamt[:, 0, :], lamt[:, 0, :], lamt[:, 1, :], ALU.mult), r=["lamt"], w=["lamt"])
        S.add("pool", TT(lamt[:, 2, :], lamt[:, 2, :], lamt[:, 3, :], ALU.mult), r=["lamt"], w=["lamt"])
        S.add("dve", RED(lamw[:, 0:1], lamt[:, 0, :]), r=["lamt"], w=["lamw"])
        S.add("dve", RED(lamw[:, 1:2], lamt[:, 2, :]), r=["lamt"], w=["lamw"])
        S.add("act", ACT(lamw[:, 3:5], lamw[:, 0:2], AF.Exp), r=["lamw"], w=["lamw2"])
        S.add("dve", STT(lamw[:, 2:3], lamw[:, 4:5], -LAM_INIT, lamw[:, 3:4], ALU.add, ALU.subtract), r=["lamw2"], w=["neglam"])
        S.add("dve", TS(subgc[:, 1:2], subgc[:, 0:1], 1.0 - LAM_INIT, None, ALU.mult), r=["subgc"], w=["subgc2"])
        neglam = lamw[:, 2:3]
        gsub = subgc[:, 1:2]

        zc = [0]

        def nextZ():
            zc[0] += 1
            return zc[0] % 2

        def sck(c):
            return "scol%d" % c

        def rms_to_bf16(src, src_key, g_tile, g_key, out_bf, out_key, col):
            a, b_ = scol[:, col:col + 1], scol[:, col + 1:col + 2]
            S.add("act", ACT(tmpA[:, 0:512], src[:, 0:512], AF.Square, accum_out=a), r=[src_key], w=["tmpA", sck(col)])
            S.add("act", ACT(tmpA[:, 0:512], src[:, 512:1024], AF.Square, accum_out=b_), r=[src_key], w=["tmpA", sck(col + 1)])
            S.add("dve", TT(a, a, b_, ALU.add), r=[sck(col), sck(col + 1)], w=[sck(col)])
            S.add("act", ACT(a, a, AF.Sqrt, scale=1.0 / D, bias=epsc), r=[sck(col), "epsc"], w=[sck(col)])
            S.add("dve", RCP(a, a), r=[sck(col)], w=[sck(col)])
            for hf in range(2):
                hs = slice(hf * 512, (hf + 1) * 512)
                S.add("act", ACT(tmpA[:, :], src[:, hs], AF.Copy, scale=a), r=[src_key, sck(col)], w=["tmpA"])
                S.add("dve", TT(out_bf[:, hs], tmpA[:, :], g_tile[:, hs], ALU.mult), r=["tmpA", g_key], w=[out_key])

        def post_norm_residual(zb, g_tile, g_key, res, res_key, col):
            a, b_ = scol[:, col:col + 1], scol[:, col + 1:col + 2]
            S.add("act", ACT(tmpA[:, :], Z[zb[0]][:, :], AF.Square, accum_out=a), r=["Z%d" % zb[0]], w=["tmpA", sck(col)])
            S.add("act", ACT(tmpA[:, :], Z[zb[1]][:, :], AF.Square, accum_out=b_), r=["Z%d" % zb[1]], w=["tmpA", sck(col + 1)])
            S.add("dve", TT(a, a, b_, ALU.add), r=[sck(col), sck(col + 1)], w=[sck(col)])
            S.add("act", ACT(a, a, AF.Sqrt, scale=1.0 / D, bias=epsc), r=[sck(col), "epsc"], w=[sck(col)])
            S.add("dve", RCP(a, a), r=[sck(col)], w=[sck(col)])
            for hf in range(2):
                hs = slice(hf * 512, (hf + 1) * 512)
                S.add("act", ACT(tmpA[:, :], Z[zb[hf]][:, :], AF.Copy, scale=a), r=["Z%d" % zb[hf], sck(col)], w=["tmpA"])
                S.add("dve", TT(tmpA[:, :], tmpA[:, :], g_tile[:, hs], ALU.mult), r=["tmpA", g_key], w=["tmpA"])
                S.add("dve", TT(res[:, hs], res[:, hs], tmpA[:, :], ALU.add), r=["tmpA", res_key], w=[res_key])

        def transpose_tile(src_bf, src_key, dstT, dst_key, col0):
            for c in range(8):
                S.add("pe", TR(TP[:, c, :], src_bf[:, c * 128:(c + 1) * 128], ident_bf[:]), r=[src_key, "ident_bf"], w=["TP"])
            S.add("act", ACT(dstT[:, :, col0:col0 + 128], TP[:, :, :], AF.Copy), r=["TP"], w=[dst_key])

        epsc = scol[:, 7:8]
        S.add("dve", MSET(epsc, EPS), w=["epsc"])

        xpv = xp.rearrange("(t p) d -> t p d", p=128)
        x1v = x1_d.rearrange("(t p) d -> t p d", p=128)
        okp = o_kp.rearrange("(t p) d -> t p d", p=128)
        ovp = o_vp.rearrange("(t p) d -> t p d", p=128)
        stc = [0]

        def featproj(col0, eng, fn_of_z, wkeys):
            z = nextZ()
            for k in range(8):
                S.add("pe", MM(Z[z][:, :], wbig[:, k, col0:col0 + 128], hT[:, k, :], start=(k == 0), stop=(k == 7)),
                      r=["wbig", "hT"], w=["Z%d" % z])
            S.add(eng, fn_of_z(Z[z][:, :]), r=["Z%d" % z], w=wkeys)

        def tokproj(tt, col0):
            z = nextZ()
            for k in range(8):
                S.add("pe", MM(Z[z][:, :], hT[:, k, tt * 128:(tt + 1) * 128], wbig[:, k, col0:col0 + 512], start=(k == 0), stop=(k == 7)),
                      r=["wbig", "hT"], w=["Z%d" % z])
            return z

        for g in range(NG):
            for tt in range(4):
                ti = 4 * g + tt
                b = ti % 2
                S.add("sp", DMA(xt[b][:], xpv[ti]), w=["xt%d" % b], chan="c_xt%d" % b)
                rms_to_bf16(xt[b], "xt%d" % b, gpre, "gpre", hbf[b], "hbf%d" % b, 0)
                transpose_tile(hbf[b], "hbf%d" % b, hT, "hT", tt * 128)
            for h in range(4):
                featproj(C_RQ + 128 * h, "act", lambda zz, h=h: ACT(rqT[:, h, :], zz, AF.Copy), ["rqT"])
                featproj(C_RK + 128 * h, "dve", lambda zz, h=h: CP(rkT[:, h, :], zz), ["rkT"])
                featproj(C_RG + 128 * h, "act", lambda zz, h=h: ACT(rgT[:, h, :], zz, AF.Silu), ["rgT"])
                featproj(C_DQ + 128 * h, "dve", lambda zz, h=h: CP(dqT[:, h, :], zz), ["dqT"])
                featproj(C_DK + 128 * h, "act", lambda zz, h=h, g=g: ACT(kT[:, h, g * GT:(g + 1) * GT], zz, AF.Copy), ["kT"])
            for tt in range(4):
                ti = 4 * g + tt
                z = tokproj(tt, C_RV)
                S.add("act", ACT(rv_sb[:, tt, :], Z[z][:, :], AF.Copy), r=["Z%d" % z], w=["rv_sb"])
                z = tokproj(tt, C_RK)
                S.add("dve", TT(kd_sb[:, tt, :], Z[z][:, :], cst[:, K_KDEC:K_KDEC + 512], ALU.mult), r=["Z%d" % z, "cst"], w=["kd_sb"])
                z = tokproj(tt, C_DK)
                si = stc[0] % 2
                stc[0] += 1
                S.add("act", ACT(stg[si][:, :], Z[z][:, :], AF.Copy), r=["Z%d" % z], w=["stg%d" % si])
                S.add("sp", DMA(okp[ti], stg[si][:, :]), r=["stg%d" % si], chan="c_stg%d" % si)
                z = tokproj(tt, C_DV)
                si = stc[0] % 2
                stc[0] += 1
                S.add("act", ACT(stg[si][:, :], Z[z][:, :], AF.Copy), r=["Z%d" % z], w=["stg%d" % si])
                S.add("dve", CP(dv_sb[:, ti, :], Z[z][:, :]), r=["Z%d" % z], w=["dv_sb"])
                S.add("sp", DMA(ovp[ti], stg[si][:, :]), r=["stg%d" % si], chan="c_stg%d" % si)

            for ci in range(4):
                cs = slice(ci * 128, (ci + 1) * 128)
                for h in range(4):
                    hs = slice(128 * h, 128 * (h + 1))
                    S.add("pe", MM(RT[:, hs], rkT[:, h, cs], rqT[:, h, cs]), r=["rkT", "rqT"], w=["RT"])
                for h in range(4):
                    hs = slice(128 * h, 128 * (h + 1))
                    S.add("pe", MM(SC[0][:, hs], kd_sb[:, ci, hs], rv_sb[:, ci, hs]), r=["kd_sb", "rv_sb"], w=["SC0"])
                for h in range(4):
                    hs = slice(128 * h, 128 * (h + 1))
                    S.add("dve", TT(attm4[:, h, :], RT[:, hs], cst[:, K_DECAY + 128 * h:K_DECAY + 128 * (h + 1)], ALU.mult),
                          r=["RT", "cst"], w=["attm"])
                    S.add("pool", TT(qd4[:, h, :], rqT[:, h, cs], cst[:, K_QDEC + 128 * h:K_QDEC + 128 * (h + 1)], ALU.mult),
                          r=["rqT", "cst"], w=["qd"])
                for h in range(4):
                    hs = slice(128 * h, 128 * (h + 1))
                    S.add("pe", MM(AV[:, hs], rv_sb[:, ci, hs], attm4[:, h, :], start=True, stop=False), r=["rv_sb", "attm"], w=["AV"])
                    S.add("pe", MM(AV[:, hs], Sb[:, h, :], qd4[:, h, :], start=False, stop=True), r=SBK + ["qd"], w=["AV"])
                S.add("act", ACT(oT[:, :, cs], AV[:, :].rearrange("p (h t) -> p h t", t=128), AF.Copy), r=["AV"], w=["oT"])
                for h in range(4):
                    hs = slice(128 * h, 128 * (h + 1))
                    S.add("dve", STT(Sf[:, h, :], Sf[:, h, :], GAM[h] ** 128, SC[0][:, hs], ALU.mult, ALU.add), r=["SC0"] + SFK, w=SFK)
                S.add("act", ACT(Sb[:, :, :], Sf[:, :, :], AF.Copy), r=SFK, w=SBK)
            for h in range(4):
                S.add("act", ACT(sq[:, :], oT[:, h, :], AF.Square), r=["oT"], w=["sq"])
                S.add("pe", MM(DEN[:, :], ones_f[:, :], sq[:, :]), r=["ones_f", "sq"], w=["DEN"])
                S.add("act", ACT(rstd[:, :], DEN[:, :], AF.Sqrt, scale=1.0 / 128, bias=epsc), r=["DEN", "epsc"], w=["rstd"])
                S.add("dve", RCP(rstd[:, :], rstd[:, :]), r=["rstd"], w=["rstd"])
                S.add("dve", TT(rstd[:, :], rstd[:, :], rgT[:, h, :], ALU.mult), r=["rstd", "rgT"], w=["rstd"])
                S.add("dve", TT(mT[:, h, :], rstd[:, :], oT[:, h, :], ALU.mult), r=["rstd", "oT"], w=["mT"])

            nkb = 4 * g + 4
            items = [(h, m, kb) for h in range(4) for m in range(2) for kb in range(nkb)]

            SCR = [(SC[0], "SC0"), (SC[1], "SC1"), (Z[0], "Z0"), (Z[1], "Z1")]

            def emit_score(i):
                h, m, kb = items[i]
                rows = slice(64 * m, 64 * (m + 1))
                c0 = 128 * max(0, kb - 4 * g)
                scb, sck_ = SCR[i % 4]
                S.add("pe", MM(scb[:, c0:512], kT[rows, h, kb * 128:(kb + 1) * 128], dqT[rows, h, c0:512], start=True, stop=(h != 0)),
                      r=["kT", "dqT"], w=[sck_])
                if h == 0:
                    S.add("pe", MM(scb[:, c0:512], ones_bf[0:1, :], shift_bf[0:1, 0:512 - c0], start=False, stop=True),
                          r=["ones_bf", "shift_bf"], w=[sck_])
            emit_score(0)
            emit_score(1)
            for i, (h, m, kb) in enumerate(items):
                hs = slice(128 * h, 128 * (h + 1))
                r_ = max(0, kb - 4 * g)
                c0 = 128 * r_
                scb, sck_ = SCR[i % 4]
                pb = i % 3
                bidx = K_ABIAS + 16 * h + (4 * g + (r_ if h == 0 else 0) - kb) + 3
                last = (kb >= 4 * g)
                if i + 2 < len(items):
                    emit_score(i + 2)
                S.add("act", ACT(PT[pb][:, c0:512], scb[:, c0:512], AF.Exp, scale=0.125, bias=cst[:, bidx:bidx + 1]),
                      r=[sck_, "cst"], w=["PT%d" % pb])
                if last:
                    S.add("pool", TT(PT[pb][:, c0:c0 + 128], PT[pb][:, c0:c0 + 128], cst[:, K_CAUS:K_CAUS + 128], ALU.mult),
                          r=["PT%d" % pb, "cst"], w=["PT%d" % pb])
                S.add("pe", MM(AV[:, c0:512], dv_sb[:, kb, hs], PT[pb][:, c0:512], start=(kb == 0), stop=last),
                      r=["dv_sb", "PT%d" % pb], w=["AV"])
                S.add("pe", MM(DEN[:, c0:512], ones_bf[:, :], PT[pb][:, c0:512], start=(kb == 0), stop=last),
                      r=["ones_bf", "PT%d" % pb], w=["DEN"])
                if kb == nkb - 1:
                    S.add("dve", RCP(rden, DEN[:, :]), r=["DEN"], w=["oT"])
                    S.add("dve", TT(onT[m], AV[:, :], rden, ALU.mult), r=["AV", "oT"], w=["oT"])
                    if m == 1:
                        S.add("dve", STT(doT, onT[1], neglam, onT[0], ALU.mult, ALU.add), r=["oT", "neglam"], w=["oT"])
                        S.add("act", ACT(sq[:, :], doT, AF.Square), r=["oT"], w=["sq"])
                        S.add("pe", MM(RT[:, :], ones_f[:, :], sq[:, :]), r=["ones_f", "sq"], w=["RT"])
                        S.add("act", ACT(rstd[:, :], RT[:, :], AF.Sqrt, scale=1.0 / 128, bias=epsc), r=["RT", "epsc"], w=["rstd"])
                        S.add("dve", RCP(rstd[:, :], rstd[:, :]), r=["rstd"], w=["rstd"])
                        S.add("dve", STT(mT[:, 4 + h, :], doT, gsub, rstd[:, :], ALU.mult, ALU.mult), r=["oT", "rstd", "subgc2"], w=["mT"])

            if dbg == "m" and g == 0:
                S.add("pool", DMA(y_p[0:1024, 0:512].rearrange("(c p) t -> p c t", p=128), mT[:, :, :]), r=["mT"], chan="c_misc")
                S.emit(nc, sem_stack=esP)
                return nc
            for tt in range(4):
                ti = 4 * g + tt
                b = ti % 2
                S.add("sp", DMA(xt[b][:], xpv[ti]), w=["xt%d" % b], chan="c_xt%d" % b)
                zb = []
                for hf in range(2):
                    z = nextZ()
                    zb.append(z)
                    for c in range(8):
                        S.add("pe", MM(Z[z][:, :], mT[:, c, tt * 128:(tt + 1) * 128], wo_sb[:, c, hf * 512:(hf + 1) * 512], start=(c == 0), stop=(c == 7)),
                              r=["mT", "wo_sb"], w=["Z%d" % z])
                post_norm_residual(zb, gpost, "gpost", xt[b], "xt%d" % b, 2)
                S.add("sp", DMA(x1v[ti], xt[b][:]), r=["xt%d" % b], w=["x1d%d" % ti], chan="c_xo%d" % b)

        S.add("sp", DMA(o_rsp.rearrange("h d e -> d h e"), Sf[:, :, :]), r=SFK, chan="c_misc")
        if dbg == "x1":
            S.add("sp", DMA(y_p, x1_d), r=["x1d%d" % i for i in range(NT)], chan="c_misc")
            S.emit(nc, sem_stack=esP)
            return nc

        S.add("sp", DMA(gpre[:], n_fpre.to_broadcast([128, D])), w=["gpre"], chan="c_g2")
        S.add("sp", DMA(gpost[:], n_fpost.to_broadcast([128, D])), w=["gpost"], chan="c_g2")
        wfo_sb = wbig.rearrange("p a b -> p (a b)")[:, 0:NFC * D].rearrange("p (c d) -> p c d", d=D)
        for c in range(NFC):
            S.add("pool", DMA(wfo_sb[:, c, :], w_fo[c * 128:(c + 1) * 128, :]), w=["wbig"], chan="c_wfo")
        aT = kT.rearrange("p a b -> p (a b)")
        aT2 = dv_sb.rearrange("p a b -> p (a b)")

        def aTc(fc):
            if fc < 16:
                return aT[:, fc * 512:(fc + 1) * 512], "kT"
            return aT2[:, (fc - 16) * 512:(fc - 15) * 512], "dv_sb"
        WB = [feat4[:, 0:2].rearrange("p a h t -> p (a h t)").rearrange("p (k c) -> p k c", c=512),
              feat4[:, 2:4].rearrange("p a h t -> p (a h t)").rearrange("p (k c) -> p k c", c=512),
              mT[:, :, :],
              tok2.rearrange("p a h t -> p (a h t)").rearrange("p (k c) -> p k c", c=512)]
        WBK = [["rqT", "rkT"], ["rgT", "dqT"], ["mT"], ["rv_sb", "kd_sb"]]
        blkc = [0]
        gsb = [oT[:, 0, :], oT[:, 1, :]]
        c1 = sq
        c2 = rstd
        yv = y_p.rearrange("(t p) d -> t p d", p=128)
        wfiv = w_fi.rearrange("(k p) c -> p k c", p=128)

        for g in range(NG):
            for tt in range(4):
                ti = 4 * g + tt
                b = ti % 2
                S.add("sp", DMA(xt[b][:], x1v[ti]), r=["x1d%d" % ti], w=["xt%d" % b], chan="c_xt%d" % b)
                rms_to_bf16(xt[b], "xt%d" % b, gpre, "gpre", hbf[b], "hbf%d" % b, 0)
                transpose_tile(hbf[b], "hbf%d" % b, hT, "hT", tt * 128)
            for fc in range(NFC):
                if fc % 4 == 0:
                    nfc_blk = min(4, NFC - fc)
                    par = blkc[0] % 2
                    blkc[0] += 1
                    gW, uW = WB[2 * par], WB[2 * par + 1]
                    gWk, uWk = WBK[2 * par], WBK[2 * par + 1]
                    ncol = 128 * nfc_blk
                    S.add("pool", DMA(gW[:, :, 0:ncol], wfiv[:, :, fc * 128:fc * 128 + ncol]), w=gWk, chan="c_wfi%d" % par)
                    S.add("pool", DMA(uW[:, :, 0:ncol], wfiv[:, :, DFF + fc * 128:DFF + fc * 128 + ncol]), w=uWk, chan="c_wfi%d" % par)
                fo = (fc % 4) * 128
                if fc % 2 == 0:
                    G_, U_, gkey, ukey = SC[0], SC[1], "SC0", "SC1"
                else:
                    G_, U_, gkey, ukey = AV, DEN, "AV", "DEN"
                for k in range(8):
                    S.add("pe", MM(G_[:, :], gW[:, k, fo:fo + 128], hT[:, k, :], start=(k == 0), stop=(k == 7)), r=gWk + ["hT"], w=[gkey])
                for k in range(8):
                    S.add("pe", MM(U_[:, :], uW[:, k, fo:fo + 128], hT[:, k, :], start=(k == 0), stop=(k == 7)), r=uWk + ["hT"], w=[ukey])
                gs = gsb[fc % 2]
                gk = "gsb%d" % (fc % 2)
                w0, w1, w2, bb = cwc[:, fc, 0:1], cwc[:, fc, 1:2], cwc[:, fc, 2:3], cwc[:, fc, 3:4]
                S.add("act", ACT(gs[:, :], G_[:, :], AF.Copy), r=[gkey], w=[gk])
                S.add("act", ACT(c1[:, :], G_[:, :], AF.Copy, scale=w2), r=[gkey, "cwc"], w=["c1"])
                S.add("dve", STT(c1[:, 1:512], gs[:, 0:511], w1, c1[:, 1:512], ALU.mult, ALU.add), r=[gk, "c1", "cwc"], w=["c1"])
                S.add("dve", STT(c1[:, 0:1], gpv_t[:, 1:2, fc], w1, c1[:, 0:1], ALU.mult, ALU.add), r=["gpv", "c1", "cwc"], w=["c1"])
                S.add("dve", STT(c1[:, 2:512], gs[:, 0:510], w0, c1[:, 2:512], ALU.mult, ALU.add), r=[gk, "c1", "cwc"], w=["c1"])
                S.add("dve", STT(c1[:, 0:2], gpv_t[:, 0:2, fc], w0, c1[:, 0:2], ALU.mult, ALU.add), r=["gpv", "c1", "cwc"], w=["c1"])
                S.add("dve", CP(gpv_t[:, 0:2, fc], gs[:, 510:512]), r=[gk, "gpv"], w=["gpv"])
                S.add("act", ACT(c2[:, :], c1[:, :], AF.Gelu_apprx_tanh, bias=bb), r=["c1", "cwc"], w=["c2"])
                a_ap, a_key = aTc(fc)
                S.add("dve", TT(a_ap, U_[:, :], c2[:, :], ALU.mult), r=[ukey, "c2"], w=[a_key])
            for tt in range(4):
                ti = 4 * g + tt
                b = ti % 2
                S.add("sp", DMA(xt[b][:], x1v[ti]), r=["x1d%d" % ti], w=["xt%d" % b], chan="c_xt%d" % b)
                zb = []
                for hf in range(2):
                    z = nextZ()
                    zb.append(z)
                    for fc in range(NFC):
                        a_ap, a_key = aTc(fc)
                        S.add("pe", MM(Z[z][:, :], a_ap[:, tt * 128:(tt + 1) * 128], wfo_sb[:, fc, hf * 512:(hf + 1) * 512],
                                       start=(fc == 0), stop=(fc == NFC - 1)), r=[a_key, "wbig"], w=["Z%d" % z])
                post_norm_residual(zb, gpost, "gpost", xt[b], "xt%d" % b, 2)
                S.add("sp", DMA(yv[ti], xt[b][:]), r=["xt%d" % b], chan="c_xo%d" % b)
        for j in range(2):
            S.add("sp", DMA(o_cp[j:j + 1, :].rearrange("o (c p) -> p (o c)", p=128), gpv_t[:, j, :], slow=True), r=["gpv"], chan="c_misc")

        print('n_ops', len(S.ops))
        S.emit(nc, limit=limit, sem_stack=esP)
        es.close()
        if with_sample and limit is None:
            nc.all_engine_barrier()
            build_sample(nc, esP, locals())
    return nc


_CACHE = {}


def _get_program(n_pool):
    if n_pool not in _CACHE:
        _CACHE[n_pool] = build_program(n_pool)
    return _CACHE[n_pool]


def kernel(x_prompt, x_sample, state_ret, cache_k, cache_v, state_conv, page_table,
           norm_mix_pre, norm_mix_post, w_in, w_o, lambda_q1, lambda_k1, lambda_q2, lambda_k2,
           subln_g, norm_ffn_pre, norm_ffn_post, w_ffn_in, conv_w, conv_b, w_ffn_out):
    f = lambda a: np.ascontiguousarray(np.asarray(a, dtype=np.float32))
    x_prompt, x_sample = f(x_prompt), f(x_sample)
    n_pool = int(np.asarray(cache_k).shape[1])
    nc = _get_program(n_pool)
    consts, shiftrow = make_consts()
    ck = f(cache_k)[0].reshape(n_pool * 128, 512)
    cv = f(cache_v)[0].reshape(n_pool * 128, 512)
    pt = np.ascontiguousarray(np.asarray(page_table, dtype=np.int32))
    shared = dict(
        w_in=f(w_in)[0], w_o=f(w_o)[0], w_fi=f(w_ffn_in)[0], w_fo=f(w_ffn_out)[0],
        n_mpre=f(norm_mix_pre), n_mpost=f(norm_mix_post), n_fpre=f(norm_ffn_pre), n_fpost=f(norm_ffn_post),
        lq1=f(lambda_q1), lk1=f(lambda_k1), lq2=f(lambda_q2), lk2=f(lambda_k2), subg=f(subln_g),
        convw=f(conv_w)[0], convb=f(conv_b), consts=consts, shiftrow=shiftrow, ck=ck, cv=cv, consts2=make_consts2(),
    )
    sret = f(state_ret)[0]
    sconv = f(state_conv)[0]
    in_maps = []
    for c in range(8):
        m = dict(shared)
        m["xp"] = x_prompt[c]
        m["xs"] = np.ascontiguousarray(x_sample[c * NS:(c + 1) * NS, 0, :])
        m["sret"] = np.ascontiguousarray(sret[c * NS:(c + 1) * NS])
        m["sconv"] = np.ascontiguousarray(sconv[c * NS:(c + 1) * NS])
        m["ptab"] = np.ascontiguousarray(pt[c * NS:(c + 1) * NS])
        in_maps.append(m)
    res = run_bass_kernel_spmd(nc, in_maps, core_ids=list(range(8)))
    R = res.results
    cat = lambda k: np.stack([np.asarray(r[k]) for r in R], axis=0)
    y_prompt = cat("y_p")
    y_sample = np.concatenate([np.asarray(r["y_s"]) for r in R], axis=0).reshape(128, 1, D)
    rsp = cat("o_rsp")[None]
    rss = np.concatenate([np.asarray(r["o_rss"]) for r in R], axis=0)[None]
    kp = cat("o_kp").reshape(1, 8, T, 4, 2, 64)
    vp = cat("o_vp").reshape(1, 8, T, 4, 128)
    ks = np.concatenate([np.asarray(r["o_ks"]) for r in R], axis=0).reshape(1, 128, 1, 4, 2, 64)
    vs = np.concatenate([np.asarray(r["o_vs"]) for r in R], axis=0).reshape(1, 128, 1, 4, 128)
    cp = cat("o_cp")[None]
    cs = np.concatenate([np.asarray(r["o_cs"]) for r in R], axis=0)[None]
    return (y_prompt, y_sample, rsp, rss, kp, vp, ks, vs, cp, cs)
```

```python
import math
import bisect
import contextlib
import numpy as np
import concourse.bass as bass
import concourse.mybir as mybir
from concourse.bass_utils import run_bass_kernel_spmd

F32 = mybir.dt.float32
BF16 = mybir.dt.bfloat16
I32 = mybir.dt.int32
AF = mybir.ActivationFunctionType
ALU = mybir.AluOpType
AX = mybir.AxisListType

D = 1024
T = 2048
NT = 16
GT = 512
NG = 4
DFF = 2816
NFC = 22
INW = 3584
EPS = 1e-6
NS = 16
H = 4
C_RQ, C_RK, C_RV, C_RG, C_DQ, C_DK, C_DV = 0, 512, 1024, 1536, 2048, 2560, 3072
GAM = [1.0 - 2.0 ** (-5.0 - h) for h in range(4)]
SLOPE = [2.0 ** (-2.0 * (h + 1)) for h in range(4)]
LAM_INIT = 0.8 - 0.6 * math.exp(-0.3 * 0)

K_ID = 0
K_DECAY = 128
K_QDEC = K_DECAY + 512
K_KDEC = K_QDEC + 512
K_CAUS = K_KDEC + 512
K_ABIAS = K_CAUS + 128
K_END = K_ABIAS + 64


def make_consts():
    c = np.zeros((128, K_END), np.float32)
    c[:, K_ID:K_ID + 128] = np.eye(128, dtype=np.float32)
    i = np.arange(128, dtype=np.float64)
    for h in range(4):
        g = GAM[h]
        diff = i[None, :] - i[:, None]
        dec = np.where(diff >= 0, g ** np.maximum(diff, 0), 0.0) * (128 ** -0.5)
        c[:, K_DECAY + 128 * h:K_DECAY + 128 * (h + 1)] = dec
        c[:, K_QDEC + 128 * h:K_QDEC + 128 * (h + 1)] = (g ** (i + 1.0))[None, :]
        c[:, K_KDEC + 128 * h:K_KDEC + 128 * (h + 1)] = ((128 ** -0.5) * g ** (127.0 - i))[:, None]
        for di in range(16):
            dd = di - 3
            c[:, K_ABIAS + 16 * h + di] = SLOPE[h] * (i - 128.0 * dd)
    c[:, K_CAUS:K_CAUS + 128] = (i[None, :] >= i[:, None]).astype(np.float32)
    sh = np.zeros((1, 4 * 512), np.float32)
    tl = np.arange(512, dtype=np.float64)
    for h in range(4):
        sh[0, 512 * h:512 * (h + 1)] = -8.0 * SLOPE[h] * tl
    return c, sh


K2_OH = 0
K2_OFFS = 256
K2_BIAS = 264
K2_Z = K2_BIAS + 128
K2_END = K2_Z + 32
NQ = 8


def make_consts2():
    c = np.zeros((128, K2_END), np.float32)
    for b in range(16):
        c[:, K2_OH + 16 * b + b] = 1.0
    p = np.arange(128)
    for q in range(NQ):
        c[:, K2_OFFS + q] = (p % 8) * 8 + q
        for t in range(2):
            s = (p // 8) * 128 + (p % 8) * 16 + 2 * q + t
            for hh in range(4):
                for m in range(2):
                    c[:, K2_BIAS + q * 16 + t * 8 + hh * 2 + m] = -8.0 * SLOPE[hh] * (2048.0 - s)
    return c


class Op:
    __slots__ = ("eng", "fn", "deps", "chan", "chan_idx", "sig", "seq", "idx", "hard")


class Sched:
    ENGS = ("pe", "act", "dve", "pool", "sp")

    def __init__(self, excl=None, tag=""):
        if excl is not None:
            self.EXCL = tuple(excl)
        self.tag = tag
        self.ops = []
        self.last_w = {}
        self.readers = {}
        self.chan_cnt = {}

    EXCL = ("Z0", "Z1", "SC0", "SC1", "AV", "DEN", "RT", "TP")

    def add(self, eng, fn, r=(), w=(), chan=None, hard=False):
        w = list(w) + [k for k in r if k in self.EXCL]
        r = [k for k in r if k not in self.EXCL]
        op = Op()
        op.eng, op.fn, op.chan, op.sig, op.seq = eng, fn, chan, False, 0
        op.hard = hard
        op.idx = len(self.ops)
        deps = set()
        for k in r:
            lw = self.last_w.get(k)
            if lw is not None:
                deps.add(lw)
        for k in w:
            lw = self.last_w.get(k)
            if lw is not None:
                deps.add(lw)
            for rd in self.readers.get(k, {}).values():
                deps.add(rd)
        rkey = ("c", chan) if chan is not None else eng
        for k in r:
            self.readers.setdefault(k, {})[rkey] = op
        for k in w:
            self.last_w[k] = op
            self.readers[k] = {}
        deps.discard(op)
        op.deps = deps
        if chan is not None:
            self.chan_cnt[chan] = self.chan_cnt.get(chan, 0) + 1
            op.chan_idx = self.chan_cnt[chan]
        else:
            op.chan_idx = 0
        self.ops.append(op)
        return op

    def emit(self, nc, final_eng="sp", limit=None, sem_stack=None):
        ops = self.ops
        if limit is not None:
            ops = ops[:limit]
            self.chan_cnt = {}
            for x in ops:
                if x.chan is not None:
                    self.chan_cnt[x.chan] = self.chan_cnt.get(x.chan, 0) + 1
        def needs_sem(x, d):
            if d.chan is not None:
                return True
            if d.eng == x.eng and x.chan is None and not x.hard:
                return False
            return True
        for x in ops:
            for d in x.deps:
                if needs_sem(x, d) and d.chan is None:
                    d.sig = True
        cnt = {e: 0 for e in self.ENGS}
        for x in ops:
            if x.chan is None and x.sig:
                cnt[x.eng] += 1
                x.seq = cnt[x.eng]
        chans = sorted(self.chan_cnt.keys())
        with contextlib.ExitStack() as es:
            ss = sem_stack if sem_stack is not None else es
            esem = {e: ss.enter_context(nc.semaphore(self.tag + "sem_" + e)) for e in self.ENGS}
            csem = {c: ss.enter_context(nc.semaphore(self.tag + "ch_" + str(c))) for c in chans}
            nc.all_engine_barrier()
            block = es.enter_context(nc.Block())
            per_eng = {e: [x for x in ops if x.eng == e] for e in self.ENGS}
            chan_idx_list = {c: [x.idx for x in ops if x.chan == c] for c in chans}

            def body(e, eng):
                waited = {}
                for x in per_eng[e]:
                    need = {}
                    for d in x.deps:
                        if not needs_sem(x, d):
                            continue
                        if d.chan is not None:
                            key, val = ("c", d.chan), 16 * bisect.bisect_left(chan_idx_list[d.chan], x.idx)
                        else:
                            key, val = ("e", d.eng), d.seq
                        if val > need.get(key, 0):
                            need[key] = val
                    for key, val in need.items():
                        if waited.get(key, 0) >= val:
                            continue
                        waited[key] = val
                        sem = csem[key[1]] if key[0] == "c" else esem[key[1]]
                        eng.wait_ge(sem, val)
                    inst = x.fn(eng)
                    if x.chan is not None:
                        inst.then_inc(csem[x.chan], 16)
                    elif x.sig:
                        inst.then_inc(esem[x.eng], 1)
                if e == final_eng:
                    for c in chans:
                        eng.wait_ge(csem[c], 16 * self.chan_cnt[c])

            @block.tensor
            def _(eng):
                body("pe", eng)

            @block.scalar
            def _(eng):
                body("act", eng)

            @block.vector
            def _(eng):
                body("dve", eng)

            @block.gpsimd
            def _(eng):
                body("pool", eng)

            @block.sync
            def _(eng):
                body("sp", eng)


def MM(out, lhsT, rhs, start=True, stop=True):
    return lambda e: e.matmul(out=out, lhsT=lhsT, rhs=rhs, start=start, stop=stop)


def TR(out, in_, identity):
    return lambda e: e.transpose(out=out, in_=in_, identity=identity)


def ACT(out, in_, func, **kw):
    return lambda e: e.activation(out=out, in_=in_, func=func, **kw)


def TT(out, in0, in1, op):
    return lambda e: e.tensor_tensor(out=out, in0=in0, in1=in1, op=op)


def STT(out, in0, scalar, in1, op0, op1):
    return lambda e: e.scalar_tensor_tensor(out=out, in0=in0, scalar=scalar, in1=in1, op0=op0, op1=op1)


def TS(out, in0, s1, s2, op0, op1=None):
    if op1 is None:
        return lambda e: e.tensor_scalar(out=out, in0=in0, scalar1=s1, scalar2=None, op0=op0)
    return lambda e: e.tensor_scalar(out=out, in0=in0, scalar1=s1, scalar2=s2, op0=op0, op1=op1)


def CP(out, in_):
    return lambda e: e.tensor_copy(out=out, in_=in_)


def RCP(out, in_):
    return lambda e: e.reciprocal(out=out, in_=in_)


def MSET(ap, v):
    return lambda e: e.memset(ap, v)


def RED(out, in_, op=None):
    return lambda e: e.tensor_reduce(out=out, in_=in_, axis=AX.X, op=(op or ALU.add))


def DMA(out, in_, slow=False):
    if slow:
        return lambda e: e.dma_start(out=out, in_=in_, allow_slow_non_contiguous=True)
    return lambda e: e.dma_start(out=out, in_=in_)


def IDMA(out, in_, idx_ap):
    return lambda e: e.indirect_dma_start(out=out, out_offset=None, in_=in_,
                                          in_offset=bass.IndirectOffsetOnAxis(ap=idx_ap, axis=0))


def build_sample(nc, esP, L):
    cst, ident_bf, ones_f, wbig, wo_sb, lamw, subgc, scol = (L[k] for k in
                                                              ("cst", "ident_bf", "ones_f", "wbig", "wo_sb", "lamw", "subgc", "scol"))
    xs, sret, ck, cv, sconv, ptab = (L[k] for k in ("xs", "sret", "ck", "cv", "sconv", "ptab"))
    w_in, w_fi = L["w_in"], L["w_fi"]
    n_mpre, n_mpost, n_fpre, n_fpost, subg, convw, convb = (L[k] for k in
                                                            ("n_mpre", "n_mpost", "n_fpre", "n_fpost", "subg", "convw", "convb"))
    y_s, o_rss, o_ks, o_vs, o_cs = (L[k] for k in ("y_s", "o_rss", "o_ks", "o_vs", "o_cs"))
    consts2_d = L["consts2_d"]
    neglam = lamw[0:NS, 2:3]
    epsc = scol[0:NS, 7:8]
    wfo_sb = wbig.rearrange("p a b -> p (a b)")[:, 0:NFC * D].rearrange("p (c d) -> p c d", d=D)

    banks = ("TPs", "Zs0", "Zs1", "QB", "SN", "Rps", "Oacc", "MISC")
    S = Sched(excl=banks, tag="s2_")
    es = contextlib.ExitStack()
    with es:
        def sb(name, shape, dtp=F32):
            return es.enter_context(nc.sbuf_tensor("s2_" + name, list(shape), dtp))

        def ps(name, shape, dtp=F32):
            return es.enter_context(nc.psum_tensor("s2_" + name, list(shape), dtp))

        cst2 = sb("cst2", [128, K2_END])
        xs_t = sb("xs_t", [NS, D])
        gA = sb("gA", [NS, D])
        gB = sb("gB", [NS, D])
        hs_bf = sb("hs_bf", [NS, D], BF16)
        hsT = sb("hsT", [128, 8, NS], BF16)
        wblk = sb("wblk", [128, 8, 512], BF16)
        zs = [sb("zs%d" % i, [NS, 512]) for i in range(7)]
        rq, rk, rv, rg, dq, dk, dv = zs
        ptsel = sb("ptsel", [128, NS], I32)
        idx = sb("idx", [128, NS, NQ], I32)
        Kq = [sb("Kq%d" % i, [128, 2, 512]) for i in range(2)]
        Vq = [sb("Vq%d" % i, [128, 2, 512]) for i in range(2)]
        prod = [sb("prod%d" % i, [128, 2, 512]) for i in range(2)]
        Vb = [sb("Vb%d" % i, [128, 2, 512], BF16) for i in range(2)]
        qb = [sb("qb%d" % i, [128, 512]) for i in range(2)]
        S4 = [sb("S4_%d" % i, [128, 16]) for i in range(2)]
        S4b = [sb("S4b_%d" % i, [128, 16]) for i in range(2)]
        Pc = [sb("Pc%d" % i, [128, 2, 8]) for i in range(2)]
        Pacc = sb("Pacc", [128, NS, 8])

        Pz = [sb("Pz%d" % i, [128, 2, 4, 32], BF16) for i in range(2)]
        St = [sb("St%d" % i, [128, 4, 128]) for i in range(2)]
        Sn = [sb("Sn%d" % i, [128, 4, 128]) for i in range(2)]
        Qz = sb("Qz", [128, 4, NS, NS])
        rqTs = sb("rqTs", [128, 4, NS])
        Km = [sb("Km%d" % i, [NS, 512]) for i in range(2)]
        Kr = [sb("Kr%d" % i, [NS, 512]) for i in range(2)]
        t16 = [sb("t16_%d" % i, [32 if i == 0 else NS, 512]) for i in range(7)]
        sm = sb("sm", [NS, 64])
        ms_bf = sb("ms_bf", [NS, D], BF16)
        a2_bf = sb("a2_bf", [NS, DFF], BF16)
        a2T = sb("a2T", [128, NFC, NS], BF16)
        arena = sb("arena", [128, 1024])
        Pq = arena[:, :].rearrange("p (q b g) -> p q b g", q=NQ, b=NS)
        cwb = arena[0:NS, :].rearrange("p (j c) -> p j c", j=4)
        scb = sb("scb", [NS, 2, 256])
        cc = [sb("cc%d" % i, [NS, 256]) for i in range(3)]
        gsub_bc = sb("gsub_bc", [NS, 4, 128])
        tmpS = t16[2]

        TPs = ps("TPs", [128, 8, 128], BF16)
        Zs = [ps("Zs%d" % i, [128, 512]) for i in range(2)]
        QB = ps("QB", [128, 512])
        SN = ps("SN", [128, 512])
        Rps = ps("Rps", [128, 512])
        Oacc = ps("Oacc", [128, 512])
        MISC = ps("MISC", [128, 512])

        A = S.add
        A("sp", DMA(cst2[:], consts2_d), w=["cst2"], chan="k_c")
        A("sp", DMA(xs_t[:], xs), w=["xs_t"], chan="k_c")
        A("sp", DMA(gA[:], n_mpre.to_broadcast([NS, D])), w=["gA"], chan="k_c")
        A("sp", DMA(gB[:], n_mpost.to_broadcast([NS, D])), w=["gB"], chan="k_c")
        for h in range(4):
            A("sp", DMA(gsub_bc[:, h, :], subg.to_broadcast([NS, 128])), w=["gsub_bc"], chan="k_c")
        for j in range(16):
            A("sp", DMA(ptsel[8 * j:8 * j + 8, :], ptab[:, j:j + 1].rearrange("b o -> o b").to_broadcast([8, NS]), slow=True),
              w=["ptsel"], chan="k_c")
        A("sp", DMA(o_cs[:, 0, :], sconv[:, 1, :]), chan="k_out")
        A("dve", TS(gsub_bc[:], gsub_bc[:], 1.0 - LAM_INIT, None, ALU.mult), r=["gsub_bc"], w=["gsub_bc"])
        for q in range(NQ):
            A("dve", TS(idx[:, :, q], ptsel[:, :], 64.0, cst2[:, K2_OFFS + q:K2_OFFS + q + 1], ALU.mult, ALU.add),
              r=["ptsel", "cst2"], w=["idx"])

        if L.get("dbg") == "idx":
            A("sp", DMA(y_s.rearrange("b (x c) -> (b x) c", x=8), idx[:].rearrange("p b q -> p (b q)").bitcast(F32)), r=["idx"], chan="k_out")
            S.emit(nc, sem_stack=esP)
            return

        def smc(c, n=1):
            return sm[:, c:c + n]

        def rms16(src, src_key, g_tile, g_key, out_ap, out_key, col):
            a = smc(col)
            A("act", ACT(tmpS[:, :], src[:, 0:512], AF.Square, accum_out=smc(col)), r=[src_key], w=["t2", "sm%d" % col])
            A("act", ACT(tmpS[:, :], src[:, 512:1024], AF.Square, accum_out=smc(col + 1)), r=[src_key], w=["t2", "sm%d" % (col + 1)])
            A("dve", TT(a, a, smc(col + 1), ALU.add), r=["sm%d" % col, "sm%d" % (col + 1)], w=["sm%d" % col])
            A("act", ACT(a, a, AF.Sqrt, scale=1.0 / D, bias=epsc), r=["sm%d" % col], w=["sm%d" % col])
            A("dve", RCP(a, a), r=["sm%d" % col], w=["sm%d" % col])
            for hf in range(2):
                hs = slice(hf * 512, (hf + 1) * 512)
                A("act", ACT(tmpS[:, :], src[:, hs], AF.Copy, scale=a), r=[src_key, "sm%d" % col], w=["t2"])
                A("dve", TT(out_ap[:, hs], tmpS[:, :], g_tile[:, hs], ALU.mult), r=["t2", g_key], w=[out_key])

        def transpose16(src_bf, src_key, dstT, dst_key, nchunk):
            for c0 in range(0, nchunk, 8):
                n = min(8, nchunk - c0)
                for c in range(n):
                    A("pe", TR(TPs[:, c, 0:NS], src_bf[:, (c0 + c) * 128:(c0 + c + 1) * 128], ident_bf[0:NS, 0:NS]),
                      r=[src_key, "ident_bf"], w=["TPs"])
                A("act", ACT(dstT[:, c0:c0 + n, :], TPs[:, 0:n, 0:NS], AF.Copy), r=["TPs"], w=[dst_key])

        def postnorm16(zb, g_tile, g_key, res, res_key, col):
            a = smc(col)
            A("act", ACT(tmpS[:, :], Zs[zb[0]][0:NS, :], AF.Square, accum_out=smc(col)), r=["Zs%d" % zb[0]], w=["t2", "sm%d" % col])
            A("act", ACT(tmpS[:, :], Zs[zb[1]][0:NS, :], AF.Square, accum_out=smc(col + 1)), r=["Zs%d" % zb[1]], w=["t2", "sm%d" % (col + 1)])
            A("dve", TT(a, a, smc(col + 1), ALU.add), r=["sm%d" % col, "sm%d" % (col + 1)], w=["sm%d" % col])
            A("act", ACT(a, a, AF.Sqrt, scale=1.0 / D, bias=epsc), r=["sm%d" % col], w=["sm%d" % col])
            A("dve", RCP(a, a), r=["sm%d" % col], w=["sm%d" % col])
            for hf in range(2):
                hs = slice(hf * 512, (hf + 1) * 512)
                A("act", ACT(tmpS[:, :], Zs[zb[hf]][0:NS, :], AF.Copy, scale=a), r=["Zs%d" % zb[hf], "sm%d" % col], w=["t2"])
                A("pool", TT(tmpS[:, :], tmpS[:, :], g_tile[:, hs], ALU.mult), r=["t2", g_key], w=["t2"])
                A("dve", TT(res[:, hs], res[:, hs], tmpS[:, :], ALU.add), r=["t2", res_key], w=[res_key])

        rms16(xs_t, "xs_t", gA, "gA", hs_bf, "hs_bf", 0)
        transpose16(hs_bf, "hs_bf", hsT, "hsT", 8)
        winv = w_in.rearrange("(k p) c -> p k c", p=128)
        for blk in range(7):
            A("pool", DMA(wblk[:, :, :], winv[:, :, blk * 512:(blk + 1) * 512]), w=["wblk"], chan="k_w")
            z = blk % 2
            for k in range(8):
                A("pe", MM(Zs[z][0:NS, :], hsT[:, k, :], wblk[:, k, :], start=(k == 0), stop=(k == 7)), r=["hsT", "wblk"], w=["Zs%d" % z])
            if blk == 3:
                A("act", ACT(zs[blk][:, :], Zs[z][0:NS, :], AF.Silu), r=["Zs%d" % z], w=["zs%d" % blk])
            else:
                A("act", ACT(zs[blk][:, :], Zs[z][0:NS, :], AF.Copy), r=["Zs%d" % z], w=["zs%d" % blk])
        A("sp", DMA(o_ks, dk[:, :]), r=["zs5"], chan="k_out")
        A("sp", DMA(o_vs, dv[:, :]), r=["zs6"], chan="k_out")

        A("act", ACT(rk[:, :], rk[:, :], AF.Copy, scale=128 ** -0.5), r=["zs1"], w=["zs1"])
        A("pool", TT(t16[1][:, :], rq[:, :], rk[:, :], ALU.mult), r=["zs0", "zs1"], w=["t1"])
        A("dve", RED(smc(8, 4), t16[1][:, :].rearrange("p (h d) -> p h d", d=128)), r=["t1"], w=["qk"])
        for h in range(4):
            A("pe", TR(MISC[:, h * NS:(h + 1) * NS], rq[:, h * 128:(h + 1) * 128], cst[0:NS, K_ID:K_ID + NS]), r=["zs0", "cst"], w=["MISC"])
        A("act", ACT(rqTs[:, :, :], MISC[:, 0:4 * NS].rearrange("p (h b) -> p h b", b=NS), AF.Copy), r=["MISC"], w=["rqTs"])
        for h in range(4):
            for b in range(NS):
                A("dve", TS(Qz[:, h, b, :], cst2[:, K2_OH + 16 * b:K2_OH + 16 * b + 16], rqTs[:, h, b:b + 1], None, ALU.mult),
                  r=["cst2", "rqTs"], w=["Qz"])
        A("pe", MM(Rps[0:NS, :], cst2[:, K2_Z:K2_Z + NS], cst[:, 0:512], start=True, stop=False), r=["cst2", "cst"], w=["Rps"])
        srv = sret.rearrange("b h d e -> b d h e")
        orv = o_rss.rearrange("b h d e -> b d h e")
        def ret_step(b):
            sbuf = b % 2
            A("sp", DMA(St[sbuf][:, :, :], srv[b]), w=["St%d" % sbuf], chan="k_st%d" % sbuf)
            for h in range(4):
                A("pe", MM(Rps[0:NS, h * 128:(h + 1) * 128], Qz[:, h, b, :], St[sbuf][:, h, :], start=False, stop=(b == NS - 1)),
                  r=["Qz", "St%d" % sbuf], w=["Rps"])
            A("act", ACT(Kr[sbuf][:, :], rk[:, :], AF.Copy, scale=cst[0:NS, K_ID + b:K_ID + b + 1]), r=["zs1", "cst"], w=["Kr%d" % sbuf])
            for h in range(4):
                hs = slice(h * 128, (h + 1) * 128)
                A("pe", MM(SN[:, hs], Kr[sbuf][:, hs], rv[:, hs], start=True, stop=True), r=["Kr%d" % sbuf, "zs2"], w=["SN"])
            for h in range(4):
                hs = slice(h * 128, (h + 1) * 128)
                A("dve", STT(Sn[sbuf][:, h, :], St[sbuf][:, h, :], GAM[h], SN[:, hs], ALU.mult, ALU.add),
                  r=["St%d" % sbuf, "SN"], w=["Sn%d" % sbuf])
            A("sp", DMA(orv[b], Sn[sbuf][:, :, :]), r=["Sn%d" % sbuf], chan="k_sn%d" % sbuf)
        ck2 = ck.rearrange("(r t) f -> r (t f)", t=2)
        cv2 = cv.rearrange("(r t) f -> r (t f)", t=2)
        A("pe", MM(Oacc[0:32, :], cst2[:, K2_Z:K2_Z + 32], cst[:, 0:512], start=True, stop=False), r=["cst2", "cst"], w=["Oacc"])

        first = False
        it = 0
        gl = [(b_, q_) for b_ in range(NS) for q_ in range(NQ)]

        def issue_gather(i):
            b_, q_ = gl[i]
            kb_ = i % 2
            A("pool", IDMA(Kq[kb_][:, :, :].rearrange("p t f -> p (t f)"), ck2, idx[:, b_, q_:q_ + 1]), r=["idx"], w=["Kq%d" % kb_], chan="k_kq%d" % kb_)
            A("pool", IDMA(Vq[kb_][:, :, :].rearrange("p t f -> p (t f)"), cv2, idx[:, b_, q_:q_ + 1]), r=["idx"], w=["Vq%d" % kb_], chan="k_vq%d" % kb_)
        issue_gather(0)
        for b in range(NS):
            qbuf = b % 2
            ret_step(b)
            A("act", ACT(Km[qbuf][:, :], dq[:, :], AF.Copy, scale=cst[0:NS, K_ID + b:K_ID + b + 1]), r=["zs4", "cst"], w=["Km%d" % qbuf])
            A("pe", MM(QB[:, :], ones_f[0:NS, :], Km[qbuf][:, :], start=True, stop=True), r=["ones_f", "Km%d" % qbuf], w=["QB"])
            A("act", ACT(qb[qbuf][:, :], QB[:, :], AF.Copy), r=["QB"], w=["qb%d" % qbuf])
            for q in range(NQ):
                kb = it % 2
                it += 1
                if it < len(gl):
                    issue_gather(it)
                A("dve", TT(prod[kb][:, 0, :], Kq[kb][:, 0, :], qb[qbuf][:, :], ALU.mult), r=["Kq%d" % kb, "qb%d" % qbuf], w=["prod%d" % kb])
                A("dve", TT(prod[kb][:, 1, :], Kq[kb][:, 1, :], qb[qbuf][:, :], ALU.mult), r=["Kq%d" % kb, "qb%d" % qbuf], w=["prod%d" % kb])
                A("dve", RED(S4[kb][:, :], prod[kb][:, :, :].rearrange("p t (g d) -> p (t g) d", d=64)), r=["prod%d" % kb], w=["S4_%d" % kb])
                A("pool", TT(S4b[kb][:, :], S4[kb][:, :], cst2[:, K2_BIAS + 16 * q:K2_BIAS + 16 * q + 16], ALU.add), r=["S4_%d" % kb, "cst2"], w=["S4b_%d" % kb])
                A("act", ACT(Pc[kb][:, :, :], S4b[kb][:, :].rearrange("p (t g) -> p t g", g=8), AF.Exp, scale=0.125), r=["S4b_%d" % kb], w=["Pc%d" % kb])
                if q < 2:
                    A("pool", MSET(Pz[kb][:], 0.0), w=["Pz%d" % kb])
                A("dve", CP(Pz[kb][:].rearrange("p t h (m s) -> p t h m s", s=16)[:, :, :, :, b],
                            Pc[kb][:, :, :].rearrange("p t (h m) -> p t h m", m=2)), r=["Pc%d" % kb], w=["Pz%d" % kb])
                A("dve", TT(Pq[:, q, b, :], Pc[kb][:, 0, :], Pc[kb][:, 1, :], ALU.add), r=["Pc%d" % kb], w=["Pq"])
                A("act", ACT(Vb[kb][:, :, :], Vq[kb][:, :, :], AF.Copy), r=["Vq%d" % kb], w=["Vb%d" % kb])
                last = (b == NS - 1 and q == NQ - 1)
                for t in range(2):
                    for h in range(4):
                        hs = slice(h * 128, (h + 1) * 128)
                        A("pe", MM(Oacc[0:32, hs], Pz[kb][:, t, h, :], Vb[kb][:, t, hs], start=(first and t == 0), stop=(last and t == 1)),
                          r=["Pz%d" % kb, "Vb%d" % kb], w=["Oacc"])
                first = False
        if L.get("dbg") == "pacc":
            A("sp", DMA(y_s.rearrange("b (x c) -> (b x) c", x=8), Pacc[:].rearrange("p b g -> p (b g)")), r=["Pacc"], chan="k_out")
            S.emit(nc, sem_stack=esP)
            return
        o_ret = t16[1]
        for h in range(4):
            hs = slice(h * 128, (h + 1) * 128)
            A("act", ACT(t16[2][:, hs], rv[:, hs], AF.Copy, scale=smc(8 + h)), r=["zs2", "qk"], w=["t2"])
        for h in range(4):
            hs = slice(h * 128, (h + 1) * 128)
            A("dve", STT(o_ret[:, hs], Rps[0:NS, hs], GAM[h], t16[2][:, hs], ALU.mult, ALU.add), r=["Rps", "t2"], w=["t1"])
        A("pool", TT(t16[2][:, :], o_ret[:, :], o_ret[:, :], ALU.mult), r=["t1"], w=["t2"])
        A("dve", RED(smc(12, 4), t16[2][:, :].rearrange("p (h d) -> p h d", d=128)), r=["t2"], w=["ss1"])
        A("act", ACT(smc(12, 4), smc(12, 4), AF.Sqrt, scale=1.0 / 128, bias=epsc), r=["ss1"], w=["ss1"])
        A("dve", RCP(smc(12, 4), smc(12, 4)), r=["ss1"], w=["ss1"])
        for h in range(4):
            hs = slice(h * 128, (h + 1) * 128)
            A("act", ACT(t16[2][:, hs], o_ret[:, hs], AF.Copy, scale=smc(12 + h)), r=["t1", "ss1"], w=["t2"])
        A("dve", TT(ms_bf[:, 0:512], t16[2][:, :], rg[:, :], ALU.mult), r=["t2", "zs3"], w=["ms_bf"])

        O_sb = t16[0]
        A("act", ACT(O_sb[:, :], Oacc[0:32, :], AF.Copy), r=["Oacc"], w=["t0"])
        A("pe", MM(Zs[0][0:NS, :], cst[0:32, K_ID + 16:K_ID + 32], O_sb[:, :], start=True, stop=True), r=["cst", "t0"], w=["Zs0"])
        A("pool", TT(Pacc[:, :, :], Pq[:, 0, :, :], Pq[:, 1, :, :], ALU.add), r=["Pq"], w=["Pacc"])
        for q_ in range(2, NQ):
            A("pool", TT(Pacc[:, :, :], Pacc[:, :, :], Pq[:, q_, :, :], ALU.add), r=["Pq", "Pacc"], w=["Pacc"])
        for b in range(NS):
            A("pe", MM(MISC[0:NS, 64:72], cst2[:, K2_OH + 16 * b:K2_OH + 16 * b + 16], Pacc[:, b, :], start=(b == 0), stop=(b == NS - 1)),
              r=["cst2", "Pacc"], w=["MISC"])
        A("pool", TT(t16[3][:, :], dq[:, :], dk[:, :], ALU.mult), r=["zs4", "zs5"], w=["t3"])
        A("dve", RED(smc(16, 8), t16[3][:, :].rearrange("p (g d) -> p g d", d=64)), r=["t3"], w=["pnew"])
        A("act", ACT(smc(16, 8), smc(16, 8), AF.Exp, scale=0.125), r=["pnew"], w=["pnew"])
        A("act", ACT(smc(24, 8), MISC[0:NS, 64:72], AF.Copy), r=["MISC"], w=["den"])
        A("pool", TT(smc(24, 8), smc(24, 8), smc(16, 8), ALU.add), r=["den", "pnew"], w=["den"])
        A("dve", RCP(smc(24, 8), smc(24, 8)), r=["den"], w=["den"])
        Ot = [t16[3], t16[4]]
        for h in range(4):
            hs = slice(h * 128, (h + 1) * 128)
            A("dve", STT(Ot[0][:, hs], dv[:, hs], smc(16 + 2 * h), O_sb[0:NS, hs], ALU.mult, ALU.add), r=["zs6", "pnew", "t0"], w=["t3"])
            A("dve", STT(Ot[1][:, hs], dv[:, hs], smc(17 + 2 * h), Zs[0][0:NS, hs], ALU.mult, ALU.add), r=["zs6", "pnew", "Zs0"], w=["t4"])
        on = [t16[5], t16[6]]
        for h in range(4):
            hs = slice(h * 128, (h + 1) * 128)
            A("act", ACT(on[0][:, hs], Ot[0][:, hs], AF.Copy, scale=smc(24 + 2 * h)), r=["t3", "den"], w=["t5"])
            A("act", ACT(on[1][:, hs], Ot[1][:, hs], AF.Copy, scale=smc(25 + 2 * h)), r=["t4", "den"], w=["t6"])
        do = t16[1]
        A("dve", STT(do[:, :], on[1][:, :], neglam, on[0][:, :], ALU.mult, ALU.add), r=["t5", "t6"], w=["t1"])
        if L.get("dbg") == "do":
            A("sp", DMA(y_s[:, 0:512], do[:, :]), r=["t1"], chan="k_out")
            A("sp", DMA(y_s[:, 512:520], smc(24, 8)), r=["den"], chan="k_out")
            A("sp", DMA(y_s[:, 520:528], smc(16, 8)), r=["pnew"], chan="k_out")
            S.emit(nc, sem_stack=esP)
            return
        A("pool", TT(t16[3][:, :], do[:, :], do[:, :], ALU.mult), r=["t1"], w=["t3"])
        A("dve", RED(smc(32, 4), t16[3][:, :].rearrange("p (h d) -> p h d", d=128)), r=["t3"], w=["ss2"])
        A("act", ACT(smc(32, 4), smc(32, 4), AF.Sqrt, scale=1.0 / 128, bias=epsc), r=["ss2"], w=["ss2"])
        A("dve", RCP(smc(32, 4), smc(32, 4)), r=["ss2"], w=["ss2"])
        for h in range(4):
            hs = slice(h * 128, (h + 1) * 128)
            A("act", ACT(t16[3][:, hs], do[:, hs], AF.Copy, scale=smc(32 + h)), r=["t1", "ss2"], w=["t3"])
        A("dve", TT(ms_bf[:, 512:1024], t16[3][:, :], gsub_bc[:].rearrange("p h e -> p (h e)"), ALU.mult), r=["t3", "gsub_bc"], w=["ms_bf"])

        if L.get("dbg") == "ms":
            A("pool", DMA(y_s, ms_bf[:, :]), r=["ms_bf"], chan="k_out")
            S.emit(nc, sem_stack=esP)
            return
        msT = hsT
        transpose16(ms_bf, "ms_bf", msT, "hsT", 8)
        for hf in range(2):
            for c in range(8):
                A("pe", MM(Zs[hf][0:NS, :], msT[:, c, :], wo_sb[:, c, hf * 512:(hf + 1) * 512], start=(c == 0), stop=(c == 7)),
                  r=["hsT"], w=["Zs%d" % hf])
        postnorm16([0, 1], gB, "gB", xs_t, "xs_t", 36)

        A("sp", DMA(gA[:], n_fpre.to_broadcast([NS, D])), w=["gA"], chan="k_g2")
        A("sp", DMA(gB[:], n_fpost.to_broadcast([NS, D])), w=["gB"], chan="k_g2")
        rms16(xs_t, "xs_t", gA, "gA", hs_bf, "hs_bf", 38)
        transpose16(hs_bf, "hs_bf", hsT, "hsT", 8)
        wfiv = w_fi.rearrange("(k p) c -> p k c", p=128)
        for i in range(11):
            cs_ = slice(i * 256, (i + 1) * 256)
            if i % 2 == 0:
                gWs, uWs, gk_, uk_, ch_ = wblk[:, :, 0:256], wblk[:, :, 256:512], "wblk", "wblk", "k_w"
            else:
                gWs = Kq[0][:, :, :].rearrange("p t f -> p (t f)").bitcast(BF16).rearrange("p (k c) -> p k c", c=256)
                uWs = Vq[0][:, :, :].rearrange("p t f -> p (t f)").bitcast(BF16).rearrange("p (k c) -> p k c", c=256)
                gk_, uk_, ch_ = "Kq0", "Vq0", "k_w2"
            A("pool", DMA(gWs, wfiv[:, :, i * 256:(i + 1) * 256]), w=[gk_], chan=ch_)
            A("pool", DMA(uWs, wfiv[:, :, DFF + i * 256:DFF + (i + 1) * 256]), w=[uk_], chan=ch_)
            for j in range(3):
                A("sp", DMA(cwb[:, j, :], convw[j:j + 1, cs_].to_broadcast([NS, 256])), w=["Pq"], chan="k_cw")
            A("sp", DMA(cwb[:, 3, :], convb[0:1, cs_].to_broadcast([NS, 256])), w=["Pq"], chan="k_cw")
            A("sp", DMA(scb[:, :, :], sconv[:, :, cs_]), w=["scb"], chan="k_cw")
            for k in range(8):
                A("pe", MM(Zs[0][0:NS, 0:256], hsT[:, k, :], gWs[:, k, :], start=(k == 0), stop=(k == 7)), r=["hsT", gk_], w=["Zs0"])
            for k in range(8):
                A("pe", MM(Zs[1][0:NS, 0:256], hsT[:, k, :], uWs[:, k, :], start=(k == 0), stop=(k == 7)), r=["hsT", uk_], w=["Zs1"])
            gS = cc[0]
            A("act", ACT(gS[:, :], Zs[0][0:NS, 0:256], AF.Copy), r=["Zs0"], w=["cc0"])
            A("sp", DMA(o_cs[:, 1, cs_], gS[:, :]), r=["cc0"], chan="k_gs")
            A("pool", TT(cc[1][:, :], gS[:, :], cwb[:, 2, :], ALU.mult), r=["cc0", "Pq"], w=["cc1"])
            A("pool", TT(cc[1][:, :], cc[1][:, :], cwb[:, 3, :], ALU.add), r=["cc1", "Pq"], w=["cc1"])
            A("pool", TT(cc[2][:, :], scb[:, 0, :], cwb[:, 0, :], ALU.mult), r=["scb", "Pq"], w=["cc2"])
            A("pool", TT(cc[1][:, :], cc[1][:, :], cc[2][:, :], ALU.add), r=["cc1", "cc2"], w=["cc1"])
            A("pool", TT(cc[2][:, :], scb[:, 1, :], cwb[:, 1, :], ALU.mult), r=["scb", "Pq"], w=["cc2"])
            A("pool", TT(cc[1][:, :], cc[1][:, :], cc[2][:, :], ALU.add), r=["cc1", "cc2"], w=["cc1"])
            A("act", ACT(cc[2][:, :], cc[1][:, :], AF.Gelu_apprx_tanh), r=["cc1"], w=["cc2"])
            A("dve", TT(a2_bf[:, cs_], Zs[1][0:NS, 0:256], cc[2][:, :], ALU.mult), r=["Zs1", "cc2"], w=["a2_bf"])
        transpose16(a2_bf, "a2_bf", a2T, "a2T", NFC)
        for hf in range(2):
            for fc in range(NFC):
                A("pe", MM(Zs[hf][0:NS, :], a2T[:, fc, :], wfo_sb[:, fc, hf * 512:(hf + 1) * 512], start=(fc == 0), stop=(fc == NFC - 1)),
                  r=["a2T"], w=["Zs%d" % hf])
        postnorm16([0, 1], gB, "gB", xs_t, "xs_t", 40)
        A("sp", DMA(y_s, xs_t[:, :]), r=["xs_t"], chan="k_out")
        print("n_ops sample", len(S.ops))
        S.emit(nc, sem_stack=esP)

def build_program(n_pool, with_sample=True, limit=None, dbg=None):
    nc = bass.Bass("TRN2", target_bir_lowering=False)
    S = Sched()
    dt = nc.dram_tensor

    def inp(name, shape, dtp=F32):
        return dt(name, list(shape), dtp, kind="ExternalInput").ap()

    def outp(name, shape):
        return dt(name, list(shape), F32, kind="ExternalOutput").ap()

    xp = inp("xp", [T, D])
    w_in = inp("w_in", [D, INW])
    w_o = inp("w_o", [D, D])
    w_fi = inp("w_fi", [D, 2 * DFF])
    w_fo = inp("w_fo", [DFF, D])
    n_mpre = inp("n_mpre", [1, D])
    n_mpost = inp("n_mpost", [1, D])
    n_fpre = inp("n_fpre", [1, D])
    n_fpost = inp("n_fpost", [1, D])
    lq1 = inp("lq1", [1, 64])
    lk1 = inp("lk1", [1, 64])
    lq2 = inp("lq2", [1, 64])
    lk2 = inp("lk2", [1, 64])
    subg = inp("subg", [1, 128])
    convw = inp("convw", [3, DFF])
    convb = inp("convb", [1, DFF])
    consts_d = inp("consts", [128, K_END])
    shift_d = inp("shiftrow", [1, 2048])
    consts2_d = inp("consts2", [128, K2_END])
    xs = inp("xs", [NS, D])
    sret = inp("sret", [NS, 4, 128, 128])
    ck = inp("ck", [n_pool * 128, 512])
    cv = inp("cv", [n_pool * 128, 512])
    sconv = inp("sconv", [NS, 2, DFF])
    ptab = inp("ptab", [NS, 16], I32)

    y_p = outp("y_p", [T, D])
    y_s = outp("y_s", [NS, D])
    o_rsp = outp("o_rsp", [4, 128, 128])
    o_rss = outp("o_rss", [NS, 4, 128, 128])
    o_kp = outp("o_kp", [T, 512])
    o_vp = outp("o_vp", [T, 512])
    o_ks = outp("o_ks", [NS, 512])
    o_vs = outp("o_vs", [NS, 512])
    o_cp = outp("o_cp", [2, DFF])
    o_cs = outp("o_cs", [NS, 2, DFF])
    x1_d = dt("x1_scratch", [T, D], F32, kind="Internal").ap()

    esP = contextlib.ExitStack()
    es = contextlib.ExitStack()
    with esP, es:
        def sbP(name, shape, dtp=F32):
            return esP.enter_context(nc.sbuf_tensor(name, list(shape), dtp))

        def sb(name, shape, dtp=F32):
            return es.enter_context(nc.sbuf_tensor(name, list(shape), dtp))

        def ps(name, shape, dtp=F32):
            return es.enter_context(nc.psum_tensor(name, list(shape), dtp))

        cst = sbP("cst", [128, K_END])
        ident_bf = sbP("ident_bf", [128, 128], BF16)
        ones_bf = sbP("ones_bf", [128, 128], BF16)
        ones_f = sbP("ones_f", [128, 128])
        lamw = sbP("lamw", [128, 8])
        subgc = sbP("subgc", [128, 2])
        cwc = sbP("cwc", [128, NFC, 4])
        wbig = sbP("wbig", [128, 8, INW], BF16)
        wo_sb = sbP("wo_sb", [128, 8, D], BF16)
        scol = sbP("scol", [128, 8])
        shift_bf = sb("shift_bf", [1, 2048], BF16)
        gpre = sb("gpre", [128, D])
        gpost = sb("gpost", [128, D])
        lamt = sb("lamt", [128, 4, 64])
        kT = sb("kT", [128, 4, T], BF16)
        dv_sb = sb("dv_sb", [128, NT, 512], BF16)
        xt = [sb("xt%d" % i, [128, D]) for i in range(2)]
        hbf = [sb("hbf%d" % i, [128, D], BF16) for i in range(2)]
        hT = sb("hT", [128, 8, GT], BF16)
        feat4 = sb("feat4", [128, 4, 4, GT], BF16)
        rqT, rkT, rgT, dqT = feat4[:, 0], feat4[:, 1], feat4[:, 2], feat4[:, 3]
        tok2 = sb("tok2", [128, 2, 4, 512], BF16)
        rv_sb, kd_sb = tok2[:, 0], tok2[:, 1]
        mT = sb("mT", [128, 8, GT], BF16)
        stg = [sb("stg%d" % i, [128, 512]) for i in range(2)]
        PT = [sb("PT%d" % i, [128, 512], BF16) for i in range(3)]
        attm4 = sb("attm4", [128, 4, 128], BF16)
        qd4 = sb("qd4", [128, 4, 128], BF16)
        oT = sb("oT", [128, 4, GT])
        Sf = sb("Sf", [128, 4, 128])
        Sb = sb("Sb", [128, 4, 128], BF16)
        onT = [oT[:, 0, :], oT[:, 1, :]]
        rden = oT[:, 2, :]
        doT = oT[:, 3, :]
        sq = sb("sq", [128, 512])
        rstd = sb("rstd", [128, 512])
        tmpA = sb("tmpA", [128, 512])
        gpv_t = sb("gpv_t", [128, 2, NFC])

        Z = [ps("Z%d" % i, [128, 512]) for i in range(2)]
        SC = [ps("SC%d" % i, [128, 512]) for i in range(2)]
        AV = ps("AV", [128, 512])
        DEN = ps("DEN", [128, 512])
        RT = ps("RT", [128, 512])
        TP = ps("TP", [128, 8, 128], BF16)
        SFK = ["Sf0", "Sf1", "Sf2", "Sf3"]
        SBK = ["Sb0", "Sb1", "Sb2", "Sb3"]

        S.add("sp", DMA(cst[:], consts_d), w=["cst"], chan="c_cst")
        S.add("pool", DMA(shift_bf[:], shift_d), w=["shift_bf"], chan="c_win")
        S.add("sp", DMA(gpre[:], n_mpre.to_broadcast([128, D])), w=["gpre"], chan="c_cst")
        S.add("sp", DMA(gpost[:], n_mpost.to_broadcast([128, D])), w=["gpost"], chan="c_cst")
        for i, d_ in enumerate((lq1, lk1, lq2, lk2)):
            S.add("sp", DMA(lamt[:, i, :], d_.to_broadcast([128, 64])), w=["lamt"], chan="c_cst")
        S.add("sp", DMA(subgc[:, 0:1], subg.rearrange("o e -> e o"), slow=True), w=["subgc"], chan="c_cst")
        for j in range(3):
            S.add("sp", DMA(cwc[:, :, j:j + 1], convw[j:j + 1, :].rearrange("o (c p) -> p c o", p=128), slow=True),
                  w=["cwc"], chan="c_cst")
        S.add("sp", DMA(cwc[:, :, 3:4], convb.rearrange("o (c p) -> p c o", p=128), slow=True), w=["cwc"], chan="c_cst")
        for c in range(8):
            S.add("pool", DMA(wbig[:, c, :], w_in[c * 128:(c + 1) * 128, :]), w=["wbig"], chan="c_win")
        for c in range(8):
            S.add("pool", DMA(wo_sb[:, c, :], w_o[c * 128:(c + 1) * 128, :]), w=["wo_sb"], chan="c_wo")

        S.add("dve", CP(ident_bf[:], cst[:, K_ID:K_ID + 128]), r=["cst"], w=["ident_bf"])
        S.add("dve", MSET(ones_bf[:], 1.0), w=["ones_bf"])
        S.add("dve", MSET(ones_f[:], 1.0), w=["ones_f"])
        S.add("dve", MSET(Sf[:], 0.0), w=SFK)
        S.add("dve", MSET(Sb[:], 0.0), w=SBK)
        S.add("dve", MSET(gpv_t[:], 0.0), w=["gpv"])
        S.add("pool", TT(lamt[:, 0, :], lamt[:, 0, :], lamt[:, 1, :], ALU.mult), r=["lamt"], w=["lamt"])
        S.add("pool", TT(lamt[:, 2, :], lamt[:, 2, :], lamt[:, 3, :], ALU.mult), r=["lamt"], w=["lamt"])
        S.add("dve", RED(lamw[:, 0:1], lamt[:, 0, :]), r=["lamt"], w=["lamw"])
        S.add("dve", RED(lamw[:, 1:2], lamt[:, 2, :]), r=["lamt"], w=["lamw"])
        S.add("act", ACT(lamw[:, 3:5], lamw[:, 0:2], AF.Exp), r=["lamw"], w=["lamw2"])
        S.add("dve", STT(lamw[:, 2:3], lamw[:, 4:5], -LAM_INIT, lamw[:, 3:4], ALU.add, ALU.subtract), r=["lamw2"], w=["neglam"])
        S.add("dve", TS(subgc[:, 1:2], subgc[:, 0:1], 1.0 - LAM_INIT, None, ALU.mult), r=["subgc"], w=["subgc2"])
        neglam = lamw[:, 2:3]
        gsub = subgc[:, 1:2]

        zc = [0]

        def nextZ():
            zc[0] += 1
            return zc[0] % 2

        def sck(c):
            return "scol%d" % c

        def rms_to_bf16(src, src_key, g_tile, g_key, out_bf, out_key, col):
            a, b_ = scol[:, col:col + 1], scol[:, col + 1:col + 2]
            S.add("act", ACT(tmpA[:, 0:512], src[:, 0:512], AF.Square, accum_out=a), r=[src_key], w=["tmpA", sck(col)])
            S.add("act", ACT(tmpA[:, 0:512], src[:, 512:1024], AF.Square, accum_out=b_), r=[src_key], w=["tmpA", sck(col + 1)])
            S.add("dve", TT(a, a, b_, ALU.add), r=[sck(col), sck(col + 1)], w=[sck(col)])
            S.add("act", ACT(a, a, AF.Sqrt, scale=1.0 / D, bias=epsc), r=[sck(col), "epsc"], w=[sck(col)])
            S.add("dve", RCP(a, a), r=[sck(col)], w=[sck(col)])
            for hf in range(2):
                hs = slice(hf * 512, (hf + 1) * 512)
                S.add("act", ACT(tmpA[:, :], src[:, hs], AF.Copy, scale=a), r=[src_key, sck(col)], w=["tmpA"])
                S.add("dve", TT(out_bf[:, hs], tmpA[:, :], g_tile[:, hs], ALU.mult), r=["tmpA", g_key], w=[out_key])

        def post_norm_residual(zb, g_tile, g_key, res, res_key, col):
            a, b_ = scol[:, col:col + 1], scol[:, col + 1:col + 2]
            S.add("act", ACT(tmpA[:, :], zb[0][0][:, :], AF.Square, accum_out=a), r=[zb[0][1]], w=["tmpA", sck(col)])
            S.add("act", ACT(tmpA[:, :], zb[1][0][:, :], AF.Square, accum_out=b_), r=[zb[1][1]], w=["tmpA", sck(col + 1)])
            S.add("dve", TT(a, a, b_, ALU.add), r=[sck(col), sck(col + 1)], w=[sck(col)])
            S.add("act", ACT(a, a, AF.Sqrt, scale=1.0 / D, bias=epsc), r=[sck(col), "epsc"], w=[sck(col)])
            S.add("dve", RCP(a, a), r=[sck(col)], w=[sck(col)])
            for hf in range(2):
                hs = slice(hf * 512, (hf + 1) * 512)
                S.add("act", ACT(tmpA[:, :], zb[hf][0][:, :], AF.Copy, scale=a), r=[zb[hf][1], sck(col)], w=["tmpA"])
                S.add("dve", TT(tmpA[:, :], tmpA[:, :], g_tile[:, hs], ALU.mult), r=["tmpA", g_key], w=["tmpA"])
                S.add("dve", TT(res[:, hs], res[:, hs], tmpA[:, :], ALU.add), r=["tmpA", res_key], w=[res_key])

        ZPAIRS = [[(Z[0], "Z0"), (Z[1], "Z1")], [(SC[0], "SC0"), (SC[1], "SC1")], [(AV, "AV"), (DEN, "DEN")]]
        zpc = [0]

        def next_pair():
            zpc[0] += 1
            return ZPAIRS[zpc[0] % 3]

        def transpose_tile(src_bf, src_key, dstT, dst_key, col0):
            for c in range(8):
                S.add("pe", TR(TP[:, c, :], src_bf[:, c * 128:(c + 1) * 128], ident_bf[:]), r=[src_key, "ident_bf"], w=["TP"])
            S.add("act", ACT(dstT[:, :, col0:col0 + 128], TP[:, :, :], AF.Copy), r=["TP"], w=[dst_key])

        epsc = scol[:, 7:8]
        S.add("dve", MSET(epsc, EPS), w=["epsc"])

        xpv = xp.rearrange("(t p) d -> t p d", p=128)
        x1v = x1_d.rearrange("(t p) d -> t p d", p=128)
        okp = o_kp.rearrange("(t p) d -> t p d", p=128)
        ovp = o_vp.rearrange("(t p) d -> t p d", p=128)
        stc = [0]

        def featproj(col0, eng, fn_of_z, wkeys):
            z = nextZ()
            for k in range(8):
                S.add("pe", MM(Z[z][:, :], wbig[:, k, col0:col0 + 128], hT[:, k, :], start=(k == 0), stop=(k == 7)),
                      r=["wbig", "hT"], w=["Z%d" % z])
            S.add(eng, fn_of_z(Z[z][:, :]), r=["Z%d" % z], w=wkeys)

        def tokproj(tt, col0):
            z = nextZ()
            for k in range(8):
                S.add("pe", MM(Z[z][:, :], hT[:, k, tt * 128:(tt + 1) * 128], wbig[:, k, col0:col0 + 512], start=(k == 0), stop=(k == 7)),
                      r=["wbig", "hT"], w=["Z%d" % z])
            return z

        for g in range(NG):
            for tt in range(4):
                ti = 4 * g + tt
                b = ti % 2
                S.add("sp", DMA(xt[b][:], xpv[ti]), w=["xt%d" % b], chan="c_xt%d" % b)
                rms_to_bf16(xt[b], "xt%d" % b, gpre, "gpre", hbf[b], "hbf%d" % b, 0)
                transpose_tile(hbf[b], "hbf%d" % b, hT, "hT", tt * 128)
            for h in range(4):
                featproj(C_RQ + 128 * h, "act", lambda zz, h=h: ACT(rqT[:, h, :], zz, AF.Copy), ["rqT"])
                featproj(C_RK + 128 * h, "dve", lambda zz, h=h: CP(rkT[:, h, :], zz), ["rkT"])
                featproj(C_RG + 128 * h, "act", lambda zz, h=h: ACT(rgT[:, h, :], zz, AF.Silu), ["rgT"])
                featproj(C_DQ + 128 * h, "dve", lambda zz, h=h: CP(dqT[:, h, :], zz), ["dqT"])
                featproj(C_DK + 128 * h, "act", lambda zz, h=h, g=g: ACT(kT[:, h, g * GT:(g + 1) * GT], zz, AF.Copy), ["kT"])
            for tt in range(4):
                ti = 4 * g + tt
                z = tokproj(tt, C_RV)
                S.add("act", ACT(rv_sb[:, tt, :], Z[z][:, :], AF.Copy), r=["Z%d" % z], w=["rv_sb"])
                z = tokproj(tt, C_RK)
                S.add("dve", TT(kd_sb[:, tt, :], Z[z][:, :], cst[:, K_KDEC:K_KDEC + 512], ALU.mult), r=["Z%d" % z, "cst"], w=["kd_sb"])
                z = tokproj(tt, C_DK)
                si = stc[0] % 2
                stc[0] += 1
                S.add("act", ACT(stg[si][:, :], Z[z][:, :], AF.Copy), r=["Z%d" % z], w=["stg%d" % si])
                S.add("sp", DMA(okp[ti], stg[si][:, :]), r=["stg%d" % si], chan="c_stg%d" % si)
                z = tokproj(tt, C_DV)
                si = stc[0] % 2
                stc[0] += 1
                S.add("act", ACT(stg[si][:, :], Z[z][:, :], AF.Copy), r=["Z%d" % z], w=["stg%d" % si])
                S.add("dve", CP(dv_sb[:, ti, :], Z[z][:, :]), r=["Z%d" % z], w=["dv_sb"])
                S.add("sp", DMA(ovp[ti], stg[si][:, :]), r=["stg%d" % si], chan="c_stg%d" % si)

            for ci in range(4):
                cs = slice(ci * 128, (ci + 1) * 128)
                for h in range(4):
                    hs = slice(128 * h, 128 * (h + 1))
                    S.add("pe", MM(RT[:, hs], rkT[:, h, cs], rqT[:, h, cs]), r=["rkT", "rqT"], w=["RT"])
                for h in range(4):
                    hs = slice(128 * h, 128 * (h + 1))
                    S.add("pe", MM(SC[0][:, hs], kd_sb[:, ci, hs], rv_sb[:, ci, hs]), r=["kd_sb", "rv_sb"], w=["SC0"])
                for h in range(4):
                    hs = slice(128 * h, 128 * (h + 1))
                    S.add("dve", TT(attm4[:, h, :], RT[:, hs], cst[:, K_DECAY + 128 * h:K_DECAY + 128 * (h + 1)], ALU.mult),
                          r=["RT", "cst"], w=["attm"])
                    S.add("pool", TT(qd4[:, h, :], rqT[:, h, cs], cst[:, K_QDEC + 128 * h:K_QDEC + 128 * (h + 1)], ALU.mult),
                          r=["rqT", "cst"], w=["qd"])
                for h in range(4):
                    hs = slice(128 * h, 128 * (h + 1))
                    S.add("pe", MM(AV[:, hs], rv_sb[:, ci, hs], attm4[:, h, :], start=True, stop=False), r=["rv_sb", "attm"], w=["AV"])
                    S.add("pe", MM(AV[:, hs], Sb[:, h, :], qd4[:, h, :], start=False, stop=True), r=SBK + ["qd"], w=["AV"])
                S.add("act", ACT(oT[:, :, cs], AV[:, :].rearrange("p (h t) -> p h t", t=128), AF.Copy), r=["AV"], w=["oT"])
                for h in range(4):
                    hs = slice(128 * h, 128 * (h + 1))
                    S.add("dve", STT(Sf[:, h, :], Sf[:, h, :], GAM[h] ** 128, SC[0][:, hs], ALU.mult, ALU.add), r=["SC0"] + SFK, w=SFK)
                S.add("act", ACT(Sb[:, :, :], Sf[:, :, :], AF.Copy), r=SFK, w=SBK)
            for h in range(4):
                S.add("act", ACT(sq[:, :], oT[:, h, :], AF.Square), r=["oT"], w=["sq"])
                S.add("pe", MM(DEN[:, :], ones_f[:, :], sq[:, :]), r=["ones_f", "sq"], w=["DEN"])
                S.add("act", ACT(rstd[:, :], DEN[:, :], AF.Sqrt, scale=1.0 / 128, bias=epsc), r=["DEN", "epsc"], w=["rstd"])
                S.add("dve", RCP(rstd[:, :], rstd[:, :]), r=["rstd"], w=["rstd"])
                S.add("dve", TT(rstd[:, :], rstd[:, :], rgT[:, h, :], ALU.mult), r=["rstd", "rgT"], w=["rstd"])
                S.add("dve", TT(mT[:, h, :], rstd[:, :], oT[:, h, :], ALU.mult), r=["rstd", "oT"], w=["mT"])

            nkb = 4 * g + 4
            items = [(h, m, kb) for h in range(4) for m in range(2) for kb in range(nkb)]

            SCR = [(SC[0], "SC0"), (SC[1], "SC1"), (Z[0], "Z0"), (Z[1], "Z1")]

            def emit_score(i):
                h, m, kb = items[i]
                rows = slice(64 * m, 64 * (m + 1))
                c0 = 128 * max(0, kb - 4 * g)
                scb, sck_ = SCR[i % 4]
                S.add("pe", MM(scb[:, c0:512], kT[rows, h, kb * 128:(kb + 1) * 128], dqT[rows, h, c0:512], start=True, stop=(h != 0)),
                      r=["kT", "dqT"], w=[sck_])
                if h == 0:
                    S.add("pe", MM(scb[:, c0:512], ones_bf[0:1, :], shift_bf[0:1, 0:512 - c0], start=False, stop=True),
                          r=["ones_bf", "shift_bf"], w=[sck_])
            emit_score(0)
            emit_score(1)
            for i, (h, m, kb) in enumerate(items):
                hs = slice(128 * h, 128 * (h + 1))
                r_ = max(0, kb - 4 * g)
                c0 = 128 * r_
                scb, sck_ = SCR[i % 4]
                pb = i % 3
                bidx = K_ABIAS + 16 * h + (4 * g + (r_ if h == 0 else 0) - kb) + 3
                last = (kb >= 4 * g)
                if i + 2 < len(items):
                    emit_score(i + 2)
                S.add("act", ACT(PT[pb][:, c0:512], scb[:, c0:512], AF.Exp, scale=0.125, bias=cst[:, bidx:bidx + 1]),
                      r=[sck_, "cst"], w=["PT%d" % pb])
                if last:
                    S.add("pool", TT(PT[pb][:, c0:c0 + 128], PT[pb][:, c0:c0 + 128], cst[:, K_CAUS:K_CAUS + 128], ALU.mult),
                          r=["PT%d" % pb, "cst"], w=["PT%d" % pb])
                S.add("pe", MM(AV[:, c0:512], dv_sb[:, kb, hs], PT[pb][:, c0:512], start=(kb == 0), stop=last),
                      r=["dv_sb", "PT%d" % pb], w=["AV"])
                S.add("pe", MM(DEN[:, c0:512], ones_bf[:, :], PT[pb][:, c0:512], start=(kb == 0), stop=last),
                      r=["ones_bf", "PT%d" % pb], w=["DEN"])
                if kb == nkb - 1:
                    S.add("dve", RCP(rden, DEN[:, :]), r=["DEN"], w=["oT"])
                    S.add("dve", TT(onT[m], AV[:, :], rden, ALU.mult), r=["AV", "oT"], w=["oT"])
                    if m == 1:
                        S.add("dve", STT(doT, onT[1], neglam, onT[0], ALU.mult, ALU.add), r=["oT", "neglam"], w=["oT"])
                        S.add("act", ACT(sq[:, :], doT, AF.Square), r=["oT"], w=["sq"])
                        S.add("pe", MM(RT[:, :], ones_f[:, :], sq[:, :]), r=["ones_f", "sq"], w=["RT"])
                        S.add("act", ACT(rstd[:, :], RT[:, :], AF.Sqrt, scale=1.0 / 128, bias=epsc), r=["RT", "epsc"], w=["rstd"])
                        S.add("dve", RCP(rstd[:, :], rstd[:, :]), r=["rstd"], w=["rstd"])
                        S.add("dve", STT(mT[:, 4 + h, :], doT, gsub, rstd[:, :], ALU.mult, ALU.mult), r=["oT", "rstd", "subgc2"], w=["mT"])

            if dbg == "m" and g == 0:
                S.add("pool", DMA(y_p[0:1024, 0:512].rearrange("(c p) t -> p c t", p=128), mT[:, :, :]), r=["mT"], chan="c_misc")
                S.emit(nc, sem_stack=esP)
                return nc
            for tt in range(4):
                ti = 4 * g + tt
                b = ti % 2
                S.add("sp", DMA(xt[b][:], xpv[ti]), w=["xt%d" % b], chan="c_xt%d" % b)
                zb = next_pair()
                for hf in range(2):
                    for c in range(8):
                        S.add("pe", MM(zb[hf][0][:, :], mT[:, c, tt * 128:(tt + 1) * 128], wo_sb[:, c, hf * 512:(hf + 1) * 512], start=(c == 0), stop=(c == 7)),
                              r=["mT", "wo_sb"], w=[zb[hf][1]])
                post_norm_residual(zb, gpost, "gpost", xt[b], "xt%d" % b, 2)
                S.add("sp", DMA(x1v[ti], xt[b][:]), r=["xt%d" % b], w=["x1d%d" % ti], chan="c_xo%d" % b)

        S.add("sp", DMA(o_rsp.rearrange("h d e -> d h e"), Sf[:, :, :]), r=SFK, chan="c_misc")
        if dbg == "x1":
            S.add("sp", DMA(y_p, x1_d), r=["x1d%d" % i for i in range(NT)], chan="c_misc")
            S.emit(nc, sem_stack=esP)
            return nc

        S.add("sp", DMA(gpre[:], n_fpre.to_broadcast([128, D])), w=["gpre"], chan="c_g2")
        S.add("sp", DMA(gpost[:], n_fpost.to_broadcast([128, D])), w=["gpost"], chan="c_g2")
        wfo_sb = wbig.rearrange("p a b -> p (a b)")[:, 0:NFC * D].rearrange("p (c d) -> p c d", d=D)
        for c in range(NFC):
            S.add("pool", DMA(wfo_sb[:, c, :], w_fo[c * 128:(c + 1) * 128, :]), w=["wbig"], chan="c_wfo")
        aT = kT.rearrange("p a b -> p (a b)")
        aT2 = dv_sb.rearrange("p a b -> p (a b)")

        def aTc(fc):
            if fc < 16:
                return aT[:, fc * 512:(fc + 1) * 512], "kT"
            return aT2[:, (fc - 16) * 512:(fc - 15) * 512], "dv_sb"
        WB = [feat4[:, 0:2].rearrange("p a h t -> p (a h t)").rearrange("p (k c) -> p k c", c=512),
              feat4[:, 2:4].rearrange("p a h t -> p (a h t)").rearrange("p (k c) -> p k c", c=512),
              mT[:, :, :],
              tok2.rearrange("p a h t -> p (a h t)").rearrange("p (k c) -> p k c", c=512)]
        WBK = [["rqT", "rkT"], ["rgT", "dqT"], ["mT"], ["rv_sb", "kd_sb"]]
        blkc = [0]
        gsb = [oT[:, 0, :], oT[:, 1, :]]
        c1 = sq
        c2 = rstd
        yv = y_p.rearrange("(t p) d -> t p d", p=128)
        wfiv = w_fi.rearrange("(k p) c -> p k c", p=128)

        for g in range(NG):
            for tt in range(4):
                ti = 4 * g + tt
                b = ti % 2
                S.add("sp", DMA(xt[b][:], x1v[ti]), r=["x1d%d" % ti], w=["xt%d" % b], chan="c_xt%d" % b)
                rms_to_bf16(xt[b], "xt%d" % b, gpre, "gpre", hbf[b], "hbf%d" % b, 0)
                transpose_tile(hbf[b], "hbf%d" % b, hT, "hT", tt * 128)
            for fc in range(NFC):
                if fc % 4 == 0:
                    nfc_blk = min(4, NFC - fc)
                    par = blkc[0] % 2
                    blkc[0] += 1
                    gW, uW = WB[2 * par], WB[2 * par + 1]
                    gWk, uWk = WBK[2 * par], WBK[2 * par + 1]
                    ncol = 128 * nfc_blk
                    S.add("pool", DMA(gW[:, :, 0:ncol], wfiv[:, :, fc * 128:fc * 128 + ncol]), w=gWk, chan="c_wfi%d" % par)
                    S.add("pool", DMA(uW[:, :, 0:ncol], wfiv[:, :, DFF + fc * 128:DFF + fc * 128 + ncol]), w=uWk, chan="c_wfi%d" % par)
                fo = (fc % 4) * 128
                if fc % 2 == 0:
                    G_, U_, gkey, ukey = SC[0], SC[1], "SC0", "SC1"
                else:
                    G_, U_, gkey, ukey = AV, DEN, "AV", "DEN"
                for k in range(8):
                    S.add("pe", MM(G_[:, :], gW[:, k, fo:fo + 128], hT[:, k, :], start=(k == 0), stop=(k == 7)), r=gWk + ["hT"], w=[gkey])
                for k in range(8):
                    S.add("pe", MM(U_[:, :], uW[:, k, fo:fo + 128], hT[:, k, :], start=(k == 0), stop=(k == 7)), r=uWk + ["hT"], w=[ukey])
                gs = gsb[fc % 2]
                gk = "gsb%d" % (fc % 2)
                w0, w1, w2, bb = cwc[:, fc, 0:1], cwc[:, fc, 1:2], cwc[:, fc, 2:3], cwc[:, fc, 3:4]
                S.add("act", ACT(gs[:, :], G_[:, :], AF.Copy), r=[gkey], w=[gk])
                S.add("act", ACT(c1[:, :], G_[:, :], AF.Copy, scale=w2), r=[gkey, "cwc"], w=["c1"])
                S.add("dve", STT(c1[:, 1:512], gs[:, 0:511], w1, c1[:, 1:512], ALU.mult, ALU.add), r=[gk, "c1", "cwc"], w=["c1"])
                S.add("dve", STT(c1[:, 0:1], gpv_t[:, 1:2, fc], w1, c1[:, 0:1], ALU.mult, ALU.add), r=["gpv", "c1", "cwc"], w=["c1"])
                S.add("dve", STT(c1[:, 2:512], gs[:, 0:510], w0, c1[:, 2:512], ALU.mult, ALU.add), r=[gk, "c1", "cwc"], w=["c1"])
                S.add("dve", STT(c1[:, 0:2], gpv_t[:, 0:2, fc], w0, c1[:, 0:2], ALU.mult, ALU.add), r=["gpv", "c1", "cwc"], w=["c1"])
                S.add("dve", CP(gpv_t[:, 0:2, fc], gs[:, 510:512]), r=[gk, "gpv"], w=["gpv"])
                S.add("act", ACT(c2[:, :], c1[:, :], AF.Gelu_apprx_tanh, bias=bb), r=["c1", "cwc"], w=["c2"])
                a_ap, a_key = aTc(fc)
                S.add("dve", TT(a_ap, U_[:, :], c2[:, :], ALU.mult), r=[ukey, "c2"], w=[a_key])
            for tt in range(4):
                ti = 4 * g + tt
                b = ti % 2
                S.add("sp", DMA(xt[b][:], x1v[ti]), r=["x1d%d" % ti], w=["xt%d" % b], chan="c_xt%d" % b)
                zb = next_pair()
                for hf in range(2):
                    for fc in range(NFC):
                        a_ap, a_key = aTc(fc)
                        S.add("pe", MM(zb[hf][0][:, :], a_ap[:, tt * 128:(tt + 1) * 128], wfo_sb[:, fc, hf * 512:(hf + 1) * 512],
                                       start=(fc == 0), stop=(fc == NFC - 1)), r=[a_key, "wbig"], w=[zb[hf][1]])
                post_norm_residual(zb, gpost, "gpost", xt[b], "xt%d" % b, 2)
                S.add("sp", DMA(yv[ti], xt[b][:]), r=["xt%d" % b], chan="c_xo%d" % b)
        for j in range(2):
            S.add("sp", DMA(o_cp[j:j + 1, :].rearrange("o (c p) -> p (o c)", p=128), gpv_t[:, j, :], slow=True), r=["gpv"], chan="c_misc")

        print('n_ops', len(S.ops))
        S.emit(nc, limit=limit, sem_stack=esP)
        es.close()
        if with_sample and limit is None:
            nc.all_engine_barrier()
            build_sample(nc, esP, locals())
    return nc


_CACHE = {}


def _get_program(n_pool):
    if n_pool not in _CACHE:
        _CACHE[n_pool] = build_program(n_pool)
    return _CACHE[n_pool]


def kernel(x_prompt, x_sample, state_ret, cache_k, cache_v, state_conv, page_table,
           norm_mix_pre, norm_mix_post, w_in, w_o, lambda_q1, lambda_k1, lambda_q2, lambda_k2,
           subln_g, norm_ffn_pre, norm_ffn_post, w_ffn_in, conv_w, conv_b, w_ffn_out):
    f = lambda a: np.ascontiguousarray(np.asarray(a, dtype=np.float32))
    x_prompt, x_sample = f(x_prompt), f(x_sample)
    n_pool = int(np.asarray(cache_k).shape[1])
    nc = _get_program(n_pool)
    consts, shiftrow = make_consts()
    ck = f(cache_k)[0].reshape(n_pool * 128, 512)
    cv = f(cache_v)[0].reshape(n_pool * 128, 512)
    pt = np.ascontiguousarray(np.asarray(page_table, dtype=np.int32))
    shared = dict(
        w_in=f(w_in)[0], w_o=f(w_o)[0], w_fi=f(w_ffn_in)[0], w_fo=f(w_ffn_out)[0],
        n_mpre=f(norm_mix_pre), n_mpost=f(norm_mix_post), n_fpre=f(norm_ffn_pre), n_fpost=f(norm_ffn_post),
        lq1=f(lambda_q1), lk1=f(lambda_k1), lq2=f(lambda_q2), lk2=f(lambda_k2), subg=f(subln_g),
        convw=f(conv_w)[0], convb=f(conv_b), consts=consts, shiftrow=shiftrow, ck=ck, cv=cv, consts2=make_consts2(),
    )
    sret = f(state_ret)[0]
    sconv = f(state_conv)[0]
    in_maps = []
    for c in range(8):
        m = dict(shared)
        m["xp"] = x_prompt[c]
        m["xs"] = np.ascontiguousarray(x_sample[c * NS:(c + 1) * NS, 0, :])
        m["sret"] = np.ascontiguousarray(sret[c * NS:(c + 1) * NS])
        m["sconv"] = np.ascontiguousarray(sconv[c * NS:(c + 1) * NS])
        m["ptab"] = np.ascontiguousarray(pt[c * NS:(c + 1) * NS])
        in_maps.append(m)
    res = run_bass_kernel_spmd(nc, in_maps, core_ids=list(range(8)))
    R = res.results
    cat = lambda k: np.stack([np.asarray(r[k]) for r in R], axis=0)
    y_prompt = cat("y_p")
    y_sample = np.concatenate([np.asarray(r["y_s"]) for r in R], axis=0).reshape(128, 1, D)
    rsp = cat("o_rsp")[None]
    rss = np.concatenate([np.asarray(r["o_rss"]) for r in R], axis=0)[None]
    kp = cat("o_kp").reshape(1, 8, T, 4, 2, 64)
    vp = cat("o_vp").reshape(1, 8, T, 4, 128)
    ks = np.concatenate([np.asarray(r["o_ks"]) for r in R], axis=0).reshape(1, 128, 1, 4, 2, 64)
    vs = np.concatenate([np.asarray(r["o_vs"]) for r in R], axis=0).reshape(1, 128, 1, 4, 128)
    cp = cat("o_cp")[None]
    cs = np.concatenate([np.asarray(r["o_cs"]) for r in R], axis=0)[None]
    return (y_prompt, y_sample, rsp, rss, kp, vp, ks, vs, cp, cs)
```

```python
import math
import bisect
import contextlib
import numpy as np
import concourse.bass as bass
import concourse.mybir as mybir
from concourse.bass_utils import run_bass_kernel_spmd

F32 = mybir.dt.float32
BF16 = mybir.dt.bfloat16
I32 = mybir.dt.int32
AF = mybir.ActivationFunctionType
ALU = mybir.AluOpType
AX = mybir.AxisListType

D = 1024
T = 2048
NT = 16
GT = 512
NG = 4
DFF = 2816
NFC = 22
INW = 3584
EPS = 1e-6
NS = 16
H = 4
C_RQ, C_RK, C_RV, C_RG, C_DQ, C_DK, C_DV = 0, 512, 1024, 1536, 2048, 2560, 3072
GAM = [1.0 - 2.0 ** (-5.0 - h) for h in range(4)]
SLOPE = [2.0 ** (-2.0 * (h + 1)) for h in range(4)]
LAM_INIT = 0.8 - 0.6 * math.exp(-0.3 * 0)

K_ID = 0
K_DECAY = 128
K_QDEC = K_DECAY + 512
K_KDEC = K_QDEC + 512
K_CAUS = K_KDEC + 512
K_ABIAS = K_CAUS + 128
K_END = K_ABIAS + 64


def make_consts():
    c = np.zeros((128, K_END), np.float32)
    c[:, K_ID:K_ID + 128] = np.eye(128, dtype=np.float32)
    i = np.arange(128, dtype=np.float64)
    for h in range(4):
        g = GAM[h]
        diff = i[None, :] - i[:, None]
        dec = np.where(diff >= 0, g ** np.maximum(diff, 0), 0.0) * (128 ** -0.5)
        c[:, K_DECAY + 128 * h:K_DECAY + 128 * (h + 1)] = dec
        c[:, K_QDEC + 128 * h:K_QDEC + 128 * (h + 1)] = (g ** (i + 1.0))[None, :]
        c[:, K_KDEC + 128 * h:K_KDEC + 128 * (h + 1)] = ((128 ** -0.5) * g ** (127.0 - i))[:, None]
        for di in range(16):
            dd = di - 3
            c[:, K_ABIAS + 16 * h + di] = SLOPE[h] * (i - 128.0 * dd)
    c[:, K_CAUS:K_CAUS + 128] = (i[None, :] >= i[:, None]).astype(np.float32)
    sh = np.zeros((1, 4 * 512), np.float32)
    tl = np.arange(512, dtype=np.float64)
    for h in range(4):
        sh[0, 512 * h:512 * (h + 1)] = -8.0 * SLOPE[h] * tl
    return c, sh


K2_OH = 0
K2_OFFS = 256
K2_BIAS = 264
K2_Z = K2_BIAS + 128
K2_END = K2_Z + 32
NQ = 8


def make_consts2():
    c = np.zeros((128, K2_END), np.float32)
    for b in range(16):
        c[:, K2_OH + 16 * b + b] = 1.0
    p = np.arange(128)
    for q in range(NQ):
        c[:, K2_OFFS + q] = (p % 8) * 8 + q
        for t in range(2):
            s = (p // 8) * 128 + (p % 8) * 16 + 2 * q + t
            for hh in range(4):
                for m in range(2):
                    c[:, K2_BIAS + q * 16 + t * 8 + hh * 2 + m] = -8.0 * SLOPE[hh] * (2048.0 - s)
    return c


class Op:
    __slots__ = ("eng", "fn", "deps", "chan", "chan_idx", "sig", "seq", "idx", "hard")


class Sched:
    ENGS = ("pe", "act", "dve", "pool", "sp")

    def __init__(self, excl=None, tag=""):
        if excl is not None:
            self.EXCL = tuple(excl)
        self.tag = tag
        self.ops = []
        self.last_w = {}
        self.readers = {}
        self.chan_cnt = {}

    EXCL = ("Z0", "Z1", "SC0", "SC1", "AV", "DEN", "RT", "TP")

    def add(self, eng, fn, r=(), w=(), chan=None, hard=False):
        w = list(w) + [k for k in r if k in self.EXCL]
        r = [k for k in r if k not in self.EXCL]
        op = Op()
        op.eng, op.fn, op.chan, op.sig, op.seq = eng, fn, chan, False, 0
        op.hard = hard
        op.idx = len(self.ops)
        deps = set()
        for k in r:
            lw = self.last_w.get(k)
            if lw is not None:
                deps.add(lw)
        for k in w:
            lw = self.last_w.get(k)
            if lw is not None:
                deps.add(lw)
            for rd in self.readers.get(k, {}).values():
                deps.add(rd)
        rkey = ("c", chan) if chan is not None else eng
        for k in r:
            self.readers.setdefault(k, {})[rkey] = op
        for k in w:
            self.last_w[k] = op
            self.readers[k] = {}
        deps.discard(op)
        op.deps = deps
        if chan is not None:
            self.chan_cnt[chan] = self.chan_cnt.get(chan, 0) + 1
            op.chan_idx = self.chan_cnt[chan]
        else:
            op.chan_idx = 0
        self.ops.append(op)
        return op

    def emit(self, nc, final_eng="sp", limit=None, sem_stack=None):
        ops = self.ops
        if limit is not None:
            ops = ops[:limit]
            self.chan_cnt = {}
            for x in ops:
                if x.chan is not None:
                    self.chan_cnt[x.chan] = self.chan_cnt.get(x.chan, 0) + 1
        def needs_sem(x, d):
            if d.chan is not None:
                return True
            if d.eng == x.eng and x.chan is None and not x.hard:
                return False
            return True
        for x in ops:
            for d in x.deps:
                if needs_sem(x, d) and d.chan is None:
                    d.sig = True
        cnt = {e: 0 for e in self.ENGS}
        for x in ops:
            if x.chan is None and x.sig:
                cnt[x.eng] += 1
                x.seq = cnt[x.eng]
        chans = sorted(self.chan_cnt.keys())
        with contextlib.ExitStack() as es:
            ss = sem_stack if sem_stack is not None else es
            esem = {e: ss.enter_context(nc.semaphore(self.tag + "sem_" + e)) for e in self.ENGS}
            csem = {c: ss.enter_context(nc.semaphore(self.tag + "ch_" + str(c))) for c in chans}
            nc.all_engine_barrier()
            block = es.enter_context(nc.Block())
            per_eng = {e: [x for x in ops if x.eng == e] for e in self.ENGS}
            chan_idx_list = {c: [x.idx for x in ops if x.chan == c] for c in chans}

            def body(e, eng):
                waited = {}
                for x in per_eng[e]:
                    need = {}
                    for d in x.deps:
                        if not needs_sem(x, d):
                            continue
                        if d.chan is not None:
                            key, val = ("c", d.chan), 16 * bisect.bisect_left(chan_idx_list[d.chan], x.idx)
                        else:
                            key, val = ("e", d.eng), d.seq
                        if val > need.get(key, 0):
                            need[key] = val
                    for key, val in need.items():
                        if waited.get(key, 0) >= val:
                            continue
                        waited[key] = val
                        sem = csem[key[1]] if key[0] == "c" else esem[key[1]]
                        eng.wait_ge(sem, val)
                    inst = x.fn(eng)
                    if x.chan is not None:
                        inst.then_inc(csem[x.chan], 16)
                    elif x.sig:
                        inst.then_inc(esem[x.eng], 1)
                if e == final_eng:
                    for c in chans:
                        eng.wait_ge(csem[c], 16 * self.chan_cnt[c])

            @block.tensor
            def _(eng):
                body("pe", eng)

            @block.scalar
            def _(eng):
                body("act", eng)

            @block.vector
            def _(eng):
                body("dve", eng)

            @block.gpsimd
            def _(eng):
                body("pool", eng)

            @block.sync
            def _(eng):
                body("sp", eng)


def MM(out, lhsT, rhs, start=True, stop=True):
    return lambda e: e.matmul(out=out, lhsT=lhsT, rhs=rhs, start=start, stop=stop)


def TR(out, in_, identity):
    return lambda e: e.transpose(out=out, in_=in_, identity=identity)


def ACT(out, in_, func, **kw):
    return lambda e: e.activation(out=out, in_=in_, func=func, **kw)


def TT(out, in0, in1, op):
    return lambda e: e.tensor_tensor(out=out, in0=in0, in1=in1, op=op)


def STT(out, in0, scalar, in1, op0, op1):
    return lambda e: e.scalar_tensor_tensor(out=out, in0=in0, scalar=scalar, in1=in1, op0=op0, op1=op1)


def TS(out, in0, s1, s2, op0, op1=None):
    if op1 is None:
        return lambda e: e.tensor_scalar(out=out, in0=in0, scalar1=s1, scalar2=None, op0=op0)
    return lambda e: e.tensor_scalar(out=out, in0=in0, scalar1=s1, scalar2=s2, op0=op0, op1=op1)


def CP(out, in_):
    return lambda e: e.tensor_copy(out=out, in_=in_)


def RCP(out, in_):
    return lambda e: e.reciprocal(out=out, in_=in_)


def MSET(ap, v):
    return lambda e: e.memset(ap, v)


def RED(out, in_, op=None):
    return lambda e: e.tensor_reduce(out=out, in_=in_, axis=AX.X, op=(op or ALU.add))


def DMA(out, in_, slow=False):
    if slow:
        return lambda e: e.dma_start(out=out, in_=in_, allow_slow_non_contiguous=True)
    return lambda e: e.dma_start(out=out, in_=in_)


def IDMA(out, in_, idx_ap):
    return lambda e: e.indirect_dma_start(out=out, out_offset=None, in_=in_,
                                          in_offset=bass.IndirectOffsetOnAxis(ap=idx_ap, axis=0))


def build_sample(nc, esP, L):
    cst, ident_bf, ones_f, wbig, wo_sb, lamw, subgc, scol = (L[k] for k in
                                                              ("cst", "ident_bf", "ones_f", "wbig", "wo_sb", "lamw", "subgc", "scol"))
    xs, sret, ck, cv, sconv, ptab = (L[k] for k in ("xs", "sret", "ck", "cv", "sconv", "ptab"))
    w_in, w_fi = L["w_in"], L["w_fi"]
    n_mpre, n_mpost, n_fpre, n_fpost, subg, convw, convb = (L[k] for k in
                                                            ("n_mpre", "n_mpost", "n_fpre", "n_fpost", "subg", "convw", "convb"))
    y_s, o_rss, o_ks, o_vs, o_cs = (L[k] for k in ("y_s", "o_rss", "o_ks", "o_vs", "o_cs"))
    consts2_d = L["consts2_d"]
    neglam = lamw[0:NS, 2:3]
    epsc = scol[0:NS, 7:8]
    wfo_sb = wbig.rearrange("p a b -> p (a b)")[:, 0:NFC * D].rearrange("p (c d) -> p c d", d=D)

    banks = ("TPs", "Zs0", "Zs1", "QB", "SN", "Rps", "Oacc", "MISC")
    S = Sched(excl=banks, tag="s2_")
    es = contextlib.ExitStack()
    with es:
        def sb(name, shape, dtp=F32):
            return es.enter_context(nc.sbuf_tensor("s2_" + name, list(shape), dtp))

        def ps(name, shape, dtp=F32):
            return es.enter_context(nc.psum_tensor("s2_" + name, list(shape), dtp))

        cst2 = sb("cst2", [128, K2_END])
        xs_t = sb("xs_t", [NS, D])
        gA = sb("gA", [NS, D])
        gB = sb("gB", [NS, D])
        hs_bf = sb("hs_bf", [NS, D], BF16)
        hsT = sb("hsT", [128, 8, NS], BF16)
        wblk = sb("wblk", [128, 8, 512], BF16)
        zs = [sb("zs%d" % i, [NS, 512]) for i in range(7)]
        rq, rk, rv, rg, dq, dk, dv = zs
        ptsel = sb("ptsel", [128, NS], I32)
        idx = sb("idx", [128, NS, NQ], I32)
        Kq = [sb("Kq%d" % i, [128, 2, 512]) for i in range(2)]
        Vq = [sb("Vq%d" % i, [128, 2, 512]) for i in range(2)]
        prod = [sb("prod%d" % i, [128, 2, 512]) for i in range(2)]
        Vb = [sb("Vb%d" % i, [128, 2, 512], BF16) for i in range(2)]
        qb = [sb("qb%d" % i, [128, 512]) for i in range(2)]
        S4 = [sb("S4_%d" % i, [128, 16]) for i in range(2)]
        S4b = [sb("S4b_%d" % i, [128, 16]) for i in range(2)]
        Pc = [sb("Pc%d" % i, [128, 2, 8]) for i in range(2)]
        Pacc = sb("Pacc", [128, NS, 8])

        Pz = [sb("Pz%d" % i, [128, 2, 4, 32], BF16) for i in range(2)]
        St = [sb("St%d" % i, [128, 4, 128]) for i in range(2)]
        Sn = [sb("Sn%d" % i, [128, 4, 128]) for i in range(2)]
        Qz = sb("Qz", [128, 4, NS, NS])
        rqTs = sb("rqTs", [128, 4, NS])
        Km = [sb("Km%d" % i, [NS, 512]) for i in range(2)]
        Kr = [sb("Kr%d" % i, [NS, 512]) for i in range(2)]
        t16 = [sb("t16_%d" % i, [32 if i == 0 else NS, 512]) for i in range(7)]
        sm = sb("sm", [NS, 64])
        ms_bf = sb("ms_bf", [NS, D], BF16)
        a2_bf = sb("a2_bf", [NS, DFF], BF16)
        a2T = sb("a2T", [128, NFC, NS], BF16)
        arena = sb("arena", [128, 1024])
        Pq = arena[:, :].rearrange("p (q b g) -> p q b g", q=NQ, b=NS)
        cwb = arena[0:NS, :].rearrange("p (j c) -> p j c", j=4)
        scb = sb("scb", [NS, 2, 256])
        cc = [sb("cc%d" % i, [NS, 256]) for i in range(3)]
        gsub_bc = sb("gsub_bc", [NS, 4, 128])
        tmpS = t16[2]

        TPs = ps("TPs", [128, 8, 128], BF16)
        Zs = [ps("Zs%d" % i, [128, 512]) for i in range(2)]
        QB = ps("QB", [128, 512])
        SN = ps("SN", [128, 512])
        Rps = ps("Rps", [128, 512])
        Oacc = ps("Oacc", [128, 512])
        MISC = ps("MISC", [128, 512])

        A = S.add
        A("sp", DMA(cst2[:], consts2_d), w=["cst2"], chan="k_c")
        A("sp", DMA(xs_t[:], xs), w=["xs_t"], chan="k_c")
        A("sp", DMA(gA[:], n_mpre.to_broadcast([NS, D])), w=["gA"], chan="k_c")
        A("sp", DMA(gB[:], n_mpost.to_broadcast([NS, D])), w=["gB"], chan="k_c")
        for h in range(4):
            A("sp", DMA(gsub_bc[:, h, :], subg.to_broadcast([NS, 128])), w=["gsub_bc"], chan="k_c")
        for j in range(16):
            A("sp", DMA(ptsel[8 * j:8 * j + 8, :], ptab[:, j:j + 1].rearrange("b o -> o b").to_broadcast([8, NS]), slow=True),
              w=["ptsel"], chan="k_c")
        A("sp", DMA(o_cs[:, 0, :], sconv[:, 1, :]), chan="k_out")
        A("dve", TS(gsub_bc[:], gsub_bc[:], 1.0 - LAM_INIT, None, ALU.mult), r=["gsub_bc"], w=["gsub_bc"])
        for q in range(NQ):
            A("dve", TS(idx[:, :, q], ptsel[:, :], 64.0, cst2[:, K2_OFFS + q:K2_OFFS + q + 1], ALU.mult, ALU.add),
              r=["ptsel", "cst2"], w=["idx"])

        if L.get("dbg") == "idx":
            A("sp", DMA(y_s.rearrange("b (x c) -> (b x) c", x=8), idx[:].rearrange("p b q -> p (b q)").bitcast(F32)), r=["idx"], chan="k_out")
            S.emit(nc, sem_stack=esP)
            return

        def smc(c, n=1):
            return sm[:, c:c + n]

        def rms16(src, src_key, g_tile, g_key, out_ap, out_key, col):
            a = smc(col)
            A("act", ACT(tmpS[:, :], src[:, 0:512], AF.Square, accum_out=smc(col)), r=[src_key], w=["t2", "sm%d" % col])
            A("act", ACT(tmpS[:, :], src[:, 512:1024], AF.Square, accum_out=smc(col + 1)), r=[src_key], w=["t2", "sm%d" % (col + 1)])
            A("dve", TT(a, a, smc(col + 1), ALU.add), r=["sm%d" % col, "sm%d" % (col + 1)], w=["sm%d" % col])
            A("act", ACT(a, a, AF.Sqrt, scale=1.0 / D, bias=epsc), r=["sm%d" % col], w=["sm%d" % col])
            A("dve", RCP(a, a), r=["sm%d" % col], w=["sm%d" % col])
            for hf in range(2):
                hs = slice(hf * 512, (hf + 1) * 512)
                A("act", ACT(tmpS[:, :], src[:, hs], AF.Copy, scale=a), r=[src_key, "sm%d" % col], w=["t2"])
                A("dve", TT(out_ap[:, hs], tmpS[:, :], g_tile[:, hs], ALU.mult), r=["t2", g_key], w=[out_key])

        def transpose16(src_bf, src_key, dstT, dst_key, nchunk):
            for c0 in range(0, nchunk, 8):
                n = min(8, nchunk - c0)
                for c in range(n):
                    A("pe", TR(TPs[:, c, 0:NS], src_bf[:, (c0 + c) * 128:(c0 + c + 1) * 128], ident_bf[0:NS, 0:NS]),
                      r=[src_key, "ident_bf"], w=["TPs"])
                A("act", ACT(dstT[:, c0:c0 + n, :], TPs[:, 0:n, 0:NS], AF.Copy), r=["TPs"], w=[dst_key])

        def postnorm16(zb, g_tile, g_key, res, res_key, col):
            a = smc(col)
            A("act", ACT(tmpS[:, :], Zs[zb[0]][0:NS, :], AF.Square, accum_out=smc(col)), r=["Zs%d" % zb[0]], w=["t2", "sm%d" % col])
            A("act", ACT(tmpS[:, :], Zs[zb[1]][0:NS, :], AF.Square, accum_out=smc(col + 1)), r=["Zs%d" % zb[1]], w=["t2", "sm%d" % (col + 1)])
            A("dve", TT(a, a, smc(col + 1), ALU.add), r=["sm%d" % col, "sm%d" % (col + 1)], w=["sm%d" % col])
            A("act", ACT(a, a, AF.Sqrt, scale=1.0 / D, bias=epsc), r=["sm%d" % col], w=["sm%d" % col])
            A("dve", RCP(a, a), r=["sm%d" % col], w=["sm%d" % col])
            for hf in range(2):
                hs = slice(hf * 512, (hf + 1) * 512)
                A("act", ACT(tmpS[:, :], Zs[zb[hf]][0:NS, :], AF.Copy, scale=a), r=["Zs%d" % zb[hf], "sm%d" % col], w=["t2"])
                A("pool", TT(tmpS[:, :], tmpS[:, :], g_tile[:, hs], ALU.mult), r=["t2", g_key], w=["t2"])
                A("dve", TT(res[:, hs], res[:, hs], tmpS[:, :], ALU.add), r=["t2", res_key], w=[res_key])

        rms16(xs_t, "xs_t", gA, "gA", hs_bf, "hs_bf", 0)
        transpose16(hs_bf, "hs_bf", hsT, "hsT", 8)
        winv = w_in.rearrange("(k p) c -> p k c", p=128)
        for blk in range(7):
            A("pool", DMA(wblk[:, :, :], winv[:, :, blk * 512:(blk + 1) * 512]), w=["wblk"], chan="k_w")
            z = blk % 2
            for k in range(8):
                A("pe", MM(Zs[z][0:NS, :], hsT[:, k, :], wblk[:, k, :], start=(k == 0), stop=(k == 7)), r=["hsT", "wblk"], w=["Zs%d" % z])
            if blk == 3:
                A("act", ACT(zs[blk][:, :], Zs[z][0:NS, :], AF.Silu), r=["Zs%d" % z], w=["zs%d" % blk])
            else:
                A("act", ACT(zs[blk][:, :], Zs[z][0:NS, :], AF.Copy), r=["Zs%d" % z], w=["zs%d" % blk])
        A("sp", DMA(o_ks, dk[:, :]), r=["zs5"], chan="k_out")
        A("sp", DMA(o_vs, dv[:, :]), r=["zs6"], chan="k_out")

        A("act", ACT(rk[:, :], rk[:, :], AF.Copy, scale=128 ** -0.5), r=["zs1"], w=["zs1"])
        A("pool", TT(t16[1][:, :], rq[:, :], rk[:, :], ALU.mult), r=["zs0", "zs1"], w=["t1"])
        A("dve", RED(smc(8, 4), t16[1][:, :].rearrange("p (h d) -> p h d", d=128)), r=["t1"], w=["qk"])
        for h in range(4):
            A("pe", TR(MISC[:, h * NS:(h + 1) * NS], rq[:, h * 128:(h + 1) * 128], cst[0:NS, K_ID:K_ID + NS]), r=["zs0", "cst"], w=["MISC"])
        A("act", ACT(rqTs[:, :, :], MISC[:, 0:4 * NS].rearrange("p (h b) -> p h b", b=NS), AF.Copy), r=["MISC"], w=["rqTs"])
        for h in range(4):
            for b in range(NS):
                A("dve", TS(Qz[:, h, b, :], cst2[:, K2_OH + 16 * b:K2_OH + 16 * b + 16], rqTs[:, h, b:b + 1], None, ALU.mult),
                  r=["cst2", "rqTs"], w=["Qz"])
        A("pe", MM(Rps[0:NS, :], cst2[:, K2_Z:K2_Z + NS], cst[:, 0:512], start=True, stop=False), r=["cst2", "cst"], w=["Rps"])
        srv = sret.rearrange("b h d e -> b d h e")
        orv = o_rss.rearrange("b h d e -> b d h e")
        def ret_step(b):
            sbuf = b % 2
            A("sp", DMA(St[sbuf][:, :, :], srv[b]), w=["St%d" % sbuf], chan="k_st%d" % sbuf)
            for h in range(4):
                A("pe", MM(Rps[0:NS, h * 128:(h + 1) * 128], Qz[:, h, b, :], St[sbuf][:, h, :], start=False, stop=(b == NS - 1)),
                  r=["Qz", "St%d" % sbuf], w=["Rps"])
            A("act", ACT(Kr[sbuf][:, :], rk[:, :], AF.Copy, scale=cst[0:NS, K_ID + b:K_ID + b + 1]), r=["zs1", "cst"], w=["Kr%d" % sbuf])
            for h in range(4):
                hs = slice(h * 128, (h + 1) * 128)
                A("pe", MM(SN[:, hs], Kr[sbuf][:, hs], rv[:, hs], start=True, stop=True), r=["Kr%d" % sbuf, "zs2"], w=["SN"])
            for h in range(4):
                hs = slice(h * 128, (h + 1) * 128)
                A("dve", STT(Sn[sbuf][:, h, :], St[sbuf][:, h, :], GAM[h], SN[:, hs], ALU.mult, ALU.add),
                  r=["St%d" % sbuf, "SN"], w=["Sn%d" % sbuf])
            A("sp", DMA(orv[b], Sn[sbuf][:, :, :]), r=["Sn%d" % sbuf], chan="k_sn%d" % sbuf)
        ck2 = ck.rearrange("(r t) f -> r (t f)", t=2)
        cv2 = cv.rearrange("(r t) f -> r (t f)", t=2)
        A("pe", MM(Oacc[0:32, :], cst2[:, K2_Z:K2_Z + 32], cst[:, 0:512], start=True, stop=False), r=["cst2", "cst"], w=["Oacc"])

        first = False
        it = 0
        gl = [(b_, q_) for b_ in range(NS) for q_ in range(NQ)]

        def issue_gather(i):
            b_, q_ = gl[i]
            kb_ = i % 2
            A("pool", IDMA(Kq[kb_][:, :, :].rearrange("p t f -> p (t f)"), ck2, idx[:, b_, q_:q_ + 1]), r=["idx"], w=["Kq%d" % kb_], chan="k_kq%d" % kb_)
            A("pool", IDMA(Vq[kb_][:, :, :].rearrange("p t f -> p (t f)"), cv2, idx[:, b_, q_:q_ + 1]), r=["idx"], w=["Vq%d" % kb_], chan="k_vq%d" % kb_)
        issue_gather(0)
        for b in range(NS):
            qbuf = b % 2
            ret_step(b)
            A("act", ACT(Km[qbuf][:, :], dq[:, :], AF.Copy, scale=cst[0:NS, K_ID + b:K_ID + b + 1]), r=["zs4", "cst"], w=["Km%d" % qbuf])
            A("pe", MM(QB[:, :], ones_f[0:NS, :], Km[qbuf][:, :], start=True, stop=True), r=["ones_f", "Km%d" % qbuf], w=["QB"])
            A("act", ACT(qb[qbuf][:, :], QB[:, :], AF.Copy), r=["QB"], w=["qb%d" % qbuf])
            for q in range(NQ):
                kb = it % 2
                it += 1
                if it < len(gl):
                    issue_gather(it)
                A("dve", TT(prod[kb][:, 0, :], Kq[kb][:, 0, :], qb[qbuf][:, :], ALU.mult), r=["Kq%d" % kb, "qb%d" % qbuf], w=["prod%d" % kb])
                A("dve", TT(prod[kb][:, 1, :], Kq[kb][:, 1, :], qb[qbuf][:, :], ALU.mult), r=["Kq%d" % kb, "qb%d" % qbuf], w=["prod%d" % kb])
                A("dve", RED(S4[kb][:, :], prod[kb][:, :, :].rearrange("p t (g d) -> p (t g) d", d=64)), r=["prod%d" % kb], w=["S4_%d" % kb])
                A("pool", TT(S4b[kb][:, :], S4[kb][:, :], cst2[:, K2_BIAS + 16 * q:K2_BIAS + 16 * q + 16], ALU.add), r=["S4_%d" % kb, "cst2"], w=["S4b_%d" % kb])
                A("act", ACT(Pc[kb][:, :, :], S4b[kb][:, :].rearrange("p (t g) -> p t g", g=8), AF.Exp, scale=0.125), r=["S4b_%d" % kb], w=["Pc%d" % kb])
                if q < 2:
                    A("pool", MSET(Pz[kb][:], 0.0), w=["Pz%d" % kb])
                A("dve", CP(Pz[kb][:].rearrange("p t h (m s) -> p t h m s", s=16)[:, :, :, :, b],
                            Pc[kb][:, :, :].rearrange("p t (h m) -> p t h m", m=2)), r=["Pc%d" % kb], w=["Pz%d" % kb])
                A("dve", TT(Pq[:, q, b, :], Pc[kb][:, 0, :], Pc[kb][:, 1, :], ALU.add), r=["Pc%d" % kb], w=["Pq"])
                A("act", ACT(Vb[kb][:, :, :], Vq[kb][:, :, :], AF.Copy), r=["Vq%d" % kb], w=["Vb%d" % kb])
                last = (b == NS - 1 and q == NQ - 1)
                for t in range(2):
                    for h in range(4):
                        hs = slice(h * 128, (h + 1) * 128)
                        A("pe", MM(Oacc[0:32, hs], Pz[kb][:, t, h, :], Vb[kb][:, t, hs], start=(first and t == 0), stop=(last and t == 1)),
                          r=["Pz%d" % kb, "Vb%d" % kb], w=["Oacc"])
                first = False
        if L.get("dbg") == "pacc":
            A("sp", DMA(y_s.rearrange("b (x c) -> (b x) c", x=8), Pacc[:].rearrange("p b g -> p (b g)")), r=["Pacc"], chan="k_out")
            S.emit(nc, sem_stack=esP)
            return
        o_ret = t16[1]
        for h in range(4):
            hs = slice(h * 128, (h + 1) * 128)
            A("act", ACT(t16[2][:, hs], rv[:, hs], AF.Copy, scale=smc(8 + h)), r=["zs2", "qk"], w=["t2"])
        for h in range(4):
            hs = slice(h * 128, (h + 1) * 128)
            A("dve", STT(o_ret[:, hs], Rps[0:NS, hs], GAM[h], t16[2][:, hs], ALU.mult, ALU.add), r=["Rps", "t2"], w=["t1"])
        A("pool", TT(t16[2][:, :], o_ret[:, :], o_ret[:, :], ALU.mult), r=["t1"], w=["t2"])
        A("dve", RED(smc(12, 4), t16[2][:, :].rearrange("p (h d) -> p h d", d=128)), r=["t2"], w=["ss1"])
        A("act", ACT(smc(12, 4), smc(12, 4), AF.Sqrt, scale=1.0 / 128, bias=epsc), r=["ss1"], w=["ss1"])
        A("dve", RCP(smc(12, 4), smc(12, 4)), r=["ss1"], w=["ss1"])
        for h in range(4):
            hs = slice(h * 128, (h + 1) * 128)
            A("act", ACT(t16[2][:, hs], o_ret[:, hs], AF.Copy, scale=smc(12 + h)), r=["t1", "ss1"], w=["t2"])
        A("dve", TT(ms_bf[:, 0:512], t16[2][:, :], rg[:, :], ALU.mult), r=["t2", "zs3"], w=["ms_bf"])

        O_sb = t16[0]
        A("act", ACT(O_sb[:, :], Oacc[0:32, :], AF.Copy), r=["Oacc"], w=["t0"])
        A("pe", MM(Zs[0][0:NS, :], cst[0:32, K_ID + 16:K_ID + 32], O_sb[:, :], start=True, stop=True), r=["cst", "t0"], w=["Zs0"])
        A("pool", TT(Pacc[:, :, :], Pq[:, 0, :, :], Pq[:, 1, :, :], ALU.add), r=["Pq"], w=["Pacc"])
        for q_ in range(2, NQ):
            A("pool", TT(Pacc[:, :, :], Pacc[:, :, :], Pq[:, q_, :, :], ALU.add), r=["Pq", "Pacc"], w=["Pacc"])
        for b in range(NS):
            A("pe", MM(MISC[0:NS, 64:72], cst2[:, K2_OH + 16 * b:K2_OH + 16 * b + 16], Pacc[:, b, :], start=(b == 0), stop=(b == NS - 1)),
              r=["cst2", "Pacc"], w=["MISC"])
        A("pool", TT(t16[3][:, :], dq[:, :], dk[:, :], ALU.mult), r=["zs4", "zs5"], w=["t3"])
        A("dve", RED(smc(16, 8), t16[3][:, :].rearrange("p (g d) -> p g d", d=64)), r=["t3"], w=["pnew"])
        A("act", ACT(smc(16, 8), smc(16, 8), AF.Exp, scale=0.125), r=["pnew"], w=["pnew"])
        A("act", ACT(smc(24, 8), MISC[0:NS, 64:72], AF.Copy), r=["MISC"], w=["den"])
        A("pool", TT(smc(24, 8), smc(24, 8), smc(16, 8), ALU.add), r=["den", "pnew"], w=["den"])
        A("dve", RCP(smc(24, 8), smc(24, 8)), r=["den"], w=["den"])
        Ot = [t16[3], t16[4]]
        for h in range(4):
            hs = slice(h * 128, (h + 1) * 128)
            A("dve", STT(Ot[0][:, hs], dv[:, hs], smc(16 + 2 * h), O_sb[0:NS, hs], ALU.mult, ALU.add), r=["zs6", "pnew", "t0"], w=["t3"])
            A("dve", STT(Ot[1][:, hs], dv[:, hs], smc(17 + 2 * h), Zs[0][0:NS, hs], ALU.mult, ALU.add), r=["zs6", "pnew", "Zs0"], w=["t4"])
        on = [t16[5], t16[6]]
        for h in range(4):
            hs = slice(h * 128, (h + 1) * 128)
            A("act", ACT(on[0][:, hs], Ot[0][:, hs], AF.Copy, scale=smc(24 + 2 * h)), r=["t3", "den"], w=["t5"])
            A("act", ACT(on[1][:, hs], Ot[1][:, hs], AF.Copy, scale=smc(25 + 2 * h)), r=["t4", "den"], w=["t6"])
        do = t16[1]
        A("dve", STT(do[:, :], on[1][:, :], neglam, on[0][:, :], ALU.mult, ALU.add), r=["t5", "t6"], w=["t1"])
        if L.get("dbg") == "do":
            A("sp", DMA(y_s[:, 0:512], do[:, :]), r=["t1"], chan="k_out")
            A("sp", DMA(y_s[:, 512:520], smc(24, 8)), r=["den"], chan="k_out")
            A("sp", DMA(y_s[:, 520:528], smc(16, 8)), r=["pnew"], chan="k_out")
            S.emit(nc, sem_stack=esP)
            return
        A("pool", TT(t16[3][:, :], do[:, :], do[:, :], ALU.mult), r=["t1"], w=["t3"])
        A("dve", RED(smc(32, 4), t16[3][:, :].rearrange("p (h d) -> p h d", d=128)), r=["t3"], w=["ss2"])
        A("act", ACT(smc(32, 4), smc(32, 4), AF.Sqrt, scale=1.0 / 128, bias=epsc), r=["ss2"], w=["ss2"])
        A("dve", RCP(smc(32, 4), smc(32, 4)), r=["ss2"], w=["ss2"])
        for h in range(4):
            hs = slice(h * 128, (h + 1) * 128)
            A("act", ACT(t16[3][:, hs], do[:, hs], AF.Copy, scale=smc(32 + h)), r=["t1", "ss2"], w=["t3"])
        A("dve", TT(ms_bf[:, 512:1024], t16[3][:, :], gsub_bc[:].rearrange("p h e -> p (h e)"), ALU.mult), r=["t3", "gsub_bc"], w=["ms_bf"])

        if L.get("dbg") == "ms":
            A("pool", DMA(y_s, ms_bf[:, :]), r=["ms_bf"], chan="k_out")
            S.emit(nc, sem_stack=esP)
            return
        msT = hsT
        transpose16(ms_bf, "ms_bf", msT, "hsT", 8)
        for hf in range(2):
            for c in range(8):
                A("pe", MM(Zs[hf][0:NS, :], msT[:, c, :], wo_sb[:, c, hf * 512:(hf + 1) * 512], start=(c == 0), stop=(c == 7)),
                  r=["hsT"], w=["Zs%d" % hf])
        postnorm16([0, 1], gB, "gB", xs_t, "xs_t", 36)

        A("sp", DMA(gA[:], n_fpre.to_broadcast([NS, D])), w=["gA"], chan="k_g2")
        A("sp", DMA(gB[:], n_fpost.to_broadcast([NS, D])), w=["gB"], chan="k_g2")
        rms16(xs_t, "xs_t", gA, "gA", hs_bf, "hs_bf", 38)
        transpose16(hs_bf, "hs_bf", hsT, "hsT", 8)
        wfiv = w_fi.rearrange("(k p) c -> p k c", p=128)
        for i in range(11):
            cs_ = slice(i * 256, (i + 1) * 256)
            if i % 2 == 0:
                gWs, uWs, gk_, uk_, ch_ = wblk[:, :, 0:256], wblk[:, :, 256:512], "wblk", "wblk", "k_w"
            else:
                gWs = Kq[0][:, :, :].rearrange("p t f -> p (t f)").bitcast(BF16).rearrange("p (k c) -> p k c", c=256)
                uWs = Vq[0][:, :, :].rearrange("p t f -> p (t f)").bitcast(BF16).rearrange("p (k c) -> p k c", c=256)
                gk_, uk_, ch_ = "Kq0", "Vq0", "k_w2"
            A("pool", DMA(gWs, wfiv[:, :, i * 256:(i + 1) * 256]), w=[gk_], chan=ch_)
            A("pool", DMA(uWs, wfiv[:, :, DFF + i * 256:DFF + (i + 1) * 256]), w=[uk_], chan=ch_)
            for j in range(3):
                A("sp", DMA(cwb[:, j, :], convw[j:j + 1, cs_].to_broadcast([NS, 256])), w=["Pq"], chan="k_cw")
            A("sp", DMA(cwb[:, 3, :], convb[0:1, cs_].to_broadcast([NS, 256])), w=["Pq"], chan="k_cw")
            A("sp", DMA(scb[:, :, :], sconv[:, :, cs_]), w=["scb"], chan="k_cw")
            for k in range(8):
                A("pe", MM(Zs[0][0:NS, 0:256], hsT[:, k, :], gWs[:, k, :], start=(k == 0), stop=(k == 7)), r=["hsT", gk_], w=["Zs0"])
            for k in range(8):
                A("pe", MM(Zs[1][0:NS, 0:256], hsT[:, k, :], uWs[:, k, :], start=(k == 0), stop=(k == 7)), r=["hsT", uk_], w=["Zs1"])
            gS = cc[0]
            A("act", ACT(gS[:, :], Zs[0][0:NS, 0:256], AF.Copy), r=["Zs0"], w=["cc0"])
            A("sp", DMA(o_cs[:, 1, cs_], gS[:, :]), r=["cc0"], chan="k_gs")
            A("pool", TT(cc[1][:, :], gS[:, :], cwb[:, 2, :], ALU.mult), r=["cc0", "Pq"], w=["cc1"])
            A("pool", TT(cc[1][:, :], cc[1][:, :], cwb[:, 3, :], ALU.add), r=["cc1", "Pq"], w=["cc1"])
            A("pool", TT(cc[2][:, :], scb[:, 0, :], cwb[:, 0, :], ALU.mult), r=["scb", "Pq"], w=["cc2"])
            A("pool", TT(cc[1][:, :], cc[1][:, :], cc[2][:, :], ALU.add), r=["cc1", "cc2"], w=["cc1"])
            A("pool", TT(cc[2][:, :], scb[:, 1, :], cwb[:, 1, :], ALU.mult), r=["scb", "Pq"], w=["cc2"])
            A("pool", TT(cc[1][:, :], cc[1][:, :], cc[2][:, :], ALU.add), r=["cc1", "cc2"], w=["cc1"])
            A("act", ACT(cc[2][:, :], cc[1][:, :], AF.Gelu_apprx_tanh), r=["cc1"], w=["cc2"])
            A("dve", TT(a2_bf[:, cs_], Zs[1][0:NS, 0:256], cc[2][:, :], ALU.mult), r=["Zs1", "cc2"], w=["a2_bf"])
        transpose16(a2_bf, "a2_bf", a2T, "a2T", NFC)
        for hf in range(2):
            for fc in range(NFC):
                A("pe", MM(Zs[hf][0:NS, :], a2T[:, fc, :], wfo_sb[:, fc, hf * 512:(hf + 1) * 512], start=(fc == 0), stop=(fc == NFC - 1)),
                  r=["a2T"], w=["Zs%d" % hf])
        postnorm16([0, 1], gB, "gB", xs_t, "xs_t", 40)
        A("sp", DMA(y_s, xs_t[:, :]), r=["xs_t"], chan="k_out")
        print("n_ops sample", len(S.ops))
        S.emit(nc, sem_stack=esP)

def build_program(n_pool, with_sample=True, limit=None, dbg=None):
    nc = bass.Bass("TRN2", target_bir_lowering=False)
    S = Sched()
    dt = nc.dram_tensor

    def inp(name, shape, dtp=F32):
        return dt(name, list(shape), dtp, kind="ExternalInput").ap()

    def outp(name, shape):
        return dt(name, list(shape), F32, kind="ExternalOutput").ap()

    xp = inp("xp", [T, D])
    w_in = inp("w_in", [D, INW])
    w_o = inp("w_o", [D, D])
    w_fi = inp("w_fi", [D, 2 * DFF])
    w_fo = inp("w_fo", [DFF, D])
    n_mpre = inp("n_mpre", [1, D])
    n_mpost = inp("n_mpost", [1, D])
    n_fpre = inp("n_fpre", [1, D])
    n_fpost = inp("n_fpost", [1, D])
    lq1 = inp("lq1", [1, 64])
    lk1 = inp("lk1", [1, 64])
    lq2 = inp("lq2", [1, 64])
    lk2 = inp("lk2", [1, 64])
    subg = inp("subg", [1, 128])
    convw = inp("convw", [3, DFF])
    convb = inp("convb", [1, DFF])
    consts_d = inp("consts", [128, K_END])
    shift_d = inp("shiftrow", [1, 2048])
    consts2_d = inp("consts2", [128, K2_END])
    xs = inp("xs", [NS, D])
    sret = inp("sret", [NS, 4, 128, 128])
    ck = inp("ck", [n_pool * 128, 512])
    cv = inp("cv", [n_pool * 128, 512])
    sconv = inp("sconv", [NS, 2, DFF])
    ptab = inp("ptab", [NS, 16], I32)

    y_p = outp("y_p", [T, D])
    y_s = outp("y_s", [NS, D])
    o_rsp = outp("o_rsp", [4, 128, 128])
    o_rss = outp("o_rss", [NS, 4, 128, 128])
    o_kp = outp("o_kp", [T, 512])
    o_vp = outp("o_vp", [T, 512])
    o_ks = outp("o_ks", [NS, 512])
    o_vs = outp("o_vs", [NS, 512])
    o_cp = outp("o_cp", [2, DFF])
    o_cs = outp("o_cs", [NS, 2, DFF])
    x1_d = dt("x1_scratch", [T, D], F32, kind="Internal").ap()

    esP = contextlib.ExitStack()
    es = contextlib.ExitStack()
    with esP, es:
        def sbP(name, shape, dtp=F32):
            return esP.enter_context(nc.sbuf_tensor(name, list(shape), dtp))

        def sb(name, shape, dtp=F32):
            return es.enter_context(nc.sbuf_tensor(name, list(shape), dtp))

        def ps(name, shape, dtp=F32):
            return es.enter_context(nc.psum_tensor(name, list(shape), dtp))

        cst = sbP("cst", [128, K_END])
        ident_bf = sbP("ident_bf", [128, 128], BF16)
        ones_bf = sbP("ones_bf", [128, 128], BF16)
        ones_f = sbP("ones_f", [128, 128])
        lamw = sbP("lamw", [128, 8])
        subgc = sbP("subgc", [128, 2])
        cwc = sbP("cwc", [128, NFC, 4])
        wbig = sbP("wbig", [128, 8, INW], BF16)
        wo_sb = sbP("wo_sb", [128, 8, D], BF16)
        scol = sbP("scol", [128, 8])
        shift_bf = sb("shift_bf", [1, 2048], BF16)
        gpre = sb("gpre", [128, D])
        gpost = sb("gpost", [128, D])
        lamt = sb("lamt", [128, 4, 64])
        kT = sb("kT", [128, 4, T], BF16)
        dv_sb = sb("dv_sb", [128, NT, 512], BF16)
        xt = [sb("xt%d" % i, [128, D]) for i in range(2)]
        hbf = [sb("hbf%d" % i, [128, D], BF16) for i in range(2)]
        hT = sb("hT", [128, 8, GT], BF16)
        feat4 = sb("feat4", [128, 4, 4, GT], BF16)
        rqT, rkT, rgT, dqT = feat4[:, 0], feat4[:, 1], feat4[:, 2], feat4[:, 3]
        tok2 = sb("tok2", [128, 2, 4, 512], BF16)
        rv_sb, kd_sb = tok2[:, 0], tok2[:, 1]
        mT = sb("mT", [128, 8, GT], BF16)
        stg = [sb("stg%d" % i, [128, 512]) for i in range(2)]
        PT = [sb("PT%d" % i, [128, 512], BF16) for i in range(3)]
        attm4 = sb("attm4", [128, 4, 128], BF16)
        qd4 = sb("qd4", [128, 4, 128], BF16)
        oT = sb("oT", [128, 4, GT])
        Sf = sb("Sf", [128, 4, 128])
        Sb = sb("Sb", [128, 4, 128], BF16)
        onT = [oT[:, 0, :], oT[:, 1, :]]
        rden = oT[:, 2, :]
        doT = oT[:, 3, :]
        sq = sb("sq", [128, 512])
        rstd = sb("rstd", [128, 512])
        tmpA = sb("tmpA", [128, 512])
        gpv_t = sb("gpv_t", [128, 2, NFC])

        Z = [ps("Z%d" % i, [128, 512]) for i in range(2)]
        SC = [ps("SC%d" % i, [128, 512]) for i in range(2)]
        AV = ps("AV", [128, 512])
        DEN = ps("DEN", [128, 512])
        RT = ps("RT", [128, 512])
        TP = ps("TP", [128, 8, 128], BF16)
        SFK = ["Sf0", "Sf1", "Sf2", "Sf3"]
        SBK = ["Sb0", "Sb1", "Sb2", "Sb3"]

        S.add("sp", DMA(cst[:], consts_d), w=["cst"], chan="c_cst")
        S.add("pool", DMA(shift_bf[:], shift_d), w=["shift_bf"], chan="c_shift")
        S.add("sp", DMA(gpre[:], n_mpre.to_broadcast([128, D])), w=["gpre"], chan="c_cst")
        S.add("sp", DMA(gpost[:], n_mpost.to_broadcast([128, D])), w=["gpost"], chan="c_cst")
        for i, d_ in enumerate((lq1, lk1, lq2, lk2)):
            S.add("sp", DMA(lamt[:, i, :], d_.to_broadcast([128, 64])), w=["lamt"], chan="c_cst")
        S.add("sp", DMA(subgc[:, 0:1], subg.rearrange("o e -> e o"), slow=True), w=["subgc"], chan="c_cst")
        for j in range(3):
            S.add("sp", DMA(cwc[:, :, j:j + 1], convw[j:j + 1, :].rearrange("o (c p) -> p c o", p=128), slow=True),
                  w=["cwc"], chan="c_cst")
        S.add("sp", DMA(cwc[:, :, 3:4], convb.rearrange("o (c p) -> p c o", p=128), slow=True), w=["cwc"], chan="c_cst")
        WBK7 = ["wbig%d" % j for j in range(7)]
        w_inv = w_in.rearrange("(k p) c -> p k c", p=128)
        for j in (0, 1, 3, 4, 5, 2, 6):
            S.add("pool", DMA(wbig[:, :, j * 512:(j + 1) * 512], w_inv[:, :, j * 512:(j + 1) * 512]), w=[WBK7[j]], chan="c_win%d" % j)
        for c in range(8):
            S.add("pool", DMA(wo_sb[:, c, :], w_o[c * 128:(c + 1) * 128, :]), w=["wo_sb"], chan="c_wo")

        S.add("dve", CP(ident_bf[:], cst[:, K_ID:K_ID + 128]), r=["cst"], w=["ident_bf"])
        S.add("dve", MSET(ones_bf[:], 1.0), w=["ones_bf"])
        S.add("dve", MSET(ones_f[:], 1.0), w=["ones_f"])
        S.add("dve", MSET(Sf[:], 0.0), w=SFK)
        S.add("dve", MSET(Sb[:], 0.0), w=SBK)
        S.add("dve", MSET(gpv_t[:], 0.0), w=["gpv"])
        S.add("pool", TT(lamt[:, 0, :], lamt[:, 0, :], lamt[:, 1, :], ALU.mult), r=["lamt"], w=["lamt"])
        S.add("pool", TT(lamt[:, 2, :], lamt[:, 2, :], lamt[:, 3, :], ALU.mult), r=["lamt"], w=["lamt"])
        S.add("dve", RED(lamw[:, 0:1], lamt[:, 0, :]), r=["lamt"], w=["lamw"])
        S.add("dve", RED(lamw[:, 1:2], lamt[:, 2, :]), r=["lamt"], w=["lamw"])
        S.add("act", ACT(lamw[:, 3:5], lamw[:, 0:2], AF.Exp), r=["lamw"], w=["lamw2"])
        S.add("dve", STT(lamw[:, 2:3], lamw[:, 4:5], -LAM_INIT, lamw[:, 3:4], ALU.add, ALU.subtract), r=["lamw2"], w=["neglam"])
        S.add("dve", TS(subgc[:, 1:2], subgc[:, 0:1], 1.0 - LAM_INIT, None, ALU.mult), r=["subgc"], w=["subgc2"])
        neglam = lamw[:, 2:3]
        gsub = subgc[:, 1:2]

        zc = [0]

        def nextZ():
            zc[0] += 1
            return zc[0] % 2

        def sck(c):
            return "scol%d" % c

        def rms_to_bf16(src, src_key, g_tile, g_key, out_bf, out_key, col):
            a, b_ = scol[:, col:col + 1], scol[:, col + 1:col + 2]
            S.add("act", ACT(tmpA[:, 0:512], src[:, 0:512], AF.Square, accum_out=a), r=[src_key], w=["tmpA", sck(col)])
            S.add("act", ACT(tmpA[:, 0:512], src[:, 512:1024], AF.Square, accum_out=b_), r=[src_key], w=["tmpA", sck(col + 1)])
            S.add("dve", TT(a, a, b_, ALU.add), r=[sck(col), sck(col + 1)], w=[sck(col)])
            S.add("act", ACT(a, a, AF.Sqrt, scale=1.0 / D, bias=epsc), r=[sck(col), "epsc"], w=[sck(col)])
            S.add("dve", RCP(a, a), r=[sck(col)], w=[sck(col)])
            for hf in range(2):
                hs = slice(hf * 512, (hf + 1) * 512)
                S.add("act", ACT(tmpA[:, :], src[:, hs], AF.Copy, scale=a), r=[src_key, sck(col)], w=["tmpA"])
                S.add("dve", TT(out_bf[:, hs], tmpA[:, :], g_tile[:, hs], ALU.mult), r=["tmpA", g_key], w=[out_key])

        def post_norm_residual(zb, g_tile, g_key, res, res_key, col):
            a, b_ = scol[:, col:col + 1], scol[:, col + 1:col + 2]
            S.add("act", ACT(tmpA[:, :], zb[0][0][:, :], AF.Square, accum_out=a), r=[zb[0][1]], w=["tmpA", sck(col)])
            S.add("act", ACT(tmpA[:, :], zb[1][0][:, :], AF.Square, accum_out=b_), r=[zb[1][1]], w=["tmpA", sck(col + 1)])
            S.add("dve", TT(a, a, b_, ALU.add), r=[sck(col), sck(col + 1)], w=[sck(col)])
            S.add("act", ACT(a, a, AF.Sqrt, scale=1.0 / D, bias=epsc), r=[sck(col), "epsc"], w=[sck(col)])
            S.add("dve", RCP(a, a), r=[sck(col)], w=[sck(col)])
            for hf in range(2):
                hs = slice(hf * 512, (hf + 1) * 512)
                S.add("act", ACT(tmpA[:, :], zb[hf][0][:, :], AF.Copy, scale=a), r=[zb[hf][1], sck(col)], w=["tmpA"])
                S.add("dve", TT(tmpA[:, :], tmpA[:, :], g_tile[:, hs], ALU.mult), r=["tmpA", g_key], w=["tmpA"])
                S.add("dve", TT(res[:, hs], res[:, hs], tmpA[:, :], ALU.add), r=["tmpA", res_key], w=[res_key])

        ZPAIRS = [[(Z[0], "Z0"), (Z[1], "Z1")], [(SC[0], "SC0"), (SC[1], "SC1")], [(AV, "AV"), (DEN, "DEN")]]
        zpc = [0]

        def next_pair():
            zpc[0] += 1
            return ZPAIRS[zpc[0] % 3]

        def transpose_tile(src_bf, src_key, dstT, dst_key, col0):
            for c in range(8):
                S.add("pe", TR(TP[:, c, :], src_bf[:, c * 128:(c + 1) * 128], ident_bf[:]), r=[src_key, "ident_bf"], w=["TP"])
            S.add("act", ACT(dstT[:, :, col0:col0 + 128], TP[:, :, :], AF.Copy), r=["TP"], w=[dst_key])

        epsc = scol[:, 7:8]
        S.add("dve", MSET(epsc, EPS), w=["epsc"])

        xpv = xp.rearrange("(t p) d -> t p d", p=128)
        x1v = x1_d.rearrange("(t p) d -> t p d", p=128)
        okp = o_kp.rearrange("(t p) d -> t p d", p=128)
        ovp = o_vp.rearrange("(t p) d -> t p d", p=128)
        stc = [0]

        def featproj(col0, eng, fn_of_z, wkeys):
            z = nextZ()
            for k in range(8):
                S.add("pe", MM(Z[z][:, :], wbig[:, k, col0:col0 + 128], hT[:, k, :], start=(k == 0), stop=(k == 7)),
                      r=[WBK7[col0 // 512], "hT"], w=["Z%d" % z])
            S.add(eng, fn_of_z(Z[z][:, :]), r=["Z%d" % z], w=wkeys)

        def tokproj(tt, col0):
            z = nextZ()
            for k in range(8):
                S.add("pe", MM(Z[z][:, :], hT[:, k, tt * 128:(tt + 1) * 128], wbig[:, k, col0:col0 + 512], start=(k == 0), stop=(k == 7)),
                      r=[WBK7[col0 // 512], "hT"], w=["Z%d" % z])
            return z

        for g in range(NG):
            for tt in range(4):
                ti = 4 * g + tt
                b = ti % 2
                S.add("sp", DMA(xt[b][:], xpv[ti]), w=["xt%d" % b], chan="c_xt%d" % b)
                rms_to_bf16(xt[b], "xt%d" % b, gpre, "gpre", hbf[b], "hbf%d" % b, 0)
                transpose_tile(hbf[b], "hbf%d" % b, hT, "hT", tt * 128)
            for h in range(4):
                featproj(C_RQ + 128 * h, "act", lambda zz, h=h: ACT(rqT[:, h, :], zz, AF.Copy), ["rqT"])
                featproj(C_RK + 128 * h, "dve", lambda zz, h=h: CP(rkT[:, h, :], zz), ["rkT"])
                featproj(C_RG + 128 * h, "act", lambda zz, h=h: ACT(rgT[:, h, :], zz, AF.Silu), ["rgT"])
                featproj(C_DQ + 128 * h, "dve", lambda zz, h=h: CP(dqT[:, h, :], zz), ["dqT"])
                featproj(C_DK + 128 * h, "act", lambda zz, h=h, g=g: ACT(kT[:, h, g * GT:(g + 1) * GT], zz, AF.Copy), ["kT"])
            for tt in range(4):
                ti = 4 * g + tt
                z = tokproj(tt, C_RV)
                S.add("act", ACT(rv_sb[:, tt, :], Z[z][:, :], AF.Copy), r=["Z%d" % z], w=["rv_sb"])
                z = tokproj(tt, C_RK)
                S.add("dve", TT(kd_sb[:, tt, :], Z[z][:, :], cst[:, K_KDEC:K_KDEC + 512], ALU.mult), r=["Z%d" % z, "cst"], w=["kd_sb"])
                z = tokproj(tt, C_DK)
                si = stc[0] % 2
                stc[0] += 1
                S.add("act", ACT(stg[si][:, :], Z[z][:, :], AF.Copy), r=["Z%d" % z], w=["stg%d" % si])
                S.add("sp", DMA(okp[ti], stg[si][:, :]), r=["stg%d" % si], chan="c_stg%d" % si)
                z = tokproj(tt, C_DV)
                si = stc[0] % 2
                stc[0] += 1
                S.add("act", ACT(stg[si][:, :], Z[z][:, :], AF.Copy), r=["Z%d" % z], w=["stg%d" % si])
                S.add("dve", CP(dv_sb[:, ti, :], Z[z][:, :]), r=["Z%d" % z], w=["dv_sb"])
                S.add("sp", DMA(ovp[ti], stg[si][:, :]), r=["stg%d" % si], chan="c_stg%d" % si)

            for ci in range(4):
                cs = slice(ci * 128, (ci + 1) * 128)
                for h in range(4):
                    hs = slice(128 * h, 128 * (h + 1))
                    S.add("pe", MM(RT[:, hs], rkT[:, h, cs], rqT[:, h, cs]), r=["rkT", "rqT"], w=["RT"])
                for h in range(4):
                    hs = slice(128 * h, 128 * (h + 1))
                    S.add("pe", MM(SC[0][:, hs], kd_sb[:, ci, hs], rv_sb[:, ci, hs]), r=["kd_sb", "rv_sb"], w=["SC0"])
                for h in range(4):
                    hs = slice(128 * h, 128 * (h + 1))
                    S.add("dve", TT(attm4[:, h, :], RT[:, hs], cst[:, K_DECAY + 128 * h:K_DECAY + 128 * (h + 1)], ALU.mult),
                          r=["RT", "cst"], w=["attm"])
                    S.add("pool", TT(qd4[:, h, :], rqT[:, h, cs], cst[:, K_QDEC + 128 * h:K_QDEC + 128 * (h + 1)], ALU.mult),
                          r=["rqT", "cst"], w=["qd"])
                for h in range(4):
                    hs = slice(128 * h, 128 * (h + 1))
                    S.add("pe", MM(AV[:, hs], rv_sb[:, ci, hs], attm4[:, h, :], start=True, stop=False), r=["rv_sb", "attm"], w=["AV"])
                    S.add("pe", MM(AV[:, hs], Sb[:, h, :], qd4[:, h, :], start=False, stop=True), r=SBK + ["qd"], w=["AV"])
                S.add("act", ACT(oT[:, :, cs], AV[:, :].rearrange("p (h t) -> p h t", t=128), AF.Copy), r=["AV"], w=["oT"])
                for h in range(4):
                    hs = slice(128 * h, 128 * (h + 1))
                    S.add("dve", STT(Sf[:, h, :], Sf[:, h, :], GAM[h] ** 128, SC[0][:, hs], ALU.mult, ALU.add), r=["SC0"] + SFK, w=SFK)
                S.add("act", ACT(Sb[:, :, :], Sf[:, :, :], AF.Copy), r=SFK, w=SBK)
            for h in range(4):
                S.add("act", ACT(sq[:, :], oT[:, h, :], AF.Square), r=["oT"], w=["sq"])
                S.add("pe", MM(DEN[:, :], ones_f[:, :], sq[:, :]), r=["ones_f", "sq"], w=["DEN"])
                S.add("act", ACT(rstd[:, :], DEN[:, :], AF.Sqrt, scale=1.0 / 128, bias=epsc), r=["DEN", "epsc"], w=["rstd"])
                S.add("dve", RCP(rstd[:, :], rstd[:, :]), r=["rstd"], w=["rstd"])
                S.add("dve", TT(rstd[:, :], rstd[:, :], rgT[:, h, :], ALU.mult), r=["rstd", "rgT"], w=["rstd"])
                S.add("dve", TT(mT[:, h, :], rstd[:, :], oT[:, h, :], ALU.mult), r=["rstd", "oT"], w=["mT"])

            nkb = 4 * g + 4
            items = [(h, m, kb) for h in range(4) for m in range(2) for kb in range(nkb)]

            SCR = [(SC[0], "SC0"), (SC[1], "SC1"), (Z[0], "Z0"), (Z[1], "Z1")]

            def emit_score(i):
                h, m, kb = items[i]
                rows = slice(64 * m, 64 * (m + 1))
                c0 = 128 * max(0, kb - 4 * g)
                scb, sck_ = SCR[i % 4]
                S.add("pe", MM(scb[:, c0:512], kT[rows, h, kb * 128:(kb + 1) * 128], dqT[rows, h, c0:512], start=True, stop=(h != 0)),
                      r=["kT", "dqT"], w=[sck_])
                if h == 0:
                    S.add("pe", MM(scb[:, c0:512], ones_bf[0:1, :], shift_bf[0:1, 0:512 - c0], start=False, stop=True),
                          r=["ones_bf", "shift_bf"], w=[sck_])
            emit_score(0)
            emit_score(1)
            for i, (h, m, kb) in enumerate(items):
                hs = slice(128 * h, 128 * (h + 1))
                r_ = max(0, kb - 4 * g)
                c0 = 128 * r_
                scb, sck_ = SCR[i % 4]
                pb = i % 3
                bidx = K_ABIAS + 16 * h + (4 * g + (r_ if h == 0 else 0) - kb) + 3
                last = (kb >= 4 * g)
                if i + 2 < len(items):
                    emit_score(i + 2)
                S.add("act", ACT(PT[pb][:, c0:512], scb[:, c0:512], AF.Exp, scale=0.125, bias=cst[:, bidx:bidx + 1]),
                      r=[sck_, "cst"], w=["PT%d" % pb])
                if last:
                    S.add("pool", TT(PT[pb][:, c0:c0 + 128], PT[pb][:, c0:c0 + 128], cst[:, K_CAUS:K_CAUS + 128], ALU.mult),
                          r=["PT%d" % pb, "cst"], w=["PT%d" % pb])
                S.add("pe", MM(AV[:, c0:512], dv_sb[:, kb, hs], PT[pb][:, c0:512], start=(kb == 0), stop=last),
                      r=["dv_sb", "PT%d" % pb], w=["AV"])
                S.add("pe", MM(DEN[:, c0:512], ones_bf[:, :], PT[pb][:, c0:512], start=(kb == 0), stop=last),
                      r=["ones_bf", "PT%d" % pb], w=["DEN"])
                if kb == nkb - 1:
                    S.add("dve", RCP(rden, DEN[:, :]), r=["DEN"], w=["oT"])
                    S.add("dve", TT(onT[m], AV[:, :], rden, ALU.mult), r=["AV", "oT"], w=["oT"])
                    if m == 1:
                        S.add("dve", STT(doT, onT[1], neglam, onT[0], ALU.mult, ALU.add), r=["oT", "neglam"], w=["oT"])
                        S.add("act", ACT(sq[:, :], doT, AF.Square), r=["oT"], w=["sq"])
                        S.add("pe", MM(RT[:, :], ones_f[:, :], sq[:, :]), r=["ones_f", "sq"], w=["RT"])
                        S.add("act", ACT(rstd[:, :], RT[:, :], AF.Sqrt, scale=1.0 / 128, bias=epsc), r=["RT", "epsc"], w=["rstd"])
                        S.add("dve", RCP(rstd[:, :], rstd[:, :]), r=["rstd"], w=["rstd"])
                        S.add("dve", STT(mT[:, 4 + h, :], doT, gsub, rstd[:, :], ALU.mult, ALU.mult), r=["oT", "rstd", "subgc2"], w=["mT"])

            if dbg == "m" and g == 0:
                S.add("pool", DMA(y_p[0:1024, 0:512].rearrange("(c p) t -> p c t", p=128), mT[:, :, :]), r=["mT"], chan="c_misc")
                S.emit(nc, sem_stack=esP)
                return nc
            for tt in range(4):
                ti = 4 * g + tt
                b = ti % 2
                S.add("sp", DMA(xt[b][:], xpv[ti]), w=["xt%d" % b], chan="c_xt%d" % b)
                zb = next_pair()
                for hf in range(2):
                    for c in range(8):
                        S.add("pe", MM(zb[hf][0][:, :], mT[:, c, tt * 128:(tt + 1) * 128], wo_sb[:, c, hf * 512:(hf + 1) * 512], start=(c == 0), stop=(c == 7)),
                              r=["mT", "wo_sb"], w=[zb[hf][1]])
                post_norm_residual(zb, gpost, "gpost", xt[b], "xt%d" % b, 2)
                S.add("sp", DMA(x1v[ti], xt[b][:]), r=["xt%d" % b], w=["x1d%d" % ti], chan="c_xo%d" % b)

        S.add("sp", DMA(o_rsp.rearrange("h d e -> d h e"), Sf[:, :, :]), r=SFK, chan="c_misc")
        if dbg == "x1":
            S.add("sp", DMA(y_p, x1_d), r=["x1d%d" % i for i in range(NT)], chan="c_misc")
            S.emit(nc, sem_stack=esP)
            return nc

        S.add("sp", DMA(gpre[:], n_fpre.to_broadcast([128, D])), w=["gpre"], chan="c_g2")
        S.add("sp", DMA(gpost[:], n_fpost.to_broadcast([128, D])), w=["gpost"], chan="c_g2")
        wfo_sb = wbig.rearrange("p a b -> p (a b)")[:, 0:NFC * D].rearrange("p (c d) -> p c d", d=D)
        for c in range(NFC):
            S.add("pool", DMA(wfo_sb[:, c, :], w_fo[c * 128:(c + 1) * 128, :]), w=WBK7, chan="c_wfo")
        aT = kT.rearrange("p a b -> p (a b)")
        aT2 = dv_sb.rearrange("p a b -> p (a b)")

        def aTc(fc):
            if fc < 16:
                return aT[:, fc * 512:(fc + 1) * 512], "kT"
            return aT2[:, (fc - 16) * 512:(fc - 15) * 512], "dv_sb"
        WB = [feat4[:, 0:2].rearrange("p a h t -> p (a h t)").rearrange("p (k c) -> p k c", c=512),
              feat4[:, 2:4].rearrange("p a h t -> p (a h t)").rearrange("p (k c) -> p k c", c=512),
              mT[:, :, :],
              tok2.rearrange("p a h t -> p (a h t)").rearrange("p (k c) -> p k c", c=512)]
        WBK = [["rqT", "rkT"], ["rgT", "dqT"], ["mT"], ["rv_sb", "kd_sb"]]
        blkc = [0]
        gsb = [oT[:, 0, :], oT[:, 1, :]]
        c1 = sq
        c2 = rstd
        yv = y_p.rearrange("(t p) d -> t p d", p=128)
        wfiv = w_fi.rearrange("(k p) c -> p k c", p=128)

        for g in range(NG):
            for tt in range(4):
                ti = 4 * g + tt
                b = ti % 2
                S.add("sp", DMA(xt[b][:], x1v[ti]), r=["x1d%d" % ti], w=["xt%d" % b], chan="c_xt%d" % b)
                rms_to_bf16(xt[b], "xt%d" % b, gpre, "gpre", hbf[b], "hbf%d" % b, 0)
                transpose_tile(hbf[b], "hbf%d" % b, hT, "hT", tt * 128)
            for fc in range(NFC):
                if fc % 4 == 0:
                    nfc_blk = min(4, NFC - fc)
                    par = blkc[0] % 2
                    blkc[0] += 1
                    gW, uW = WB[2 * par], WB[2 * par + 1]
                    gWk, uWk = WBK[2 * par], WBK[2 * par + 1]
                    ncol = 128 * nfc_blk
                    S.add("pool", DMA(gW[:, :, 0:ncol], wfiv[:, :, fc * 128:fc * 128 + ncol]), w=gWk, chan="c_wfi%d" % par)
                    S.add("pool", DMA(uW[:, :, 0:ncol], wfiv[:, :, DFF + fc * 128:DFF + fc * 128 + ncol]), w=uWk, chan="c_wfi%d" % par)
                fo = (fc % 4) * 128
                if fc % 2 == 0:
                    G_, U_, gkey, ukey = SC[0], SC[1], "SC0", "SC1"
                else:
                    G_, U_, gkey, ukey = AV, DEN, "AV", "DEN"
                for k in range(8):
                    S.add("pe", MM(G_[:, :], gW[:, k, fo:fo + 128], hT[:, k, :], start=(k == 0), stop=(k == 7)), r=gWk + ["hT"], w=[gkey])
                for k in range(8):
                    S.add("pe", MM(U_[:, :], uW[:, k, fo:fo + 128], hT[:, k, :], start=(k == 0), stop=(k == 7)), r=uWk + ["hT"], w=[ukey])
                gs = gsb[fc % 2]
                gk = "gsb%d" % (fc % 2)
                w0, w1, w2, bb = cwc[:, fc, 0:1], cwc[:, fc, 1:2], cwc[:, fc, 2:3], cwc[:, fc, 3:4]
                S.add("act", ACT(gs[:, :], G_[:, :], AF.Copy), r=[gkey], w=[gk])
                S.add("act", ACT(c1[:, :], G_[:, :], AF.Copy, scale=w2), r=[gkey, "cwc"], w=["c1"])
                S.add("dve", STT(c1[:, 1:512], gs[:, 0:511], w1, c1[:, 1:512], ALU.mult, ALU.add), r=[gk, "c1", "cwc"], w=["c1"])
                S.add("dve", STT(c1[:, 0:1], gpv_t[:, 1:2, fc], w1, c1[:, 0:1], ALU.mult, ALU.add), r=["gpv", "c1", "cwc"], w=["c1"])
                S.add("dve", STT(c1[:, 2:512], gs[:, 0:510], w0, c1[:, 2:512], ALU.mult, ALU.add), r=[gk, "c1", "cwc"], w=["c1"])
                S.add("dve", STT(c1[:, 0:2], gpv_t[:, 0:2, fc], w0, c1[:, 0:2], ALU.mult, ALU.add), r=["gpv", "c1", "cwc"], w=["c1"])
                S.add("dve", CP(gpv_t[:, 0:2, fc], gs[:, 510:512]), r=[gk, "gpv"], w=["gpv"])
                S.add("act", ACT(c2[:, :], c1[:, :], AF.Gelu_apprx_tanh, bias=bb), r=["c1", "cwc"], w=["c2"])
                a_ap, a_key = aTc(fc)
                S.add("dve", TT(a_ap, U_[:, :], c2[:, :], ALU.mult), r=[ukey, "c2"], w=[a_key])
            for tt in range(4):
                ti = 4 * g + tt
                b = ti % 2
                S.add("sp", DMA(xt[b][:], x1v[ti]), r=["x1d%d" % ti], w=["xt%d" % b], chan="c_xt%d" % b)
                zb = next_pair()
                for hf in range(2):
                    for fc in range(NFC):
                        a_ap, a_key = aTc(fc)
                        S.add("pe", MM(zb[hf][0][:, :], a_ap[:, tt * 128:(tt + 1) * 128], wfo_sb[:, fc, hf * 512:(hf + 1) * 512],
                                       start=(fc == 0), stop=(fc == NFC - 1)), r=[a_key] + WBK7, w=[zb[hf][1]])
                post_norm_residual(zb, gpost, "gpost", xt[b], "xt%d" % b, 2)
                S.add("sp", DMA(yv[ti], xt[b][:]), r=["xt%d" % b], chan="c_xo%d" % b)
        for j in range(2):
            S.add("sp", DMA(o_cp[j:j + 1, :].rearrange("o (c p) -> p (o c)", p=128), gpv_t[:, j, :], slow=True), r=["gpv"], chan="c_misc")

        print('n_ops', len(S.ops))
        S.emit(nc, limit=limit, sem_stack=esP)
        es.close()
        if with_sample and limit is None:
            nc.all_engine_barrier()
            build_sample(nc, esP, locals())
    return nc


_CACHE = {}


def _get_program(n_pool):
    if n_pool not in _CACHE:
        _CACHE[n_pool] = build_program(n_pool)
    return _CACHE[n_pool]


def kernel(x_prompt, x_sample, state_ret, cache_k, cache_v, state_conv, page_table,
           norm_mix_pre, norm_mix_post, w_in, w_o, lambda_q1, lambda_k1, lambda_q2, lambda_k2,
           subln_g, norm_ffn_pre, norm_ffn_post, w_ffn_in, conv_w, conv_b, w_ffn_out):
    f = lambda a: np.ascontiguousarray(np.asarray(a, dtype=np.float32))
    x_prompt, x_sample = f(x_prompt), f(x_sample)
    n_pool = int(np.asarray(cache_k).shape[1])
    nc = _get_program(n_pool)
    consts, shiftrow = make_consts()
    ck = f(cache_k)[0].reshape(n_pool * 128, 512)
    cv = f(cache_v)[0].reshape(n_pool * 128, 512)
    pt = np.ascontiguousarray(np.asarray(page_table, dtype=np.int32))
    shared = dict(
        w_in=f(w_in)[0], w_o=f(w_o)[0], w_fi=f(w_ffn_in)[0], w_fo=f(w_ffn_out)[0],
        n_mpre=f(norm_mix_pre), n_mpost=f(norm_mix_post), n_fpre=f(norm_ffn_pre), n_fpost=f(norm_ffn_post),
        lq1=f(lambda_q1), lk1=f(lambda_k1), lq2=f(lambda_q2), lk2=f(lambda_k2), subg=f(subln_g),
        convw=f(conv_w)[0], convb=f(conv_b), consts=consts, shiftrow=shiftrow, ck=ck, cv=cv, consts2=make_consts2(),
    )
    sret = f(state_ret)[0]
    sconv = f(state_conv)[0]
    in_maps = []
    for c in range(8):
        m = dict(shared)
        m["xp"] = x_prompt[c]
        m["xs"] = np.ascontiguousarray(x_sample[c * NS:(c + 1) * NS, 0, :])
        m["sret"] = np.ascontiguousarray(sret[c * NS:(c + 1) * NS])
        m["sconv"] = np.ascontiguousarray(sconv[c * NS:(c + 1) * NS])
        m["ptab"] = np.ascontiguousarray(pt[c * NS:(c + 1) * NS])
        in_maps.append(m)
    res = run_bass_kernel_spmd(nc, in_maps, core_ids=list(range(8)))
    R = res.results
    cat = lambda k: np.stack([np.asarray(r[k]) for r in R], axis=0)
    y_prompt = cat("y_p")
    y_sample = np.concatenate([np.asarray(r["y_s"]) for r in R], axis=0).reshape(128, 1, D)
    rsp = cat("o_rsp")[None]
    rss = np.concatenate([np.asarray(r["o_rss"]) for r in R], axis=0)[None]
    kp = cat("o_kp").reshape(1, 8, T, 4, 2, 64)
    vp = cat("o_vp").reshape(1, 8, T, 4, 128)
    ks = np.concatenate([np.asarray(r["o_ks"]) for r in R], axis=0).reshape(1, 128, 1, 4, 2, 64)
    vs = np.concatenate([np.asarray(r["o_vs"]) for r in R], axis=0).reshape(1, 128, 1, 4, 128)
    cp = cat("o_cp")[None]
    cs = np.concatenate([np.asarray(r["o_cs"]) for r in R], axis=0)[None]
    return (y_prompt, y_sample, rsp, rss, kp, vp, ks, vs, cp, cs)
```

```python
import math
import bisect
import contextlib
import numpy as np
import concourse.bass as bass
import concourse.mybir as mybir
from concourse.bass_utils import run_bass_kernel_spmd

F32 = mybir.dt.float32
BF16 = mybir.dt.bfloat16
I32 = mybir.dt.int32
AF = mybir.ActivationFunctionType
ALU = mybir.AluOpType
AX = mybir.AxisListType

D = 1024
T = 2048
NT = 16
GT = 512
NG = 4
DFF = 2816
NFC = 22
INW = 3584
EPS = 1e-6
NS = 16
H = 4
C_RQ, C_RK, C_RV, C_RG, C_DQ, C_DK, C_DV = 0, 512, 1024, 1536, 2048, 2560, 3072
GAM = [1.0 - 2.0 ** (-5.0 - h) for h in range(4)]
SLOPE = [2.0 ** (-2.0 * (h + 1)) for h in range(4)]
LAM_INIT = 0.8 - 0.6 * math.exp(-0.3 * 0)

K_ID = 0
K_DECAY = 128
K_QDEC = K_DECAY + 512
K_KDEC = K_QDEC + 512
K_CAUS = K_KDEC + 512
K_ABIAS = K_CAUS + 128
K_END = K_ABIAS + 64


def make_consts():
    c = np.zeros((128, K_END), np.float32)
    c[:, K_ID:K_ID + 128] = np.eye(128, dtype=np.float32)
    i = np.arange(128, dtype=np.float64)
    for h in range(4):
        g = GAM[h]
        diff = i[None, :] - i[:, None]
        dec = np.where(diff >= 0, g ** np.maximum(diff, 0), 0.0) * (128 ** -0.5)
        c[:, K_DECAY + 128 * h:K_DECAY + 128 * (h + 1)] = dec
        c[:, K_QDEC + 128 * h:K_QDEC + 128 * (h + 1)] = (g ** (i + 1.0))[None, :]
        c[:, K_KDEC + 128 * h:K_KDEC + 128 * (h + 1)] = ((128 ** -0.5) * g ** (127.0 - i))[:, None]
        for di in range(16):
            dd = di - 3
            c[:, K_ABIAS + 16 * h + di] = SLOPE[h] * (i - 128.0 * dd)
    c[:, K_CAUS:K_CAUS + 128] = (i[None, :] >= i[:, None]).astype(np.float32)
    sh = np.zeros((1, 4 * 512), np.float32)
    tl = np.arange(512, dtype=np.float64)
    for h in range(4):
        sh[0, 512 * h:512 * (h + 1)] = -8.0 * SLOPE[h] * tl
    return c, sh


K2_OH = 0
K2_OFFS = 256
K2_BIAS = 264
K2_Z = K2_BIAS + 128
K2_END = K2_Z + 32
NQ = 8


def make_consts2():
    c = np.zeros((128, K2_END), np.float32)
    for b in range(16):
        c[:, K2_OH + 16 * b + b] = 1.0
    p = np.arange(128)
    for q in range(NQ):
        c[:, K2_OFFS + q] = (p % 8) * 8 + q
        for t in range(2):
            s = (p // 8) * 128 + (p % 8) * 16 + 2 * q + t
            for hh in range(4):
                for m in range(2):
                    c[:, K2_BIAS + q * 16 + t * 8 + hh * 2 + m] = -8.0 * SLOPE[hh] * (2048.0 - s)
    return c


class Op:
    __slots__ = ("eng", "fn", "deps", "chan", "chan_idx", "sig", "seq", "idx", "hard")


class Sched:
    ENGS = ("pe", "act", "dve", "pool", "sp")

    def __init__(self, excl=None, tag=""):
        if excl is not None:
            self.EXCL = tuple(excl)
        self.tag = tag
        self.ops = []
        self.last_w = {}
        self.readers = {}
        self.chan_cnt = {}

    EXCL = ("Z0", "Z1", "SC0", "SC1", "AV", "DEN", "RT", "TP")

    def add(self, eng, fn, r=(), w=(), chan=None, hard=False):
        w = list(w) + [k for k in r if k in self.EXCL]
        r = [k for k in r if k not in self.EXCL]
        op = Op()
        op.eng, op.fn, op.chan, op.sig, op.seq = eng, fn, chan, False, 0
        op.hard = hard
        op.idx = len(self.ops)
        deps = set()
        for k in r:
            lw = self.last_w.get(k)
            if lw is not None:
                deps.add(lw)
        for k in w:
            lw = self.last_w.get(k)
            if lw is not None:
                deps.add(lw)
            for rd in self.readers.get(k, {}).values():
                deps.add(rd)
        rkey = ("c", chan) if chan is not None else eng
        for k in r:
            self.readers.setdefault(k, {})[rkey] = op
        for k in w:
            self.last_w[k] = op
            self.readers[k] = {}
        deps.discard(op)
        op.deps = deps
        if chan is not None:
            self.chan_cnt[chan] = self.chan_cnt.get(chan, 0) + 1
            op.chan_idx = self.chan_cnt[chan]
        else:
            op.chan_idx = 0
        self.ops.append(op)
        return op

    def emit(self, nc, final_eng="sp", limit=None, sem_stack=None):
        ops = self.ops
        if limit is not None:
            ops = ops[:limit]
            self.chan_cnt = {}
            for x in ops:
                if x.chan is not None:
                    self.chan_cnt[x.chan] = self.chan_cnt.get(x.chan, 0) + 1
        def needs_sem(x, d):
            if d.chan is not None:
                return True
            if d.eng == x.eng and x.chan is None and not x.hard:
                return False
            return True
        for x in ops:
            for d in x.deps:
                if needs_sem(x, d) and d.chan is None:
                    d.sig = True
        cnt = {e: 0 for e in self.ENGS}
        for x in ops:
            if x.chan is None and x.sig:
                cnt[x.eng] += 1
                x.seq = cnt[x.eng]
        chans = sorted(self.chan_cnt.keys())
        with contextlib.ExitStack() as es:
            ss = sem_stack if sem_stack is not None else es
            esem = {e: ss.enter_context(nc.semaphore(self.tag + "sem_" + e)) for e in self.ENGS}
            csem = {c: ss.enter_context(nc.semaphore(self.tag + "ch_" + str(c))) for c in chans}
            nc.all_engine_barrier()
            block = es.enter_context(nc.Block())
            per_eng = {e: [x for x in ops if x.eng == e] for e in self.ENGS}
            chan_idx_list = {c: [x.idx for x in ops if x.chan == c] for c in chans}

            def body(e, eng):
                waited = {}
                for x in per_eng[e]:
                    need = {}
                    for d in x.deps:
                        if not needs_sem(x, d):
                            continue
                        if d.chan is not None:
                            key, val = ("c", d.chan), 16 * bisect.bisect_left(chan_idx_list[d.chan], x.idx)
                        else:
                            key, val = ("e", d.eng), d.seq
                        if val > need.get(key, 0):
                            need[key] = val
                    for key, val in need.items():
                        if waited.get(key, 0) >= val:
                            continue
                        waited[key] = val
                        sem = csem[key[1]] if key[0] == "c" else esem[key[1]]
                        eng.wait_ge(sem, val)
                    inst = x.fn(eng)
                    if x.chan is not None:
                        inst.then_inc(csem[x.chan], 16)
                    elif x.sig:
                        inst.then_inc(esem[x.eng], 1)
                if e == final_eng:
                    for c in chans:
                        eng.wait_ge(csem[c], 16 * self.chan_cnt[c])

            @block.tensor
            def _(eng):
                body("pe", eng)

            @block.scalar
            def _(eng):
                body("act", eng)

            @block.vector
            def _(eng):
                body("dve", eng)

            @block.gpsimd
            def _(eng):
                body("pool", eng)

            @block.sync
            def _(eng):
                body("sp", eng)


def MM(out, lhsT, rhs, start=True, stop=True):
    return lambda e: e.matmul(out=out, lhsT=lhsT, rhs=rhs, start=start, stop=stop)


def TR(out, in_, identity):
    return lambda e: e.transpose(out=out, in_=in_, identity=identity)


def ACT(out, in_, func, **kw):
    return lambda e: e.activation(out=out, in_=in_, func=func, **kw)


def TT(out, in0, in1, op):
    return lambda e: e.tensor_tensor(out=out, in0=in0, in1=in1, op=op)


def STT(out, in0, scalar, in1, op0, op1):
    return lambda e: e.scalar_tensor_tensor(out=out, in0=in0, scalar=scalar, in1=in1, op0=op0, op1=op1)


def TS(out, in0, s1, s2, op0, op1=None):
    if op1 is None:
        return lambda e: e.tensor_scalar(out=out, in0=in0, scalar1=s1, scalar2=None, op0=op0)
    return lambda e: e.tensor_scalar(out=out, in0=in0, scalar1=s1, scalar2=s2, op0=op0, op1=op1)


def CP(out, in_):
    return lambda e: e.tensor_copy(out=out, in_=in_)


def RCP(out, in_):
    return lambda e: e.reciprocal(out=out, in_=in_)


def MSET(ap, v):
    return lambda e: e.memset(ap, v)


def RED(out, in_, op=None):
    return lambda e: e.tensor_reduce(out=out, in_=in_, axis=AX.X, op=(op or ALU.add))


def DMA(out, in_, slow=False):
    if slow:
        return lambda e: e.dma_start(out=out, in_=in_, allow_slow_non_contiguous=True)
    return lambda e: e.dma_start(out=out, in_=in_)


def IDMA(out, in_, idx_ap):
    return lambda e: e.indirect_dma_start(out=out, out_offset=None, in_=in_,
                                          in_offset=bass.IndirectOffsetOnAxis(ap=idx_ap, axis=0))


def build_sample(nc, esP, L):
    cst, ident_bf, ones_f, wbig, wo_sb, lamw, subgc, scol = (L[k] for k in
                                                              ("cst", "ident_bf", "ones_f", "wbig", "wo_sb", "lamw", "subgc", "scol"))
    xs, sret, ck, cv, sconv, ptab = (L[k] for k in ("xs", "sret", "ck", "cv", "sconv", "ptab"))
    w_in, w_fi = L["w_in"], L["w_fi"]
    n_mpre, n_mpost, n_fpre, n_fpost, subg, convw, convb = (L[k] for k in
                                                            ("n_mpre", "n_mpost", "n_fpre", "n_fpost", "subg", "convw", "convb"))
    y_s, o_rss, o_ks, o_vs, o_cs = (L[k] for k in ("y_s", "o_rss", "o_ks", "o_vs", "o_cs"))
    consts2_d = L["consts2_d"]
    neglam = lamw[0:NS, 2:3]
    epsc = scol[0:NS, 7:8]
    wfo_sb = wbig.rearrange("p a b -> p (a b)")[:, 0:NFC * D].rearrange("p (c d) -> p c d", d=D)

    banks = ("TPs", "Zs0", "Zs1", "QB", "SN", "Rps", "Oacc", "MISC")
    S = Sched(excl=banks, tag="s2_")
    es = contextlib.ExitStack()
    with es:
        def sb(name, shape, dtp=F32):
            return es.enter_context(nc.sbuf_tensor("s2_" + name, list(shape), dtp))

        def ps(name, shape, dtp=F32):
            return es.enter_context(nc.psum_tensor("s2_" + name, list(shape), dtp))

        cst2 = sb("cst2", [128, K2_END])
        xs_t = sb("xs_t", [NS, D])
        gA = sb("gA", [NS, D])
        gB = sb("gB", [NS, D])
        hs_bf = sb("hs_bf", [NS, D], BF16)
        hsT = sb("hsT", [128, 8, NS], BF16)
        wblk = sb("wblk", [128, 8, 512], BF16)
        zs = [sb("zs%d" % i, [NS, 512]) for i in range(7)]
        rq, rk, rv, rg, dq, dk, dv = zs
        ptsel = sb("ptsel", [128, NS], I32)
        idx = sb("idx", [128, NS, NQ], I32)
        Kq = [sb("Kq%d" % i, [128, 2, 512]) for i in range(2)]
        Vq = [sb("Vq%d" % i, [128, 2, 512]) for i in range(2)]
        prod = [sb("prod%d" % i, [128, 2, 512]) for i in range(2)]
        Vb = [sb("Vb%d" % i, [128, 2, 512], BF16) for i in range(2)]
        qb = [sb("qb%d" % i, [128, 512]) for i in range(2)]
        S4 = [sb("S4_%d" % i, [128, 16]) for i in range(2)]
        S4b = [sb("S4b_%d" % i, [128, 16]) for i in range(2)]
        Pc = [sb("Pc%d" % i, [128, 2, 8]) for i in range(2)]
        Pacc = sb("Pacc", [128, NS, 8])

        Pz = [sb("Pz%d" % i, [128, 2, 4, 32], BF16) for i in range(2)]
        St = [sb("St%d" % i, [128, 4, 128]) for i in range(2)]
        Sn = [sb("Sn%d" % i, [128, 4, 128]) for i in range(2)]
        Qz = sb("Qz", [128, 4, NS, NS])
        rqTs = sb("rqTs", [128, 4, NS])
        Km = [sb("Km%d" % i, [NS, 512]) for i in range(2)]
        Kr = [sb("Kr%d" % i, [NS, 512]) for i in range(2)]
        t16 = [sb("t16_%d" % i, [32 if i == 0 else NS, 512]) for i in range(7)]
        sm = sb("sm", [NS, 64])
        ms_bf = sb("ms_bf", [NS, D], BF16)
        a2_bf = sb("a2_bf", [NS, DFF], BF16)
        a2T = sb("a2T", [128, NFC, NS], BF16)
        arena = sb("arena", [128, 1024])
        Pq = arena[:, :].rearrange("p (q b g) -> p q b g", q=NQ, b=NS)
        cwb = arena[0:NS, :].rearrange("p (j c) -> p j c", j=4)
        scb = sb("scb", [NS, 2, 256])
        cc = [sb("cc%d" % i, [NS, 256]) for i in range(3)]
        gsub_bc = sb("gsub_bc", [NS, 4, 128])
        tmpS = t16[2]

        TPs = ps("TPs", [128, 8, 128], BF16)
        Zs = [ps("Zs%d" % i, [128, 512]) for i in range(2)]
        QB = ps("QB", [128, 512])
        SN = ps("SN", [128, 512])
        Rps = ps("Rps", [128, 512])
        Oacc = ps("Oacc", [128, 512])
        MISC = ps("MISC", [128, 512])

        A = S.add
        A("sp", DMA(cst2[:], consts2_d), w=["cst2"], chan="k_c")
        A("sp", DMA(xs_t[:], xs), w=["xs_t"], chan="k_c")
        A("sp", DMA(gA[:], n_mpre.to_broadcast([NS, D])), w=["gA"], chan="k_c")
        A("sp", DMA(gB[:], n_mpost.to_broadcast([NS, D])), w=["gB"], chan="k_c")
        for h in range(4):
            A("sp", DMA(gsub_bc[:, h, :], subg.to_broadcast([NS, 128])), w=["gsub_bc"], chan="k_c")
        for j in range(16):
            A("sp", DMA(ptsel[8 * j:8 * j + 8, :], ptab[:, j:j + 1].rearrange("b o -> o b").to_broadcast([8, NS]), slow=True),
              w=["ptsel"], chan="k_c")
        A("sp", DMA(o_cs[:, 0, :], sconv[:, 1, :]), chan="k_out")
        A("dve", TS(gsub_bc[:], gsub_bc[:], 1.0 - LAM_INIT, None, ALU.mult), r=["gsub_bc"], w=["gsub_bc"])
        for q in range(NQ):
            A("dve", TS(idx[:, :, q], ptsel[:, :], 64.0, cst2[:, K2_OFFS + q:K2_OFFS + q + 1], ALU.mult, ALU.add),
              r=["ptsel", "cst2"], w=["idx"])

        if L.get("dbg") == "idx":
            A("sp", DMA(y_s.rearrange("b (x c) -> (b x) c", x=8), idx[:].rearrange("p b q -> p (b q)").bitcast(F32)), r=["idx"], chan="k_out")
            S.emit(nc, sem_stack=esP)
            return

        def smc(c, n=1):
            return sm[:, c:c + n]

        def rms16(src, src_key, g_tile, g_key, out_ap, out_key, col):
            a = smc(col)
            A("act", ACT(tmpS[:, :], src[:, 0:512], AF.Square, accum_out=smc(col)), r=[src_key], w=["t2", "sm%d" % col])
            A("act", ACT(tmpS[:, :], src[:, 512:1024], AF.Square, accum_out=smc(col + 1)), r=[src_key], w=["t2", "sm%d" % (col + 1)])
            A("dve", TT(a, a, smc(col + 1), ALU.add), r=["sm%d" % col, "sm%d" % (col + 1)], w=["sm%d" % col])
            A("act", ACT(a, a, AF.Sqrt, scale=1.0 / D, bias=epsc), r=["sm%d" % col], w=["sm%d" % col])
            A("dve", RCP(a, a), r=["sm%d" % col], w=["sm%d" % col])
            for hf in range(2):
                hs = slice(hf * 512, (hf + 1) * 512)
                A("act", ACT(tmpS[:, :], src[:, hs], AF.Copy, scale=a), r=[src_key, "sm%d" % col], w=["t2"])
                A("dve", TT(out_ap[:, hs], tmpS[:, :], g_tile[:, hs], ALU.mult), r=["t2", g_key], w=[out_key])

        def transpose16(src_bf, src_key, dstT, dst_key, nchunk):
            for c0 in range(0, nchunk, 8):
                n = min(8, nchunk - c0)
                for c in range(n):
                    A("pe", TR(TPs[:, c, 0:NS], src_bf[:, (c0 + c) * 128:(c0 + c + 1) * 128], ident_bf[0:NS, 0:NS]),
                      r=[src_key, "ident_bf"], w=["TPs"])
                A("act", ACT(dstT[:, c0:c0 + n, :], TPs[:, 0:n, 0:NS], AF.Copy), r=["TPs"], w=[dst_key])

        def postnorm16(zb, g_tile, g_key, res, res_key, col):
            a = smc(col)
            A("act", ACT(tmpS[:, :], Zs[zb[0]][0:NS, :], AF.Square, accum_out=smc(col)), r=["Zs%d" % zb[0]], w=["t2", "sm%d" % col])
            A("act", ACT(tmpS[:, :], Zs[zb[1]][0:NS, :], AF.Square, accum_out=smc(col + 1)), r=["Zs%d" % zb[1]], w=["t2", "sm%d" % (col + 1)])
            A("dve", TT(a, a, smc(col + 1), ALU.add), r=["sm%d" % col, "sm%d" % (col + 1)], w=["sm%d" % col])
            A("act", ACT(a, a, AF.Sqrt, scale=1.0 / D, bias=epsc), r=["sm%d" % col], w=["sm%d" % col])
            A("dve", RCP(a, a), r=["sm%d" % col], w=["sm%d" % col])
            for hf in range(2):
                hs = slice(hf * 512, (hf + 1) * 512)
                A("act", ACT(tmpS[:, :], Zs[zb[hf]][0:NS, :], AF.Copy, scale=a), r=["Zs%d" % zb[hf], "sm%d" % col], w=["t2"])
                A("pool", TT(tmpS[:, :], tmpS[:, :], g_tile[:, hs], ALU.mult), r=["t2", g_key], w=["t2"])
                A("dve", TT(res[:, hs], res[:, hs], tmpS[:, :], ALU.add), r=["t2", res_key], w=[res_key])

        rms16(xs_t, "xs_t", gA, "gA", hs_bf, "hs_bf", 0)
        transpose16(hs_bf, "hs_bf", hsT, "hsT", 8)
        winv = w_in.rearrange("(k p) c -> p k c", p=128)
        for blk in range(7):
            A("pool", DMA(wblk[:, :, :], winv[:, :, blk * 512:(blk + 1) * 512]), w=["wblk"], chan="k_w")
            z = blk % 2
            for k in range(8):
                A("pe", MM(Zs[z][0:NS, :], hsT[:, k, :], wblk[:, k, :], start=(k == 0), stop=(k == 7)), r=["hsT", "wblk"], w=["Zs%d" % z])
            if blk == 3:
                A("act", ACT(zs[blk][:, :], Zs[z][0:NS, :], AF.Silu), r=["Zs%d" % z], w=["zs%d" % blk])
            else:
                A("act", ACT(zs[blk][:, :], Zs[z][0:NS, :], AF.Copy), r=["Zs%d" % z], w=["zs%d" % blk])
        A("sp", DMA(o_ks, dk[:, :]), r=["zs5"], chan="k_out")
        A("sp", DMA(o_vs, dv[:, :]), r=["zs6"], chan="k_out")

        A("act", ACT(rk[:, :], rk[:, :], AF.Copy, scale=128 ** -0.5), r=["zs1"], w=["zs1"])
        A("pool", TT(t16[1][:, :], rq[:, :], rk[:, :], ALU.mult), r=["zs0", "zs1"], w=["t1"])
        A("dve", RED(smc(8, 4), t16[1][:, :].rearrange("p (h d) -> p h d", d=128)), r=["t1"], w=["qk"])
        for h in range(4):
            A("pe", TR(MISC[:, h * NS:(h + 1) * NS], rq[:, h * 128:(h + 1) * 128], cst[0:NS, K_ID:K_ID + NS]), r=["zs0", "cst"], w=["MISC"])
        A("act", ACT(rqTs[:, :, :], MISC[:, 0:4 * NS].rearrange("p (h b) -> p h b", b=NS), AF.Copy), r=["MISC"], w=["rqTs"])
        for h in range(4):
            for b in range(NS):
                A("dve", TS(Qz[:, h, b, :], cst2[:, K2_OH + 16 * b:K2_OH + 16 * b + 16], rqTs[:, h, b:b + 1], None, ALU.mult),
                  r=["cst2", "rqTs"], w=["Qz"])
        A("pe", MM(Rps[0:NS, :], cst2[:, K2_Z:K2_Z + NS], cst[:, 0:512], start=True, stop=False), r=["cst2", "cst"], w=["Rps"])
        srv = sret.rearrange("b h d e -> b d h e")
        orv = o_rss.rearrange("b h d e -> b d h e")
        def ret_step(b):
            sbuf = b % 2
            A("sp", DMA(St[sbuf][:, :, :], srv[b]), w=["St%d" % sbuf], chan="k_st%d" % sbuf)
            for h in range(4):
                A("pe", MM(Rps[0:NS, h * 128:(h + 1) * 128], Qz[:, h, b, :], St[sbuf][:, h, :], start=False, stop=(b == NS - 1)),
                  r=["Qz", "St%d" % sbuf], w=["Rps"])
            A("act", ACT(Kr[sbuf][:, :], rk[:, :], AF.Copy, scale=cst[0:NS, K_ID + b:K_ID + b + 1]), r=["zs1", "cst"], w=["Kr%d" % sbuf])
            for h in range(4):
                hs = slice(h * 128, (h + 1) * 128)
                A("pe", MM(SN[:, hs], Kr[sbuf][:, hs], rv[:, hs], start=True, stop=True), r=["Kr%d" % sbuf, "zs2"], w=["SN"])
            for h in range(4):
                hs = slice(h * 128, (h + 1) * 128)
                A("dve", STT(Sn[sbuf][:, h, :], St[sbuf][:, h, :], GAM[h], SN[:, hs], ALU.mult, ALU.add),
                  r=["St%d" % sbuf, "SN"], w=["Sn%d" % sbuf])
            A("sp", DMA(orv[b], Sn[sbuf][:, :, :]), r=["Sn%d" % sbuf], chan="k_sn%d" % sbuf)
        ck2 = ck.rearrange("(r t) f -> r (t f)", t=2)
        cv2 = cv.rearrange("(r t) f -> r (t f)", t=2)
        A("pe", MM(Oacc[0:32, :], cst2[:, K2_Z:K2_Z + 32], cst[:, 0:512], start=True, stop=False), r=["cst2", "cst"], w=["Oacc"])

        first = False
        it = 0
        gl = [(b_, q_) for b_ in range(NS) for q_ in range(NQ)]

        def issue_gather(i):
            b_, q_ = gl[i]
            kb_ = i % 2
            A("pool", IDMA(Kq[kb_][:, :, :].rearrange("p t f -> p (t f)"), ck2, idx[:, b_, q_:q_ + 1]), r=["idx"], w=["Kq%d" % kb_], chan="k_kq%d" % kb_)
            A("pool", IDMA(Vq[kb_][:, :, :].rearrange("p t f -> p (t f)"), cv2, idx[:, b_, q_:q_ + 1]), r=["idx"], w=["Vq%d" % kb_], chan="k_vq%d" % kb_)
        issue_gather(0)
        for b in range(NS):
            qbuf = b % 2
            ret_step(b)
            A("act", ACT(Km[qbuf][:, :], dq[:, :], AF.Copy, scale=cst[0:NS, K_ID + b:K_ID + b + 1]), r=["zs4", "cst"], w=["Km%d" % qbuf])
            A("pe", MM(QB[:, :], ones_f[0:NS, :], Km[qbuf][:, :], start=True, stop=True), r=["ones_f", "Km%d" % qbuf], w=["QB"])
            A("act", ACT(qb[qbuf][:, :], QB[:, :], AF.Copy), r=["QB"], w=["qb%d" % qbuf])
            for q in range(NQ):
                kb = it % 2
                it += 1
                if it < len(gl):
                    issue_gather(it)
                A("dve", TT(prod[kb][:, 0, :], Kq[kb][:, 0, :], qb[qbuf][:, :], ALU.mult), r=["Kq%d" % kb, "qb%d" % qbuf], w=["prod%d" % kb])
                A("dve", TT(prod[kb][:, 1, :], Kq[kb][:, 1, :], qb[qbuf][:, :], ALU.mult), r=["Kq%d" % kb, "qb%d" % qbuf], w=["prod%d" % kb])
                A("dve", RED(S4[kb][:, :], prod[kb][:, :, :].rearrange("p t (g d) -> p (t g) d", d=64)), r=["prod%d" % kb], w=["S4_%d" % kb])
                A("pool", TT(S4b[kb][:, :], S4[kb][:, :], cst2[:, K2_BIAS + 16 * q:K2_BIAS + 16 * q + 16], ALU.add), r=["S4_%d" % kb, "cst2"], w=["S4b_%d" % kb])
                A("act", ACT(Pc[kb][:, :, :], S4b[kb][:, :].rearrange("p (t g) -> p t g", g=8), AF.Exp, scale=0.125), r=["S4b_%d" % kb], w=["Pc%d" % kb])
                if q < 2:
                    A("pool", MSET(Pz[kb][:], 0.0), w=["Pz%d" % kb])
                A("dve", CP(Pz[kb][:].rearrange("p t h (m s) -> p t h m s", s=16)[:, :, :, :, b],
                            Pc[kb][:, :, :].rearrange("p t (h m) -> p t h m", m=2)), r=["Pc%d" % kb], w=["Pz%d" % kb])
                A("dve", TT(Pq[:, q, b, :], Pc[kb][:, 0, :], Pc[kb][:, 1, :], ALU.add), r=["Pc%d" % kb], w=["Pq"])
                A("act", ACT(Vb[kb][:, :, :], Vq[kb][:, :, :], AF.Copy), r=["Vq%d" % kb], w=["Vb%d" % kb])
                last = (b == NS - 1 and q == NQ - 1)
                for t in range(2):
                    for h in range(4):
                        hs = slice(h * 128, (h + 1) * 128)
                        A("pe", MM(Oacc[0:32, hs], Pz[kb][:, t, h, :], Vb[kb][:, t, hs], start=(first and t == 0), stop=(last and t == 1)),
                          r=["Pz%d" % kb, "Vb%d" % kb], w=["Oacc"])
                first = False
        if L.get("dbg") == "pacc":
            A("sp", DMA(y_s.rearrange("b (x c) -> (b x) c", x=8), Pacc[:].rearrange("p b g -> p (b g)")), r=["Pacc"], chan="k_out")
            S.emit(nc, sem_stack=esP)
            return
        o_ret = t16[1]
        for h in range(4):
            hs = slice(h * 128, (h + 1) * 128)
            A("act", ACT(t16[2][:, hs], rv[:, hs], AF.Copy, scale=smc(8 + h)), r=["zs2", "qk"], w=["t2"])
        for h in range(4):
            hs = slice(h * 128, (h + 1) * 128)
            A("dve", STT(o_ret[:, hs], Rps[0:NS, hs], GAM[h], t16[2][:, hs], ALU.mult, ALU.add), r=["Rps", "t2"], w=["t1"])
        A("pool", TT(t16[2][:, :], o_ret[:, :], o_ret[:, :], ALU.mult), r=["t1"], w=["t2"])
        A("dve", RED(smc(12, 4), t16[2][:, :].rearrange("p (h d) -> p h d", d=128)), r=["t2"], w=["ss1"])
        A("act", ACT(smc(12, 4), smc(12, 4), AF.Sqrt, scale=1.0 / 128, bias=epsc), r=["ss1"], w=["ss1"])
        A("dve", RCP(smc(12, 4), smc(12, 4)), r=["ss1"], w=["ss1"])
        for h in range(4):
            hs = slice(h * 128, (h + 1) * 128)
            A("act", ACT(t16[2][:, hs], o_ret[:, hs], AF.Copy, scale=smc(12 + h)), r=["t1", "ss1"], w=["t2"])
        A("dve", TT(ms_bf[:, 0:512], t16[2][:, :], rg[:, :], ALU.mult), r=["t2", "zs3"], w=["ms_bf"])

        O_sb = t16[0]
        A("act", ACT(O_sb[:, :], Oacc[0:32, :], AF.Copy), r=["Oacc"], w=["t0"])
        A("pe", MM(Zs[0][0:NS, :], cst[0:32, K_ID + 16:K_ID + 32], O_sb[:, :], start=True, stop=True), r=["cst", "t0"], w=["Zs0"])
        A("pool", TT(Pacc[:, :, :], Pq[:, 0, :, :], Pq[:, 1, :, :], ALU.add), r=["Pq"], w=["Pacc"])
        for q_ in range(2, NQ):
            A("pool", TT(Pacc[:, :, :], Pacc[:, :, :], Pq[:, q_, :, :], ALU.add), r=["Pq", "Pacc"], w=["Pacc"])
        for b in range(NS):
            A("pe", MM(MISC[0:NS, 64:72], cst2[:, K2_OH + 16 * b:K2_OH + 16 * b + 16], Pacc[:, b, :], start=(b == 0), stop=(b == NS - 1)),
              r=["cst2", "Pacc"], w=["MISC"])
        A("pool", TT(t16[3][:, :], dq[:, :], dk[:, :], ALU.mult), r=["zs4", "zs5"], w=["t3"])
        A("dve", RED(smc(16, 8), t16[3][:, :].rearrange("p (g d) -> p g d", d=64)), r=["t3"], w=["pnew"])
        A("act", ACT(smc(16, 8), smc(16, 8), AF.Exp, scale=0.125), r=["pnew"], w=["pnew"])
        A("act", ACT(smc(24, 8), MISC[0:NS, 64:72], AF.Copy), r=["MISC"], w=["den"])
        A("pool", TT(smc(24, 8), smc(24, 8), smc(16, 8), ALU.add), r=["den", "pnew"], w=["den"])
        A("dve", RCP(smc(24, 8), smc(24, 8)), r=["den"], w=["den"])
        Ot = [t16[3], t16[4]]
        for h in range(4):
            hs = slice(h * 128, (h + 1) * 128)
            A("dve", STT(Ot[0][:, hs], dv[:, hs], smc(16 + 2 * h), O_sb[0:NS, hs], ALU.mult, ALU.add), r=["zs6", "pnew", "t0"], w=["t3"])
            A("dve", STT(Ot[1][:, hs], dv[:, hs], smc(17 + 2 * h), Zs[0][0:NS, hs], ALU.mult, ALU.add), r=["zs6", "pnew", "Zs0"], w=["t4"])
        on = [t16[5], t16[6]]
        for h in range(4):
            hs = slice(h * 128, (h + 1) * 128)
            A("act", ACT(on[0][:, hs], Ot[0][:, hs], AF.Copy, scale=smc(24 + 2 * h)), r=["t3", "den"], w=["t5"])
            A("act", ACT(on[1][:, hs], Ot[1][:, hs], AF.Copy, scale=smc(25 + 2 * h)), r=["t4", "den"], w=["t6"])
        do = t16[1]
        A("dve", STT(do[:, :], on[1][:, :], neglam, on[0][:, :], ALU.mult, ALU.add), r=["t5", "t6"], w=["t1"])
        if L.get("dbg") == "do":
            A("sp", DMA(y_s[:, 0:512], do[:, :]), r=["t1"], chan="k_out")
            A("sp", DMA(y_s[:, 512:520], smc(24, 8)), r=["den"], chan="k_out")
            A("sp", DMA(y_s[:, 520:528], smc(16, 8)), r=["pnew"], chan="k_out")
            S.emit(nc, sem_stack=esP)
            return
        A("pool", TT(t16[3][:, :], do[:, :], do[:, :], ALU.mult), r=["t1"], w=["t3"])
        A("dve", RED(smc(32, 4), t16[3][:, :].rearrange("p (h d) -> p h d", d=128)), r=["t3"], w=["ss2"])
        A("act", ACT(smc(32, 4), smc(32, 4), AF.Sqrt, scale=1.0 / 128, bias=epsc), r=["ss2"], w=["ss2"])
        A("dve", RCP(smc(32, 4), smc(32, 4)), r=["ss2"], w=["ss2"])
        for h in range(4):
            hs = slice(h * 128, (h + 1) * 128)
            A("act", ACT(t16[3][:, hs], do[:, hs], AF.Copy, scale=smc(32 + h)), r=["t1", "ss2"], w=["t3"])
        A("dve", TT(ms_bf[:, 512:1024], t16[3][:, :], gsub_bc[:].rearrange("p h e -> p (h e)"), ALU.mult), r=["t3", "gsub_bc"], w=["ms_bf"])

        if L.get("dbg") == "ms":
            A("pool", DMA(y_s, ms_bf[:, :]), r=["ms_bf"], chan="k_out")
            S.emit(nc, sem_stack=esP)
            return
        msT = hsT
        transpose16(ms_bf, "ms_bf", msT, "hsT", 8)
        for hf in range(2):
            for c in range(8):
                A("pe", MM(Zs[hf][0:NS, :], msT[:, c, :], wo_sb[:, c, hf * 512:(hf + 1) * 512], start=(c == 0), stop=(c == 7)),
                  r=["hsT"], w=["Zs%d" % hf])
        postnorm16([0, 1], gB, "gB", xs_t, "xs_t", 36)

        A("sp", DMA(gA[:], n_fpre.to_broadcast([NS, D])), w=["gA"], chan="k_g2")
        A("sp", DMA(gB[:], n_fpost.to_broadcast([NS, D])), w=["gB"], chan="k_g2")
        rms16(xs_t, "xs_t", gA, "gA", hs_bf, "hs_bf", 38)
        transpose16(hs_bf, "hs_bf", hsT, "hsT", 8)
        wfiv = w_fi.rearrange("(k p) c -> p k c", p=128)
        for i in range(11):
            cs_ = slice(i * 256, (i + 1) * 256)
            if i % 2 == 0:
                gWs, uWs, gk_, uk_, ch_ = wblk[:, :, 0:256], wblk[:, :, 256:512], "wblk", "wblk", "k_w"
            else:
                gWs = Kq[0][:, :, :].rearrange("p t f -> p (t f)").bitcast(BF16).rearrange("p (k c) -> p k c", c=256)
                uWs = Vq[0][:, :, :].rearrange("p t f -> p (t f)").bitcast(BF16).rearrange("p (k c) -> p k c", c=256)
                gk_, uk_, ch_ = "Kq0", "Vq0", "k_w2"
            A("pool", DMA(gWs, wfiv[:, :, i * 256:(i + 1) * 256]), w=[gk_], chan=ch_)
            A("pool", DMA(uWs, wfiv[:, :, DFF + i * 256:DFF + (i + 1) * 256]), w=[uk_], chan=ch_)
            for j in range(3):
                A("sp", DMA(cwb[:, j, :], convw[j:j + 1, cs_].to_broadcast([NS, 256])), w=["Pq"], chan="k_cw")
            A("sp", DMA(cwb[:, 3, :], convb[0:1, cs_].to_broadcast([NS, 256])), w=["Pq"], chan="k_cw")
            A("sp", DMA(scb[:, :, :], sconv[:, :, cs_]), w=["scb"], chan="k_cw")
            for k in range(8):
                A("pe", MM(Zs[0][0:NS, 0:256], hsT[:, k, :], gWs[:, k, :], start=(k == 0), stop=(k == 7)), r=["hsT", gk_], w=["Zs0"])
            for k in range(8):
                A("pe", MM(Zs[1][0:NS, 0:256], hsT[:, k, :], uWs[:, k, :], start=(k == 0), stop=(k == 7)), r=["hsT", uk_], w=["Zs1"])
            gS = cc[0]
            A("act", ACT(gS[:, :], Zs[0][0:NS, 0:256], AF.Copy), r=["Zs0"], w=["cc0"])
            A("sp", DMA(o_cs[:, 1, cs_], gS[:, :]), r=["cc0"], chan="k_gs")
            A("pool", TT(cc[1][:, :], gS[:, :], cwb[:, 2, :], ALU.mult), r=["cc0", "Pq"], w=["cc1"])
            A("pool", TT(cc[1][:, :], cc[1][:, :], cwb[:, 3, :], ALU.add), r=["cc1", "Pq"], w=["cc1"])
            A("pool", TT(cc[2][:, :], scb[:, 0, :], cwb[:, 0, :], ALU.mult), r=["scb", "Pq"], w=["cc2"])
            A("pool", TT(cc[1][:, :], cc[1][:, :], cc[2][:, :], ALU.add), r=["cc1", "cc2"], w=["cc1"])
            A("pool", TT(cc[2][:, :], scb[:, 1, :], cwb[:, 1, :], ALU.mult), r=["scb", "Pq"], w=["cc2"])
            A("pool", TT(cc[1][:, :], cc[1][:, :], cc[2][:, :], ALU.add), r=["cc1", "cc2"], w=["cc1"])
            A("act", ACT(cc[2][:, :], cc[1][:, :], AF.Gelu_apprx_tanh), r=["cc1"], w=["cc2"])
            A("dve", TT(a2_bf[:, cs_], Zs[1][0:NS, 0:256], cc[2][:, :], ALU.mult), r=["Zs1", "cc2"], w=["a2_bf"])
        transpose16(a2_bf, "a2_bf", a2T, "a2T", NFC)
        for hf in range(2):
            for fc in range(NFC):
                A("pe", MM(Zs[hf][0:NS, :], a2T[:, fc, :], wfo_sb[:, fc, hf * 512:(hf + 1) * 512], start=(fc == 0), stop=(fc == NFC - 1)),
                  r=["a2T"], w=["Zs%d" % hf])
        postnorm16([0, 1], gB, "gB", xs_t, "xs_t", 40)
        A("sp", DMA(y_s, xs_t[:, :]), r=["xs_t"], chan="k_out")
        print("n_ops sample", len(S.ops))
        S.emit(nc, sem_stack=esP)

def build_program(n_pool, with_sample=True, limit=None, dbg=None):
    nc = bass.Bass("TRN2", target_bir_lowering=False)
    S = Sched()
    dt = nc.dram_tensor

    def inp(name, shape, dtp=F32):
        return dt(name, list(shape), dtp, kind="ExternalInput").ap()

    def outp(name, shape):
        return dt(name, list(shape), F32, kind="ExternalOutput").ap()

    xp = inp("xp", [T, D])
    w_in = inp("w_in", [D, INW])
    w_o = inp("w_o", [D, D])
    w_fi = inp("w_fi", [D, 2 * DFF])
    w_fo = inp("w_fo", [DFF, D])
    n_mpre = inp("n_mpre", [1, D])
    n_mpost = inp("n_mpost", [1, D])
    n_fpre = inp("n_fpre", [1, D])
    n_fpost = inp("n_fpost", [1, D])
    lq1 = inp("lq1", [1, 64])
    lk1 = inp("lk1", [1, 64])
    lq2 = inp("lq2", [1, 64])
    lk2 = inp("lk2", [1, 64])
    subg = inp("subg", [1, 128])
    convw = inp("convw", [3, DFF])
    convb = inp("convb", [1, DFF])
    consts_d = inp("consts", [128, K_END])
    shift_d = inp("shiftrow", [1, 2048])
    consts2_d = inp("consts2", [128, K2_END])
    xs = inp("xs", [NS, D])
    sret = inp("sret", [NS, 4, 128, 128])
    ck = inp("ck", [n_pool * 128, 512])
    cv = inp("cv", [n_pool * 128, 512])
    sconv = inp("sconv", [NS, 2, DFF])
    ptab = inp("ptab", [NS, 16], I32)

    y_p = outp("y_p", [T, D])
    y_s = outp("y_s", [NS, D])
    o_rsp = outp("o_rsp", [4, 128, 128])
    o_rss = outp("o_rss", [NS, 4, 128, 128])
    o_kp = outp("o_kp", [T, 512])
    o_vp = outp("o_vp", [T, 512])
    o_ks = outp("o_ks", [NS, 512])
    o_vs = outp("o_vs", [NS, 512])
    o_cp = outp("o_cp", [2, DFF])
    o_cs = outp("o_cs", [NS, 2, DFF])
    x1_d = dt("x1_scratch", [T, D], F32, kind="Internal").ap()

    esP = contextlib.ExitStack()
    es = contextlib.ExitStack()
    with esP, es:
        def sbP(name, shape, dtp=F32):
            return esP.enter_context(nc.sbuf_tensor(name, list(shape), dtp))

        def sb(name, shape, dtp=F32):
            return es.enter_context(nc.sbuf_tensor(name, list(shape), dtp))

        def ps(name, shape, dtp=F32):
            return es.enter_context(nc.psum_tensor(name, list(shape), dtp))

        cst = sbP("cst", [128, K_END])
        ident_bf = sbP("ident_bf", [128, 128], BF16)
        ones_bf = sbP("ones_bf", [128, 128], BF16)
        ones_f = sbP("ones_f", [128, 128])
        lamw = sbP("lamw", [128, 8])
        subgc = sbP("subgc", [128, 2])
        cwc = sbP("cwc", [128, NFC, 4])
        wbig = sbP("wbig", [128, 8, INW], BF16)
        wo_sb = sbP("wo_sb", [128, 8, D], BF16)
        scol = sbP("scol", [128, 8])
        shift_bf = sb("shift_bf", [1, 2048], BF16)
        gpre = sb("gpre", [128, D])
        gpost = sb("gpost", [128, D])
        lamt = sb("lamt", [128, 4, 64])
        kT = sb("kT", [128, 4, T], BF16)
        dv_sb = sb("dv_sb", [128, NT, 512], BF16)
        xt = [sb("xt%d" % i, [128, D]) for i in range(2)]
        hbf = [sb("hbf%d" % i, [128, D], BF16) for i in range(2)]
        hT = sb("hT", [128, 8, GT], BF16)
        feat4 = sb("feat4", [128, 4, 4, GT], BF16)
        rqT, rkT, rgT, dqT = feat4[:, 0], feat4[:, 1], feat4[:, 2], feat4[:, 3]
        tok2 = sb("tok2", [128, 2, 4, 512], BF16)
        rv_sb, kd_sb = tok2[:, 0], tok2[:, 1]
        mT = sb("mT", [128, 8, GT], BF16)
        stg = [sb("stg%d" % i, [128, 512]) for i in range(2)]
        PT = [sb("PT%d" % i, [128, 512], BF16) for i in range(3)]
        attm4 = sb("attm4", [128, 4, 128], BF16)
        qd4 = sb("qd4", [128, 4, 128], BF16)
        oT = sb("oT", [128, 4, GT])
        Sf = sb("Sf", [128, 4, 128])
        Sb = sb("Sb", [128, 4, 128], BF16)
        onT = [oT[:, 0, :], oT[:, 1, :]]
        rden = oT[:, 2, :]
        doT = oT[:, 3, :]
        sq = sb("sq", [128, 512])
        rstd = sb("rstd", [128, 512])
        tmpA = sb("tmpA", [128, 512])
        gpv_t = sb("gpv_t", [128, 2, NFC])

        Z = [ps("Z%d" % i, [128, 512]) for i in range(2)]
        SC = [ps("SC%d" % i, [128, 512]) for i in range(2)]
        AV = ps("AV", [128, 512])
        DEN = ps("DEN", [128, 512])
        RT = ps("RT", [128, 512])
        TP = ps("TP", [128, 8, 128], BF16)
        SFK = ["Sf0", "Sf1", "Sf2", "Sf3"]
        SBK = ["Sb0", "Sb1", "Sb2", "Sb3"]

        S.add("sp", DMA(cst[:], consts_d), w=["cst"], chan="c_cst")
        S.add("pool", DMA(shift_bf[:], shift_d), w=["shift_bf"], chan="c_shift")
        S.add("sp", DMA(gpre[:], n_mpre.to_broadcast([128, D])), w=["gpre"], chan="c_cst")
        S.add("sp", DMA(gpost[:], n_mpost.to_broadcast([128, D])), w=["gpost"], chan="c_cst")
        for i, d_ in enumerate((lq1, lk1, lq2, lk2)):
            S.add("sp", DMA(lamt[:, i, :], d_.to_broadcast([128, 64])), w=["lamt"], chan="c_cst")
        S.add("sp", DMA(subgc[:, 0:1], subg.rearrange("o e -> e o"), slow=True), w=["subgc"], chan="c_cst")
        for j in range(3):
            S.add("sp", DMA(cwc[:, :, j:j + 1], convw[j:j + 1, :].rearrange("o (c p) -> p c o", p=128), slow=True),
                  w=["cwc"], chan="c_cst")
        S.add("sp", DMA(cwc[:, :, 3:4], convb.rearrange("o (c p) -> p c o", p=128), slow=True), w=["cwc"], chan="c_cst")
        WBK7 = ["wbig%d" % j for j in range(7)]
        w_inv = w_in.rearrange("(k p) c -> p k c", p=128)
        for j in (0, 1, 3, 4, 5, 2, 6):
            S.add("pool", DMA(wbig[:, :, j * 512:(j + 1) * 512], w_inv[:, :, j * 512:(j + 1) * 512]), w=[WBK7[j]], chan="c_win%d" % j)
        for c in range(8):
            S.add("pool", DMA(wo_sb[:, c, :], w_o[c * 128:(c + 1) * 128, :]), w=["wo_sb"], chan="c_wo")

        S.add("dve", CP(ident_bf[:], cst[:, K_ID:K_ID + 128]), r=["cst"], w=["ident_bf"])
        S.add("dve", MSET(ones_bf[:], 1.0), w=["ones_bf"])
        S.add("dve", MSET(ones_f[:], 1.0), w=["ones_f"])
        S.add("dve", MSET(Sf[:], 0.0), w=SFK)
        S.add("dve", MSET(Sb[:], 0.0), w=SBK)
        S.add("dve", MSET(gpv_t[:], 0.0), w=["gpv"])
        S.add("pool", TT(lamt[:, 0, :], lamt[:, 0, :], lamt[:, 1, :], ALU.mult), r=["lamt"], w=["lamt"])
        S.add("pool", TT(lamt[:, 2, :], lamt[:, 2, :], lamt[:, 3, :], ALU.mult), r=["lamt"], w=["lamt"])
        S.add("dve", RED(lamw[:, 0:1], lamt[:, 0, :]), r=["lamt"], w=["lamw"])
        S.add("dve", RED(lamw[:, 1:2], lamt[:, 2, :]), r=["lamt"], w=["lamw"])
        S.add("act", ACT(lamw[:, 3:5], lamw[:, 0:2], AF.Exp), r=["lamw"], w=["lamw2"])
        S.add("dve", STT(lamw[:, 2:3], lamw[:, 4:5], -LAM_INIT, lamw[:, 3:4], ALU.add, ALU.subtract), r=["lamw2"], w=["neglam"])
        S.add("dve", TS(subgc[:, 1:2], subgc[:, 0:1], 1.0 - LAM_INIT, None, ALU.mult), r=["subgc"], w=["subgc2"])
        neglam = lamw[:, 2:3]
        gsub = subgc[:, 1:2]

        zc = [0]

        def nextZ():
            zc[0] += 1
            return zc[0] % 2

        def sck(c):
            return "scol%d" % c

        def rms_to_bf16(src, src_key, g_tile, g_key, out_bf, out_key, col):
            a, b_ = scol[:, col:col + 1], scol[:, col + 1:col + 2]
            S.add("act", ACT(tmpA[:, 0:512], src[:, 0:512], AF.Square, accum_out=a), r=[src_key], w=["tmpA", sck(col)])
            S.add("act", ACT(tmpA[:, 0:512], src[:, 512:1024], AF.Square, accum_out=b_), r=[src_key], w=["tmpA", sck(col + 1)])
            S.add("dve", TT(a, a, b_, ALU.add), r=[sck(col), sck(col + 1)], w=[sck(col)])
            S.add("act", ACT(a, a, AF.Sqrt, scale=1.0 / D, bias=epsc), r=[sck(col), "epsc"], w=[sck(col)])
            S.add("dve", RCP(a, a), r=[sck(col)], w=[sck(col)])
            for hf in range(2):
                hs = slice(hf * 512, (hf + 1) * 512)
                S.add("act", ACT(tmpA[:, :], src[:, hs], AF.Copy, scale=a), r=[src_key, sck(col)], w=["tmpA"])
                S.add("dve", TT(out_bf[:, hs], tmpA[:, :], g_tile[:, hs], ALU.mult), r=["tmpA", g_key], w=[out_key])

        def post_norm_residual(zb, g_tile, g_key, res, res_key, col):
            a, b_ = scol[:, col:col + 1], scol[:, col + 1:col + 2]
            S.add("act", ACT(tmpA[:, :], zb[0][0][:, :], AF.Square, accum_out=a), r=[zb[0][1]], w=["tmpA", sck(col)])
            S.add("act", ACT(tmpA[:, :], zb[1][0][:, :], AF.Square, accum_out=b_), r=[zb[1][1]], w=["tmpA", sck(col + 1)])
            S.add("dve", TT(a, a, b_, ALU.add), r=[sck(col), sck(col + 1)], w=[sck(col)])
            S.add("act", ACT(a, a, AF.Sqrt, scale=1.0 / D, bias=epsc), r=[sck(col), "epsc"], w=[sck(col)])
            S.add("dve", RCP(a, a), r=[sck(col)], w=[sck(col)])
            for hf in range(2):
                hs = slice(hf * 512, (hf + 1) * 512)
                S.add("act", ACT(tmpA[:, :], zb[hf][0][:, :], AF.Copy, scale=a), r=[zb[hf][1], sck(col)], w=["tmpA"])
                S.add("dve", TT(tmpA[:, :], tmpA[:, :], g_tile[:, hs], ALU.mult), r=["tmpA", g_key], w=["tmpA"])
                S.add("dve", TT(res[:, hs], res[:, hs], tmpA[:, :], ALU.add), r=["tmpA", res_key], w=[res_key])

        ZPAIRS = [[(Z[0], "Z0"), (Z[1], "Z1")], [(SC[0], "SC0"), (SC[1], "SC1")], [(AV, "AV"), (DEN, "DEN")]]
        zpc = [0]

        def next_pair():
            zpc[0] += 1
            return ZPAIRS[zpc[0] % 3]

        def transpose_tile(src_bf, src_key, dstT, dst_key, col0):
            for c in range(8):
                S.add("pe", TR(TP[:, c, :], src_bf[:, c * 128:(c + 1) * 128], ident_bf[:]), r=[src_key, "ident_bf"], w=["TP"])
            S.add("act", ACT(dstT[:, :, col0:col0 + 128], TP[:, :, :], AF.Copy), r=["TP"], w=[dst_key])

        epsc = scol[:, 7:8]
        S.add("dve", MSET(epsc, EPS), w=["epsc"])

        xpv = xp.rearrange("(t p) d -> t p d", p=128)
        x1v = x1_d.rearrange("(t p) d -> t p d", p=128)
        okp = o_kp.rearrange("(t p) d -> t p d", p=128)
        ovp = o_vp.rearrange("(t p) d -> t p d", p=128)
        stc = [0]

        def featproj(col0, eng, fn_of_z, wkeys):
            z = nextZ()
            for k in range(8):
                S.add("pe", MM(Z[z][:, :], wbig[:, k, col0:col0 + 128], hT[:, k, :], start=(k == 0), stop=(k == 7)),
                      r=[WBK7[col0 // 512], "hT"], w=["Z%d" % z])
            S.add(eng, fn_of_z(Z[z][:, :]), r=["Z%d" % z], w=wkeys)

        def tokproj(tt, col0):
            z = nextZ()
            for k in range(8):
                S.add("pe", MM(Z[z][:, :], hT[:, k, tt * 128:(tt + 1) * 128], wbig[:, k, col0:col0 + 512], start=(k == 0), stop=(k == 7)),
                      r=[WBK7[col0 // 512], "hT"], w=["Z%d" % z])
            return z

        for g in range(NG):
            for tt in range(4):
                ti = 4 * g + tt
                b = ti % 2
                S.add("sp", DMA(xt[b][:], xpv[ti]), w=["xt%d" % b], chan="c_xt%d" % b)
                rms_to_bf16(xt[b], "xt%d" % b, gpre, "gpre", hbf[b], "hbf%d" % b, 0)
                transpose_tile(hbf[b], "hbf%d" % b, hT, "hT", tt * 128)
            for h in range(4):
                featproj(C_RQ + 128 * h, "act", lambda zz, h=h: ACT(rqT[:, h, :], zz, AF.Copy), ["rqT"])
                featproj(C_RK + 128 * h, "dve", lambda zz, h=h: CP(rkT[:, h, :], zz), ["rkT"])
                featproj(C_RG + 128 * h, "act", lambda zz, h=h: ACT(rgT[:, h, :], zz, AF.Silu), ["rgT"])
                featproj(C_DQ + 128 * h, "dve", lambda zz, h=h: CP(dqT[:, h, :], zz), ["dqT"])
                featproj(C_DK + 128 * h, "act", lambda zz, h=h, g=g: ACT(kT[:, h, g * GT:(g + 1) * GT], zz, AF.Copy), ["kT"])
            for tt in range(4):
                ti = 4 * g + tt
                z = tokproj(tt, C_RV)
                S.add("act", ACT(rv_sb[:, tt, :], Z[z][:, :], AF.Copy), r=["Z%d" % z], w=["rv_sb"])
                z = tokproj(tt, C_RK)
                S.add("dve", TT(kd_sb[:, tt, :], Z[z][:, :], cst[:, K_KDEC:K_KDEC + 512], ALU.mult), r=["Z%d" % z, "cst"], w=["kd_sb"])
                z = tokproj(tt, C_DK)
                si = stc[0] % 2
                stc[0] += 1
                S.add("act", ACT(stg[si][:, :], Z[z][:, :], AF.Copy), r=["Z%d" % z], w=["stg%d" % si])
                S.add("sp", DMA(okp[ti], stg[si][:, :]), r=["stg%d" % si], chan="c_stg%d" % si)
                z = tokproj(tt, C_DV)
                si = stc[0] % 2
                stc[0] += 1
                S.add("act", ACT(stg[si][:, :], Z[z][:, :], AF.Copy), r=["Z%d" % z], w=["stg%d" % si])
                S.add("dve", CP(dv_sb[:, ti, :], Z[z][:, :]), r=["Z%d" % z], w=["dv_sb"])
                S.add("sp", DMA(ovp[ti], stg[si][:, :]), r=["stg%d" % si], chan="c_stg%d" % si)

            for ci in range(4):
                cs = slice(ci * 128, (ci + 1) * 128)
                for h in range(4):
                    hs = slice(128 * h, 128 * (h + 1))
                    S.add("pe", MM(RT[:, hs], rkT[:, h, cs], rqT[:, h, cs]), r=["rkT", "rqT"], w=["RT"])
                for h in range(4):
                    hs = slice(128 * h, 128 * (h + 1))
                    S.add("pe", MM(SC[0][:, hs], kd_sb[:, ci, hs], rv_sb[:, ci, hs]), r=["kd_sb", "rv_sb"], w=["SC0"])
                for h in range(4):
                    hs = slice(128 * h, 128 * (h + 1))
                    S.add("dve", TT(attm4[:, h, :], RT[:, hs], cst[:, K_DECAY + 128 * h:K_DECAY + 128 * (h + 1)], ALU.mult),
                          r=["RT", "cst"], w=["attm"])
                    S.add("pool", TT(qd4[:, h, :], rqT[:, h, cs], cst[:, K_QDEC + 128 * h:K_QDEC + 128 * (h + 1)], ALU.mult),
                          r=["rqT", "cst"], w=["qd"])
                for h in range(4):
                    hs = slice(128 * h, 128 * (h + 1))
                    S.add("pe", MM(AV[:, hs], rv_sb[:, ci, hs], attm4[:, h, :], start=True, stop=False), r=["rv_sb", "attm"], w=["AV"])
                    S.add("pe", MM(AV[:, hs], Sb[:, h, :], qd4[:, h, :], start=False, stop=True), r=SBK + ["qd"], w=["AV"])
                S.add("act", ACT(oT[:, :, cs], AV[:, :].rearrange("p (h t) -> p h t", t=128), AF.Copy), r=["AV"], w=["oT"])
                for h in range(4):
                    hs = slice(128 * h, 128 * (h + 1))
                    S.add("dve", STT(Sf[:, h, :], Sf[:, h, :], GAM[h] ** 128, SC[0][:, hs], ALU.mult, ALU.add), r=["SC0"] + SFK, w=SFK)
                S.add("act", ACT(Sb[:, :, :], Sf[:, :, :], AF.Copy), r=SFK, w=SBK)
            for h in range(4):
                S.add("act", ACT(sq[:, :], oT[:, h, :], AF.Square), r=["oT"], w=["sq"])
                S.add("pe", MM(DEN[:, :], ones_f[:, :], sq[:, :]), r=["ones_f", "sq"], w=["DEN"])
                S.add("act", ACT(rstd[:, :], DEN[:, :], AF.Sqrt, scale=1.0 / 128, bias=epsc), r=["DEN", "epsc"], w=["rstd"])
                S.add("dve", RCP(rstd[:, :], rstd[:, :]), r=["rstd"], w=["rstd"])
                S.add("dve", TT(rstd[:, :], rstd[:, :], rgT[:, h, :], ALU.mult), r=["rstd", "rgT"], w=["rstd"])
                S.add("dve", TT(mT[:, h, :], rstd[:, :], oT[:, h, :], ALU.mult), r=["rstd", "oT"], w=["mT"])

            nkb = 4 * g + 4
            items = [(h, m, kb) for h in range(4) for m in range(2) for kb in range(nkb)]

            SCR = [(SC[0], "SC0"), (SC[1], "SC1"), (Z[0], "Z0"), (Z[1], "Z1")]

            def emit_score(i):
                h, m, kb = items[i]
                rows = slice(64 * m, 64 * (m + 1))
                c0 = 128 * max(0, kb - 4 * g)
                scb, sck_ = SCR[i % 4]
                S.add("pe", MM(scb[:, c0:512], kT[rows, h, kb * 128:(kb + 1) * 128], dqT[rows, h, c0:512], start=True, stop=(h != 0)),
                      r=["kT", "dqT"], w=[sck_])
                if h == 0:
                    S.add("pe", MM(scb[:, c0:512], ones_bf[0:1, :], shift_bf[0:1, 0:512 - c0], start=False, stop=True),
                          r=["ones_bf", "shift_bf"], w=[sck_])
            emit_score(0)
            emit_score(1)
            for i, (h, m, kb) in enumerate(items):
                hs = slice(128 * h, 128 * (h + 1))
                r_ = max(0, kb - 4 * g)
                c0 = 128 * r_
                scb, sck_ = SCR[i % 4]
                pb = i % 3
                bidx = K_ABIAS + 16 * h + (4 * g + (r_ if h == 0 else 0) - kb) + 3
                last = (kb >= 4 * g)
                if i + 2 < len(items):
                    emit_score(i + 2)
                S.add("act", ACT(PT[pb][:, c0:512], scb[:, c0:512], AF.Exp, scale=0.125, bias=cst[:, bidx:bidx + 1]),
                      r=[sck_, "cst"], w=["PT%d" % pb])
                if last:
                    S.add("pool", TT(PT[pb][:, c0:c0 + 128], PT[pb][:, c0:c0 + 128], cst[:, K_CAUS:K_CAUS + 128], ALU.mult),
                          r=["PT%d" % pb, "cst"], w=["PT%d" % pb])
                S.add("pe", MM(AV[:, c0:512], dv_sb[:, kb, hs], PT[pb][:, c0:512], start=(kb == 0), stop=last),
                      r=["dv_sb", "PT%d" % pb], w=["AV"])
                S.add("pe", MM(DEN[:, c0:512], ones_bf[:, :], PT[pb][:, c0:512], start=(kb == 0), stop=last),
                      r=["ones_bf", "PT%d" % pb], w=["DEN"])
                if kb == nkb - 1:
                    S.add("dve", RCP(rden, DEN[:, :]), r=["DEN"], w=["oT"])
                    S.add("dve", TT(onT[m], AV[:, :], rden, ALU.mult), r=["AV", "oT"], w=["oT"])
                    if m == 1:
                        S.add("dve", STT(doT, onT[1], neglam, onT[0], ALU.mult, ALU.add), r=["oT", "neglam"], w=["oT"])
                        S.add("pool", TT(sq[:, :], doT, doT, ALU.mult), r=["oT"], w=["sq"])
                        S.add("pe", MM(RT[:, :], ones_f[:, :], sq[:, :]), r=["ones_f", "sq"], w=["RT"])
                        S.add("act", ACT(rstd[:, :], RT[:, :], AF.Ln, scale=1.0 / 128, bias=epsc), r=["RT", "epsc"], w=["rstd"])
                        S.add("act", ACT(rstd[:, :], rstd[:, :], AF.Exp, scale=-0.5), r=["rstd"], w=["rstd"])
                        S.add("dve", STT(mT[:, 4 + h, :], doT, gsub, rstd[:, :], ALU.mult, ALU.mult), r=["oT", "rstd", "subgc2"], w=["mT"])

            if dbg == "m" and g == 0:
                S.add("pool", DMA(y_p[0:1024, 0:512].rearrange("(c p) t -> p c t", p=128), mT[:, :, :]), r=["mT"], chan="c_misc")
                S.emit(nc, sem_stack=esP)
                return nc
            for tt in range(4):
                ti = 4 * g + tt
                b = ti % 2
                S.add("sp", DMA(xt[b][:], xpv[ti]), w=["xt%d" % b], chan="c_xt%d" % b)
                zb = next_pair()
                for hf in range(2):
                    for c in range(8):
                        S.add("pe", MM(zb[hf][0][:, :], mT[:, c, tt * 128:(tt + 1) * 128], wo_sb[:, c, hf * 512:(hf + 1) * 512], start=(c == 0), stop=(c == 7)),
                              r=["mT", "wo_sb"], w=[zb[hf][1]])
                post_norm_residual(zb, gpost, "gpost", xt[b], "xt%d" % b, 2)
                S.add("sp", DMA(x1v[ti], xt[b][:]), r=["xt%d" % b], w=["x1d%d" % ti], chan="c_xo%d" % b)

        S.add("sp", DMA(o_rsp.rearrange("h d e -> d h e"), Sf[:, :, :]), r=SFK, chan="c_misc")
        if dbg == "x1":
            S.add("sp", DMA(y_p, x1_d), r=["x1d%d" % i for i in range(NT)], chan="c_misc")
            S.emit(nc, sem_stack=esP)
            return nc

        S.add("sp", DMA(gpre[:], n_fpre.to_broadcast([128, D])), w=["gpre"], chan="c_g2")
        S.add("sp", DMA(gpost[:], n_fpost.to_broadcast([128, D])), w=["gpost"], chan="c_g2")
        wfo_sb = wbig.rearrange("p a b -> p (a b)")[:, 0:NFC * D].rearrange("p (c d) -> p c d", d=D)
        for c in range(NFC):
            S.add("pool", DMA(wfo_sb[:, c, :], w_fo[c * 128:(c + 1) * 128, :]), w=WBK7, chan="c_wfo")
        aT = kT.rearrange("p a b -> p (a b)")
        aT2 = dv_sb.rearrange("p a b -> p (a b)")

        def aTc(fc):
            if fc < 16:
                return aT[:, fc * 512:(fc + 1) * 512], "kT"
            return aT2[:, (fc - 16) * 512:(fc - 15) * 512], "dv_sb"
        WB = [feat4[:, 0:2].rearrange("p a h t -> p (a h t)").rearrange("p (k c) -> p k c", c=512),
              feat4[:, 2:4].rearrange("p a h t -> p (a h t)").rearrange("p (k c) -> p k c", c=512),
              mT[:, :, :],
              tok2.rearrange("p a h t -> p (a h t)").rearrange("p (k c) -> p k c", c=512)]
        WBK = [["rqT", "rkT"], ["rgT", "dqT"], ["mT"], ["rv_sb", "kd_sb"]]
        blkc = [0]
        gsb = [oT[:, 0, :], oT[:, 1, :]]
        c1 = sq
        c2 = rstd
        yv = y_p.rearrange("(t p) d -> t p d", p=128)
        wfiv = w_fi.rearrange("(k p) c -> p k c", p=128)

        for g in range(NG):
            for tt in range(4):
                ti = 4 * g + tt
                b = ti % 2
                S.add("sp", DMA(xt[b][:], x1v[ti]), r=["x1d%d" % ti], w=["xt%d" % b], chan="c_xt%d" % b)
                rms_to_bf16(xt[b], "xt%d" % b, gpre, "gpre", hbf[b], "hbf%d" % b, 0)
                transpose_tile(hbf[b], "hbf%d" % b, hT, "hT", tt * 128)
            for fc in range(NFC):
                if fc % 4 == 0:
                    nfc_blk = min(4, NFC - fc)
                    par = blkc[0] % 2
                    blkc[0] += 1
                    gW, uW = WB[2 * par], WB[2 * par + 1]
                    gWk, uWk = WBK[2 * par], WBK[2 * par + 1]
                    ncol = 128 * nfc_blk
                    S.add("pool", DMA(gW[:, :, 0:ncol], wfiv[:, :, fc * 128:fc * 128 + ncol]), w=gWk, chan="c_wfi%d" % par)
                    S.add("pool", DMA(uW[:, :, 0:ncol], wfiv[:, :, DFF + fc * 128:DFF + fc * 128 + ncol]), w=uWk, chan="c_wfi%d" % par)
                fo = (fc % 4) * 128
                if fc % 2 == 0:
                    G_, U_, gkey, ukey = SC[0], SC[1], "SC0", "SC1"
                else:
                    G_, U_, gkey, ukey = AV, DEN, "AV", "DEN"
                for k in range(8):
                    S.add("pe", MM(G_[:, :], gW[:, k, fo:fo + 128], hT[:, k, :], start=(k == 0), stop=(k == 7)), r=gWk + ["hT"], w=[gkey])
                for k in range(8):
                    S.add("pe", MM(U_[:, :], uW[:, k, fo:fo + 128], hT[:, k, :], start=(k == 0), stop=(k == 7)), r=uWk + ["hT"], w=[ukey])
                gs = gsb[fc % 2]
                gk = "gsb%d" % (fc % 2)
                w0, w1, w2, bb = cwc[:, fc, 0:1], cwc[:, fc, 1:2], cwc[:, fc, 2:3], cwc[:, fc, 3:4]
                S.add("act", ACT(gs[:, :], G_[:, :], AF.Copy), r=[gkey], w=[gk])
                S.add("act", ACT(c1[:, :], G_[:, :], AF.Copy, scale=w2), r=[gkey, "cwc"], w=["c1"])
                S.add("dve", STT(c1[:, 1:512], gs[:, 0:511], w1, c1[:, 1:512], ALU.mult, ALU.add), r=[gk, "c1", "cwc"], w=["c1"])
                S.add("dve", STT(c1[:, 0:1], gpv_t[:, 1:2, fc], w1, c1[:, 0:1], ALU.mult, ALU.add), r=["gpv", "c1", "cwc"], w=["c1"])
                S.add("dve", STT(c1[:, 2:512], gs[:, 0:510], w0, c1[:, 2:512], ALU.mult, ALU.add), r=[gk, "c1", "cwc"], w=["c1"])
                S.add("dve", STT(c1[:, 0:2], gpv_t[:, 0:2, fc], w0, c1[:, 0:2], ALU.mult, ALU.add), r=["gpv", "c1", "cwc"], w=["c1"])
                S.add("dve", CP(gpv_t[:, 0:2, fc], gs[:, 510:512]), r=[gk, "gpv"], w=["gpv"])
                S.add("act", ACT(c2[:, :], c1[:, :], AF.Gelu_apprx_tanh, bias=bb), r=["c1", "cwc"], w=["c2"])
                a_ap, a_key = aTc(fc)
                S.add("dve", TT(a_ap, U_[:, :], c2[:, :], ALU.mult), r=[ukey, "c2"], w=[a_key])
            for tt in range(4):
                ti = 4 * g + tt
                b = ti % 2
                S.add("sp", DMA(xt[b][:], x1v[ti]), r=["x1d%d" % ti], w=["xt%d" % b], chan="c_xt%d" % b)
                zb = next_pair()
                for hf in range(2):
                    for fc in range(NFC):
                        a_ap, a_key = aTc(fc)
                        S.add("pe", MM(zb[hf][0][:, :], a_ap[:, tt * 128:(tt + 1) * 128], wfo_sb[:, fc, hf * 512:(hf + 1) * 512],
                                       start=(fc == 0), stop=(fc == NFC - 1)), r=[a_key] + WBK7, w=[zb[hf][1]])
                post_norm_residual(zb, gpost, "gpost", xt[b], "xt%d" % b, 2)
                S.add("sp", DMA(yv[ti], xt[b][:]), r=["xt%d" % b], chan="c_xo%d" % b)
        for j in range(2):
            S.add("sp", DMA(o_cp[j:j + 1, :].rearrange("o (c p) -> p (o c)", p=128), gpv_t[:, j, :], slow=True), r=["gpv"], chan="c_misc")

        print('n_ops', len(S.ops))
        S.emit(nc, limit=limit, sem_stack=esP)
        es.close()
        if with_sample and limit is None:
            nc.all_engine_barrier()
            build_sample(nc, esP, locals())
    return nc


_CACHE = {}


def _get_program(n_pool):
    if n_pool not in _CACHE:
        _CACHE[n_pool] = build_program(n_pool)
    return _CACHE[n_pool]


def kernel(x_prompt, x_sample, state_ret, cache_k, cache_v, state_conv, page_table,
           norm_mix_pre, norm_mix_post, w_in, w_o, lambda_q1, lambda_k1, lambda_q2, lambda_k2,
           subln_g, norm_ffn_pre, norm_ffn_post, w_ffn_in, conv_w, conv_b, w_ffn_out):
    f = lambda a: np.ascontiguousarray(np.asarray(a, dtype=np.float32))
    x_prompt, x_sample = f(x_prompt), f(x_sample)
    n_pool = int(np.asarray(cache_k).shape[1])
    nc = _get_program(n_pool)
    consts, shiftrow = make_consts()
    ck = f(cache_k)[0].reshape(n_pool * 128, 512)
    cv = f(cache_v)[0].reshape(n_pool * 128, 512)
    pt = np.ascontiguousarray(np.asarray(page_table, dtype=np.int32))
    shared = dict(
        w_in=f(w_in)[0], w_o=f(w_o)[0], w_fi=f(w_ffn_in)[0], w_fo=f(w_ffn_out)[0],
        n_mpre=f(norm_mix_pre), n_mpost=f(norm_mix_post), n_fpre=f(norm_ffn_pre), n_fpost=f(norm_ffn_post),
        lq1=f(lambda_q1), lk1=f(lambda_k1), lq2=f(lambda_q2), lk2=f(lambda_k2), subg=f(subln_g),
        convw=f(conv_w)[0], convb=f(conv_b), consts=consts, shiftrow=shiftrow, ck=ck, cv=cv, consts2=make_consts2(),
    )
    sret = f(state_ret)[0]
    sconv = f(state_conv)[0]
    in_maps = []
    for c in range(8):
        m = dict(shared)
        m["xp"] = x_prompt[c]
        m["xs"] = np.ascontiguousarray(x_sample[c * NS:(c + 1) * NS, 0, :])
        m["sret"] = np.ascontiguousarray(sret[c * NS:(c + 1) * NS])
        m["sconv"] = np.ascontiguousarray(sconv[c * NS:(c + 1) * NS])
        m["ptab"] = np.ascontiguousarray(pt[c * NS:(c + 1) * NS])
        in_maps.append(m)
    res = run_bass_kernel_spmd(nc, in_maps, core_ids=list(range(8)))
    R = res.results
    cat = lambda k: np.stack([np.asarray(r[k]) for r in R], axis=0)
    y_prompt = cat("y_p")
    y_sample = np.concatenate([np.asarray(r["y_s"]) for r in R], axis=0).reshape(128, 1, D)
    rsp = cat("o_rsp")[None]
    rss = np.concatenate([np.asarray(r["o_rss"]) for r in R], axis=0)[None]
    kp = cat("o_kp").reshape(1, 8, T, 4, 2, 64)
    vp = cat("o_vp").reshape(1, 8, T, 4, 128)
    ks = np.concatenate([np.asarray(r["o_ks"]) for r in R], axis=0).reshape(1, 128, 1, 4, 2, 64)
    vs = np.concatenate([np.asarray(r["o_vs"]) for r in R], axis=0).reshape(1, 128, 1, 4, 128)
    cp = cat("o_cp")[None]
    cs = np.concatenate([np.asarray(r["o_cs"]) for r in R], axis=0)[None]
    return (y_prompt, y_sample, rsp, rss, kp, vp, ks, vs, cp, cs)
```
